# Optimizing a Trainium2 kernel written in Bass

```python
import jax, jax.numpy as jnp
from jax import lax
import numpy as np

D_MODEL = 1024
BATCH = 8
SEQ = 2048
DEPTH = 1
DEC_BATCH = 32
DEC_SEQ = 64
PAST_LEN = 4096

CHUNK = 64
Q_BLOCK = 128
FOX_HEADS = 8
FOX_DIM = 64
MLA_HEADS = 8
MLA_NOPE = 64
MLA_ROPE = 32
MLA_VDIM = 64
Q_LORA = 384
KV_LORA = 256
D_FF = 2816
CONV_W = 3
ROPE_THETA = 10000.0
EPS = 1e-6
NEG = -1e30

FOX_W = FOX_HEADS * FOX_DIM
MLA_W = MLA_HEADS * MLA_VDIM
D_MIX = FOX_W + MLA_W
IN_SIZES = [FOX_W, FOX_W, FOX_W, FOX_HEADS, Q_LORA, KV_LORA, MLA_ROPE]
D_IN = sum(IN_SIZES)
IN_SPLIT_POINTS = [int(s) for s in np.cumsum(IN_SIZES)[:-1]]

kernel_name = "fox_mla_hybrid_convffn_stream_step"


def rmsnorm(x, g):
    xf = x.astype(jnp.float32)
    y = xf * lax.rsqrt(jnp.mean(xf * xf, axis=-1, keepdims=True) + EPS)
    return (y * g.astype(jnp.float32)).astype(x.dtype)


def rope(x, pos):
    half = MLA_ROPE // 2
    inv = ROPE_THETA ** (-jnp.arange(half, dtype=jnp.float32) / half)
    ang = pos.astype(jnp.float32)[:, None] * inv[None, :]
    cos = jnp.cos(ang)[:, None, :]
    sin = jnp.sin(ang)[:, None, :]
    xf = x.astype(jnp.float32)
    x1, x2 = xf[..., :half], xf[..., half:]
    return jnp.concatenate([x1 * cos - x2 * sin, x1 * sin + x2 * cos], axis=-1).astype(x.dtype)


def over_query_blocks(attend, q_side, q_pos):
    T = q_pos.shape[0]
    blk = min(Q_BLOCK, T)
    nb = T // blk

    def split(a):
        return jnp.moveaxis(a.reshape(a.shape[0], nb, blk, *a.shape[2:]), 1, 0)

    xs = (tuple(split(a) for a in q_side), q_pos.reshape(nb, blk))
    out = lax.map(lambda b: attend(*b[0], b[1]), xs)
    out = jnp.moveaxis(out, 0, 1)
    return out.reshape(out.shape[0], T, *out.shape[3:])


def fox_attend(q, k, v, fq, fk, q_pos, k_pos):
    s = jnp.einsum('bqhd,bkhd->bhqk', q, k, preferred_element_type=jnp.float32) * (FOX_DIM ** -0.5)
    s = s + jnp.transpose(fq, (0, 2, 1))[:, :, :, None] - jnp.transpose(fk, (0, 2, 1))[:, :, None, :]
    mask = k_pos[None, :] <= q_pos[:, None]
    p = jax.nn.softmax(jnp.where(mask, s, NEG), axis=-1)
    return jnp.einsum('bhqk,bkhd->bqhd', p.astype(v.dtype), v)


def mla_attend(q_lat, q_rope, q_pos, c_kv, k_rope, k_pos, w_uv):
    s = (jnp.einsum('bqhc,bkc->bhqk', q_lat, c_kv, preferred_element_type=jnp.float32)
         + jnp.einsum('bqhr,bkr->bhqk', q_rope, k_rope, preferred_element_type=jnp.float32))
    s = s * ((MLA_NOPE + MLA_ROPE) ** -0.5)
    mask = (k_pos // CHUNK)[None, :] <= (q_pos // CHUNK)[:, None]
    p = jax.nn.softmax(jnp.where(mask, s, NEG), axis=-1)
    o_lat = jnp.einsum('bhqk,bkc->bqhc', p.astype(c_kv.dtype), c_kv)
    return jnp.einsum('bqhc,chd->bqhd', o_lat, w_uv)


def layer(x, past, lp):
    (g_attn, w_in, b_f, g_q, w_q_up, g_kv, w_uk, w_uv, w_out,
     g_ffn, w_up, conv_w, conv_b, w_down) = lp
    pk, pv, plogf, pc, pkr, pconv = past
    B, T, _ = x.shape
    P = pk.shape[1]
    q_pos = P + jnp.arange(T, dtype=jnp.int32)
    k_pos = jnp.arange(P + T, dtype=jnp.int32)

    h = rmsnorm(x, g_attn)
    z = h @ w_in
    q, k, v, f_lin, q_c, c_kv, k_r = jnp.split(z, IN_SPLIT_POINTS, axis=-1)

    q = q.reshape(B, T, FOX_HEADS, FOX_DIM)
    k = k.reshape(B, T, FOX_HEADS, FOX_DIM)
    v = v.reshape(B, T, FOX_HEADS, FOX_DIM)
    logf = jax.nn.log_sigmoid(f_lin.astype(jnp.float32) + b_f.astype(jnp.float32))
    plogf32 = plogf.astype(jnp.float32)
    f_past = jnp.cumsum(plogf32, axis=1)
    f_new = jnp.sum(plogf32, axis=1, keepdims=True) + jnp.cumsum(logf, axis=1)
    f_keys = jnp.concatenate([f_past, f_new], axis=1)
    k_all = jnp.concatenate([pk.astype(k.dtype), k], axis=1)
    v_all = jnp.concatenate([pv.astype(v.dtype), v], axis=1)
    fox = over_query_blocks(
        lambda qb, fb, pb: fox_attend(qb, k_all, v_all, fb, f_keys, pb, k_pos),
        (q, f_new), q_pos)

    q_full = (rmsnorm(q_c, g_q) @ w_q_up).reshape(B, T, MLA_HEADS, MLA_NOPE + MLA_ROPE)
    q_nope = q_full[..., :MLA_NOPE]
    q_rope = rope(q_full[..., MLA_NOPE:], q_pos)
    q_lat = jnp.einsum('bthd,chd->bthc', q_nope, w_uk)
    c_kv = rmsnorm(c_kv, g_kv)
    k_r = rope(k_r[:, :, None, :], q_pos)[:, :, 0, :]
    c_all = jnp.concatenate([pc.astype(c_kv.dtype), c_kv], axis=1)
    kr_all = jnp.concatenate([pkr.astype(k_r.dtype), k_r], axis=1)
    mla = over_query_blocks(
        lambda lb, rb, pb: mla_attend(lb, rb, pb, c_all, kr_all, k_pos, w_uv),
        (q_lat, q_rope), q_pos)

    mixed = jnp.concatenate([fox.reshape(B, T, FOX_W), mla.reshape(B, T, MLA_W)], axis=-1)
    x = x + mixed @ w_out

    u = rmsnorm(x, g_ffn) @ w_up
    up = jnp.concatenate([pconv.astype(u.dtype), u], axis=1)
    uc = conv_b + sum(conv_w[j] * up[:, j:j + T] for j in range(CONV_W))
    gate, val = jnp.split(uc, 2, axis=-1)
    x = x + (jax.nn.silu(gate) * val) @ w_down
    conv_new = up[:, -(CONV_W - 1):]
    return x, (k, v, logf, c_kv, k_r, conv_new)


def setup_inputs(seed: int = 0) -> dict:
    key = jax.random.key(seed)
    ks = jax.random.split(key, 32)
    f32 = jnp.float32
    nrm = lambda k, s: jax.random.normal(k, s, f32)
    return {
        "x_prompt": nrm(ks[0], (BATCH, SEQ, D_MODEL)),
        "x_sample": nrm(ks[1], (DEC_BATCH, DEC_SEQ, D_MODEL)),
        "cache_fox_k": nrm(ks[2], (DEPTH, DEC_BATCH, PAST_LEN, FOX_HEADS, FOX_DIM)),
        "cache_fox_v": nrm(ks[3], (DEPTH, DEC_BATCH, PAST_LEN, FOX_HEADS, FOX_DIM)),
        "cache_fox_logf": jax.nn.log_sigmoid(2.0 + 0.5 * nrm(ks[4], (DEPTH, DEC_BATCH, PAST_LEN, FOX_HEADS))),
        "cache_mla_latent": nrm(ks[5], (DEPTH, DEC_BATCH, PAST_LEN, KV_LORA)),
        "cache_mla_krope": nrm(ks[6], (DEPTH, DEC_BATCH, PAST_LEN, MLA_ROPE)),
        "state_ffn_conv": nrm(ks[7], (DEPTH, DEC_BATCH, CONV_W - 1, 2 * D_FF)),
        "attn_norm": 1.0 + 0.05 * nrm(ks[8], (DEPTH, D_MODEL)),
        "w_in": nrm(ks[9], (DEPTH, D_MODEL, D_IN)) * D_MODEL ** -0.5,
        "b_forget": 2.0 + 0.5 * nrm(ks[10], (DEPTH, FOX_HEADS)),
        "q_norm": 1.0 + 0.05 * nrm(ks[11], (DEPTH, Q_LORA)),
        "w_q_up": nrm(ks[12], (DEPTH, Q_LORA, MLA_HEADS * (MLA_NOPE + MLA_ROPE))) * Q_LORA ** -0.5,
        "kv_norm": 1.0 + 0.05 * nrm(ks[13], (DEPTH, KV_LORA)),
        "w_uk": nrm(ks[14], (DEPTH, KV_LORA, MLA_HEADS, MLA_NOPE)) * KV_LORA ** -0.5,
        "w_uv": nrm(ks[15], (DEPTH, KV_LORA, MLA_HEADS, MLA_VDIM)) * KV_LORA ** -0.5,
        "w_out": nrm(ks[16], (DEPTH, D_MIX, D_MODEL)) * D_MIX ** -0.5,
        "ffn_norm": 1.0 + 0.05 * nrm(ks[17], (DEPTH, D_MODEL)),
        "w_up": nrm(ks[18], (DEPTH, D_MODEL, 2 * D_FF)) * D_MODEL ** -0.5,
        "conv_w": nrm(ks[19], (DEPTH, CONV_W, 2 * D_FF)) * CONV_W ** -0.5,
        "conv_b": 0.01 * nrm(ks[20], (DEPTH, 2 * D_FF)),
        "w_down": nrm(ks[21], (DEPTH, D_FF, D_MODEL)) * D_FF ** -0.5,
        "final_norm": 1.0 + 0.05 * nrm(ks[22], (D_MODEL,)),
    }


def reference(x_prompt, x_sample, cache_fox_k, cache_fox_v, cache_fox_logf, cache_mla_latent,
              cache_mla_krope, state_ffn_conv, attn_norm, w_in, b_forget, q_norm, w_q_up, kv_norm,
              w_uk, w_uv, w_out, ffn_norm, w_up, conv_w, conv_b, w_down, final_norm):
    hp, hs = x_prompt, x_sample
    Bp = x_prompt.shape[0]
    dt = x_prompt.dtype
    new_p, new_s = [], []
    for l in range(DEPTH):
        lp = (attn_norm[l], w_in[l], b_forget[l], q_norm[l], w_q_up[l], kv_norm[l], w_uk[l], w_uv[l],
              w_out[l], ffn_norm[l], w_up[l], conv_w[l], conv_b[l], w_down[l])
        no_past = (jnp.zeros((Bp, 0, FOX_HEADS, FOX_DIM), dt), jnp.zeros((Bp, 0, FOX_HEADS, FOX_DIM), dt),
                   jnp.zeros((Bp, 0, FOX_HEADS), jnp.float32), jnp.zeros((Bp, 0, KV_LORA), dt),
                   jnp.zeros((Bp, 0, MLA_ROPE), dt), jnp.zeros((Bp, CONV_W - 1, 2 * D_FF), dt))
        hp, st_p = layer(hp, no_past, lp)
        past = (cache_fox_k[l], cache_fox_v[l], cache_fox_logf[l], cache_mla_latent[l],
                cache_mla_krope[l], state_ffn_conv[l])
        hs, st_s = layer(hs, past, lp)
        new_p.append(st_p)
        new_s.append(st_s)
    y_prompt = rmsnorm(hp, final_norm)
    y_sample = rmsnorm(hs, final_norm)
    fox_k_p, fox_v_p, fox_logf_p, mla_latent_p, mla_krope_p, ffn_conv_p = [jnp.stack(f) for f in zip(*new_p)]
    fox_k_s, fox_v_s, fox_logf_s, mla_latent_s, mla_krope_s, ffn_conv_s = [jnp.stack(f) for f in zip(*new_s)]
    return (y_prompt, y_sample,
            fox_k_p, fox_v_p, fox_logf_p, mla_latent_p, mla_krope_p, ffn_conv_p,
            fox_k_s, fox_v_s, fox_logf_s, mla_latent_s, mla_krope_s, ffn_conv_s)
```

```python
import os
import numpy as np
import concourse.bass as bass
import concourse.mybir as mybir
from concourse.bass_utils import run_bass_kernel_spmd

F32 = mybir.dt.float32
BF16 = mybir.dt.bfloat16
AF = mybir.ActivationFunctionType
ALU = mybir.AluOpType

NCORES = 8
D = 1024
T_P = 2048
NS = 4
T_S = 64
PAST = 4096
DIN = 2216
DFF = 2816
NFC = 22
QG = 256
FG = 256
NEGM = -30000.0


class Trk:
    def __init__(self, nc):
        self.nc = nc
        self.eng = {"pe": nc.tensor, "act": nc.scalar, "dve": nc.vector, "pool": nc.gpsimd, "sp": nc.sync}
        self.sem = {}
        self.cnt = {}
        for e in ("pe", "act", "dve", "pool"):
            self.sem[e] = nc.alloc_semaphore("sem_" + e)
            self.cnt[e] = 0
        self.dsem = {}
        self.seen = {e: {} for e in self.eng}
        self.lw = {}
        self.rd = {}

    def _wait(self, e, tok):
        name, sem, val = tok
        if name == "pe" and e == "pe":
            return
        s = self.seen[e]
        if s.get(name, 0) >= val:
            return
        s[name] = val
        self.eng[e].wait_ge(sem, val)

    def _deps(self, e, reads, writes):
        for k in list(reads) + list(writes):
            t = self.lw.get(k)
            if t is not None:
                self._wait(e, t)
        for k in writes:
            for t in self.rd.get(k, ()):
                self._wait(e, t)

    def _commit(self, tok, reads, writes):
        for k in reads:
            self.rd.setdefault(k, []).append(tok)
        for k in writes:
            self.lw[k] = tok
            self.rd[k] = []

    def op(self, e, reads, writes, fn):
        writes = list(writes) + [k for k in reads if k.startswith("ps") and k not in writes]
        reads = [k for k in reads if not k.startswith("ps")]
        self._deps(e, reads, writes)
        ins = fn(self.eng[e])
        self.cnt[e] += 1
        ins.then_inc(self.sem[e], 1)
        tok = (e, self.sem[e], self.cnt[e])
        self._commit(tok, reads, writes)
        return tok

    def dma(self, q, slot, reads, writes, out, in_, **kw):
        if slot not in self.dsem:
            self.dsem[slot] = [self.nc.alloc_semaphore("d_%d" % len(self.dsem)), 0]
        ds = self.dsem[slot]
        name = "d_" + str(slot)
        if ds[1] > 0:
            self._wait(q, (name, ds[0], ds[1]))
        self._deps(q, reads, writes)
        if q == "pool":
            kw.setdefault("max_dma_last_dim", 4096)
        ins = self.eng[q].dma_start(out=out, in_=in_, **kw)
        ds[1] += 16
        ins.then_inc(ds[0], 16)
        tok = (name, ds[0], ds[1])
        self._commit(tok, reads, writes)
        return tok

    def barrier(self):
        toks = []
        for slot, (sem, c) in self.dsem.items():
            if c > 0:
                toks.append(("d_" + str(slot), sem, c))
        for x in ("pe", "act", "dve", "pool"):
            if self.cnt[x] > 0:
                toks.append((x, self.sem[x], self.cnt[x]))
        for e in self.eng:
            for t in toks:
                if not (t[0] == e):
                    self._wait(e, t)

    def finish(self, e="sp"):
        for slot, (sem, c) in self.dsem.items():
            if c > 0:
                self._wait(e, ("d_" + str(slot), sem, c))
        for x in ("pe", "act", "dve", "pool"):
            if self.cnt[x] > 0:
                self._wait(e, (x, self.sem[x], self.cnt[x]))


def run(gen):
    for _ in gen:
        pass


def interleave(main, side, ratio):
    side_alive = side is not None
    for _ in main:
        if side_alive:
            for _i in range(ratio):
                try:
                    next(side)
                except StopIteration:
                    side_alive = False
                    break
    if side_alive:
        run(side)


def build_program():
    nc = bass.Bass("TRN2", target_bir_lowering=False, dynamic_dma_scratch_size=4096)
    T = Trk(nc)
    STAGE = int(os.environ.get("KSTAGE", "9"))

    def din(name, shape):
        return nc.dram_tensor(name, list(shape), F32, kind="ExternalInput").ap()

    def dout(name, shape):
        return nc.dram_tensor(name, list(shape), F32, kind="ExternalOutput").ap()

    xp = din("xp", [T_P, D]); xs = din("xs", [NS, T_S, D])
    cfk = din("cfk", [NS, PAST, 512]); cfv = din("cfv", [NS, PAST, 512]); clf = din("clf", [NS, PAST, 8])
    cml = din("cml", [NS, PAST, 256]); ckr = din("ckr", [NS, PAST, 32]); scv = din("scv", [NS, 2, 2 * DFF])
    g_attn_d = din("attn_norm", [D]); w_in_d = din("w_in", [D, DIN]); b_f_d = din("b_forget", [8])
    g_q_d = din("q_norm", [384]); w_qup_d = din("w_q_up", [384, 768]); g_kv_d = din("kv_norm", [256])
    w_uk_d = din("w_uk", [256, 512]); w_uv_d = din("w_uv", [256, 512]); w_out_d = din("w_out", [D, D])
    g_ffn_d = din("ffn_norm", [D]); w_up_d = din("w_up", [D, 2 * DFF]); cw_d = din("conv_w", [3, 2 * DFF])
    cb_d = din("conv_b", [2 * DFF]); w_dn_d = din("w_down", [DFF, D]); g_fin_d = din("final_norm", [D])
    rope_d = din("rope_tab", [T_P + T_S, 64])
    cst_d = din("consts", [128, 5 * 128])

    y_p = dout("y_p", [T_P, D]); y_s = dout("y_s", [NS, T_S, D])
    ok_p = dout("ok_p", [T_P, 512]); ov_p = dout("ov_p", [T_P, 512]); olf_p = dout("olf_p", [T_P, 8])
    oml_p = dout("oml_p", [T_P, 256]); okr_p = dout("okr_p", [T_P, 32]); ocv_p = dout("ocv_p", [2, 2 * DFF])
    ok_s = dout("ok_s", [NS, T_S, 512]); ov_s = dout("ov_s", [NS, T_S, 512]); olf_s = dout("olf_s", [NS, T_S, 8])
    oml_s = dout("oml_s", [NS, T_S, 256]); okr_s = dout("okr_s", [NS, T_S, 32]); ocv_s = dout("ocv_s", [NS, 2, 2 * DFF])
    x1_d = nc.dram_tensor("x1_scratch", [T_P + NS * T_S, D], F32).ap()

    sb = nc.alloc_sbuf_tensor
    cst_f = sb("cst_f", [128, 3 * 128], F32)
    cst_b = sb("cst_b", [128, 5 * 128], BF16)
    ident = cst_b[:, 0:128]; ones_b = cst_b[:, 256:384]; mtri = cst_b[:, 384:512]; mblk = cst_b[:, 512:640]
    ident_f = cst_f[:, 0:128]; tri_f = cst_f[:, 128:256]; ones_f = cst_f[:, 256:384]
    T.dma("sp", "c0", [], ["cst_f"], cst_f[:], cst_d[:, 0:384])
    T.dma("pool", "c1", [], ["cst_b"], cst_b[:], cst_d)
    g_big = sb("g_big", [128, D], F32)
    g_q = sb("g_q", [128, 384], F32); g_kv = sb("g_kv", [128, 256], F32)
    b_f = sb("b_f", [128, 8], F32)
    T.dma("sp", "c2", [], ["g_big"], g_big[:], g_attn_d.partition_broadcast(128))
    T.dma("sp", "c3", [], ["g_q"], g_q[:], g_q_d.partition_broadcast(128))
    T.dma("sp", "c4", [], ["g_kv"], g_kv[:], g_kv_d.partition_broadcast(128))
    T.dma("sp", "c5", [], ["b_f"], b_f[:], b_f_d.partition_broadcast(128))
    xt = sb("xt", [128, D], F32)
    hb = sb("hb", [128, D], BF16)
    ss = sb("ss", [128, 8], F32)
    tot = sb("tot", [128, 8], F32)
    mhalf = sb("mhalf", [128, 8], F32)
    T.op("dve", [], ["mhalf"], lambda e: e.memset(mhalf[:], -0.5))

    ARENA = (nc.sbuf_bytes_remaining - 256) // 2
    arena = sb("arena", [128, ARENA], BF16)
    off = [0]

    def carve(n, dt=BF16):
        nb = 2 * n if dt == F32 else n
        a = off[0]; off[0] += (nb + 15) // 16 * 16
        assert off[0] <= ARENA, (off[0], ARENA)
        v = arena[:, a:a + nb]
        return v.bitcast(F32) if dt == F32 else v

    NQT = QG // 128
    w_in_sb = carve(8 * DIN).rearrange("p (k n) -> p k n", k=8)
    w_qup_sb = carve(3 * 768).rearrange("p (k n) -> p k n", k=3)
    w_uk_sb = carve(2 * 512).rearrange("p (k n) -> p k n", k=2)
    w_uv_sb = carve(2 * 512).rearrange("p (k n) -> p k n", k=2)
    w_out_sb = carve(8 * D).rearrange("p (k n) -> p k n", k=8)
    KTf = carve(16 * 8 * 128).rearrange("p (t h s) -> p t h s", t=16, h=8)
    _o_ktm = off[0]
    KTm = carve(16 * 8 * 128).rearrange("p (t h s) -> p t h s", t=16, h=8)
    VW = 768
    _o_vf = off[0]
    Vf = carve(16 * VW).rearrange("p (t n) -> p t n", t=16)
    _o_vm = off[0]
    Vm = carve(16 * VW).rearrange("p (t n) -> p t n", t=16)
    _wd_slots = [_o_vf + 5 * VW + i * 1024 for i in range(8)] + [_o_vm + 5 * VW + i * 1024 for i in range(8)] + [_o_ktm + 6 * 1024 + i * 1024 for i in range(6)]
    w_dn_c = [arena[:, a:a + 1024] for a in _wd_slots]
    APcls = type(arena[:, 0:1])

    def vaug(vtile, h, nk=128):
        a = (h // 2) * 192 + (h % 2) * 64
        return vtile[0:nk, a:a + 128]

    def vdst(vtile, nt):
        a = vtile[0:nt, 0:64]
        return APcls(a.tensor, a.offset, [list(a.ap[0]), [192, 4], [128, 2], [1, 64]])
    hT = carve(8 * 128).rearrange("p (k s) -> p k s", k=8)
    kout = carve(512, F32)
    cn = carve(256, F32); cnb = carve(256); cT = carve(256).rearrange("p (k s) -> p k s", k=2)
    qn = carve(384); qnT = carve(384).rearrange("p (k s) -> p k s", k=3)
    Kaug = carve(8 * 70).rearrange("p (h d) -> p h d", h=8)
    Kmla = carve(8 * 96).rearrange("p (h d) -> p h d", h=8)
    QaugS = carve(NQT * 8 * 70).rearrange("p (t h d) -> p t h d", t=NQT, h=8)
    QmlaS = carve(NQT * 8 * 96).rearrange("p (t h d) -> p t h d", t=NQT, h=8)
    rp = carve(64, F32); kro = carve(32, F32)
    tmp16 = carve(8 * 16, F32).rearrange("p (h d) -> p h d", h=8)
    tmp16b = carve(8 * 16, F32).rearrange("p (h d) -> p h d", h=8)
    tmp16c = carve(16, F32).rearrange("p (h d) -> p h d", h=1)
    tmp16d = carve(16, F32).rearrange("p (h d) -> p h d", h=1)
    lf = carve(8, F32); Ft = carve(8, F32); r1 = carve(8, F32)
    QTf = carve(8 * QG).rearrange("p (h s) -> p h s", h=8)
    QTm = carve(8 * QG).rearrange("p (h s) -> p h s", h=8)
    PT = [carve(512) for _ in range(2)]
    OT = carve(8 * QG).rearrange("p (k s) -> p k s", k=8)
    rden = carve(512, F32)
    _kf = KTf[:, 3:16, :, :].rearrange("p t h s -> p (t h s)")
    _ko = [0]

    def kcarve(n, dt=BF16):
        nb = 2 * n if dt == F32 else n
        a = _ko[0]; _ko[0] += (nb + 15) // 16 * 16
        assert _ko[0] <= 13 * 1024
        v = _kf[:, a:a + nb]
        return v.bitcast(F32) if dt == F32 else v
    KaugC = [kcarve(8 * 70).rearrange("p (h d) -> p h d", h=8) for _ in range(2)]
    w_ukT = kcarve(8 * 256).rearrange("p (h c) -> p h c", h=8)
    qlatT = kcarve(2 * 512).rearrange("p (k n) -> p k n", k=2)
    qrT = kcarve(512).rearrange("p (h s) -> p h s", h=8)
    LT = [kcarve(3 * 128).rearrange("p (k s) -> p k s", k=3) for _ in range(2)]
    LTn = kcarve(3 * 64).rearrange("p (k s) -> p k s", k=3)
    rdenL = kcarve(512, F32)
    KstF = [kcarve(512, F32) for _ in range(2)]
    VstF = [kcarve(512, F32) for _ in range(2)]
    LstF = [kcarve(288, F32) for _ in range(2)]
    KTc = [KTf[:, 1, :, :], KTf[:, 2, :, :]]
    Vfc = [Vf[:, 1 + i, 0:512] for i in range(4)]
    Lb = [Vm[:, 1 + i, 0:288] for i in range(4)]
    _km = KTm[:, 2:16, :, :].rearrange("p t h s -> p (t h s)")
    lfc = _km[:, 0:512].bitcast(F32).rearrange("p (t h) -> p t h", t=32)
    Fc = _km[:, 512:1024].bitcast(F32).rearrange("p (t h) -> p t h", t=32)
    Ec = _km[:, 1024:1024 + 528].bitcast(F32).rearrange("p (t h) -> p t h", t=33)
    rc = _km[:, 2048:2560].bitcast(F32).rearrange("p (t h) -> p t h", t=32)
    pcs = _km[:, 3072:3072 + 768].rearrange("p (t h c) -> p t h c", t=32, h=8)

    wcnt = [0]

    def wload(dst3, src, K, key):
        for k in range(K):
            T.dma("pool", "w%d" % (wcnt[0] % 4), [], [key], dst3[:, k, :], src[k * 128:(k + 1) * 128, :])
            wcnt[0] += 1
    wload(w_in_sb, w_in_d, 8, "w_in")
    wload(w_qup_sb, w_qup_d, 3, "w_qup")
    wload(w_uk_sb, w_uk_d, 2, "w_uk")
    wload(w_uv_sb, w_uv_d, 2, "w_uv")
    wload(w_out_sb, w_out_d, 8, "w_out")

    bank = [nc.alloc_psum_tensor("bank%d" % i, [128, 512], F32) for i in range(8)]
    bk = ["ps%d" % i for i in range(8)]

    def bfv(i):
        return bank[i][:, :].bitcast(BF16).rearrange("p (h s) -> p h s", h=8)

    def preset(e):
        e.memset(Kaug[:], 1.0)
        e.memset(Vf[:], 1.0)
        e.memset(Vm[:], 1.0)
        return e.memset(QaugS[:], 1.0)
    T.op("dve", [], ["Kaug", "Qaug"] + ["V%d" % t for t in range(16)], preset)
    T.op("dve", [], ["tot"], lambda e: e.memset(tot[:], 0.0))

    def tr8(src, nt, w, dst, dkey, skey, tb, nh=8, evac=None):
        ptv = bfv(tb)

        def f(e):
            r = None
            for h in range(nh):
                r = e.transpose(out=ptv[0:w, h, 0:nt], in_=src[0:nt, h, 0:w], identity=ident[0:nt, 0:nt])
            return r
        T.op("pe", [skey, "cst_b"], [bk[tb]], f)
        yield
        if evac is None:
            T.op("act", [bk[tb]], [dkey], lambda e: e.copy(out=dst, in_=ptv[0:w, 0:nh, 0:nt]))
        else:
            T.op("act", [bk[tb]], [dkey], lambda e: evac(e, ptv))
        yield

    def rmsn_stats(src, skey, nt, width, col):
        c = "ss%d" % col
        T.op("act", [skey], ["hb", c],
             lambda e: e.activation(out=hb[0:nt, 0:width], in_=src, func=AF.Square, accum_out=ss[0:nt, col:col + 1]))
        yield
        T.op("dve", [c], [c],
             lambda e: e.tensor_scalar(out=ss[0:nt, col:col + 1], in0=ss[0:nt, col:col + 1], scalar1=1.0 / width, scalar2=1e-6, op0=ALU.mult, op1=ALU.add))
        yield
        T.op("pool", [c, "mhalf"], [c], lambda e: e.tensor_tensor(out=ss[0:nt, col:col + 1], in0=ss[0:nt, col:col + 1], in1=mhalf[0:nt, 0:1], op=ALU.pow))
        yield

    def rope(src, dst, nt, nh, c0, dcol, rk, wk):
        cs = rp[0:nt, c0:c0 + 16].unsqueeze(1).to_broadcast([nt, nh, 16])
        sn = rp[0:nt, c0 + 16:c0 + 32].unsqueeze(1).to_broadcast([nt, nh, 16])
        x1 = src[:, :, 0:16]; x2 = src[:, :, 16:32]
        a = tmp16[0:nt, 0:nh, :]; b = tmp16b[0:nt, 0:nh, :]
        R = list(rk) + ["rp"]
        T.op("dve", R, ["tmpa"], lambda e: e.tensor_tensor(out=a, in0=x1, in1=cs, op=ALU.mult)); yield
        T.op("dve", R, ["tmpb"], lambda e: e.tensor_tensor(out=b, in0=x2, in1=sn, op=ALU.mult)); yield
        T.op("dve", ["tmpa", "tmpb"], wk, lambda e: e.tensor_tensor(out=dst[:, :, dcol:dcol + 16], in0=a, in1=b, op=ALU.subtract)); yield
        T.op("dve", R, ["tmpa"], lambda e: e.tensor_tensor(out=a, in0=x1, in1=sn, op=ALU.mult)); yield
        T.op("dve", R, ["tmpb"], lambda e: e.tensor_tensor(out=b, in0=x2, in1=cs, op=ALU.mult)); yield
        T.op("dve", ["tmpa", "tmpb"], wk, lambda e: e.tensor_tensor(out=dst[:, :, dcol + 16:dcol + 32], in0=a, in1=b, op=ALU.add)); yield

    def upproj(nt, kmla_dst, kmkey, vm_dst, vkey, pa):
        def um(w):
            def f(e):
                r = None
                for k in range(2):
                    r = e.matmul(bank[pa][0:nt, :], lhsT=cT[:, k, 0:nt], rhs=w[:, k, :], start=(k == 0), stop=(k == 1))
                return r
            return f
        T.op("pe", ["cT", "w_uk"], [bk[pa]], um(w_uk_sb)); yield
        T.op("act", [bk[pa]], [kmkey], lambda e: e.copy(out=kmla_dst[0:nt, :, 0:64], in_=bank[pa][0:nt, :].rearrange("p (h d) -> p h d", h=8))); yield
        T.op("pe", ["cT", "w_uv"], [bk[pa]], um(w_uv_sb)); yield
        T.op("dve", [bk[pa]], [vkey], lambda e: e.tensor_copy(out=vm_dst, in_=bank[pa][0:nt, :].rearrange("p (a c d) -> p a c d", a=4, c=2))); yield

    def roundrobin(gens):
        gens = list(gens)
        while gens:
            alive = []
            for g_ in gens:
                try:
                    next(g_)
                    alive.append(g_)
                except StopIteration:
                    pass
            gens = alive
            yield

    def rope2(src, dst, nt, nh, c0, dcol, rk, wk, ta, tbb, ka, kb):
        cs = rp[0:nt, c0:c0 + 16].unsqueeze(1).to_broadcast([nt, nh, 16])
        sn = rp[0:nt, c0 + 16:c0 + 32].unsqueeze(1).to_broadcast([nt, nh, 16])
        x1 = src[:, :, 0:16]; x2 = src[:, :, 16:32]
        a = ta[0:nt, 0:nh, :]; b = tbb[0:nt, 0:nh, :]
        R = list(rk) + ["rp"]
        T.op("dve", R, [ka], lambda e: e.tensor_tensor(out=a, in0=x1, in1=cs, op=ALU.mult)); yield
        T.op("dve", R, [kb], lambda e: e.tensor_tensor(out=b, in0=x2, in1=sn, op=ALU.mult)); yield
        T.op("dve", [ka, kb], wk, lambda e: e.tensor_tensor(out=dst[:, :, dcol:dcol + 16], in0=a, in1=b, op=ALU.subtract)); yield
        T.op("dve", R, [ka], lambda e: e.tensor_tensor(out=a, in0=x1, in1=sn, op=ALU.mult)); yield
        T.op("dve", R, [kb], lambda e: e.tensor_tensor(out=b, in0=x2, in1=cs, op=ALU.mult)); yield
        T.op("dve", [ka, kb], wk, lambda e: e.tensor_tensor(out=dst[:, :, dcol + 16:dcol + 32], in0=a, in1=b, op=ALU.add)); yield

    def front_tile(x_src, nt, rope_row, o_k, o_v, o_lf, o_ml, o_kr, vf_dst, vm_dst, vkey, Qaug, Qmla, pa, tb):
        pal = list(pa) if isinstance(pa, (list, tuple)) else [pa]
        nb_ = len(pal)
        T.dma("sp", "xt", [], ["xt"], xt[0:nt, :], x_src)
        T.dma("sp", "rp", [], ["rp"], rp[0:nt, :], rope_d[rope_row:rope_row + nt, :])
        yield
        yield from rmsn_stats(xt[0:nt, :], "xt", nt, D, 0)
        T.op("dve", ["xt", "ss0", "g_big"], ["hb"],
             lambda e: e.scalar_tensor_tensor(out=hb[0:nt, :], in0=xt[0:nt, :], scalar=ss[0:nt, 0:1], in1=g_big[0:nt, :], op0=ALU.mult, op1=ALU.mult))
        yield
        yield from tr8(hb.rearrange("p (k c) -> p k c", k=8), nt, 128, hT[:, :, 0:nt], "hT", "hb", tb)

        def zmm(pb, c0, n):
            def f(e):
                r = None
                for k in range(8):
                    r = e.matmul(bank[pb][0:nt, 0:n], lhsT=hT[:, k, 0:nt], rhs=w_in_sb[:, k, c0:c0 + n], start=(k == 0), stop=(k == 7))
                return r
            return f

        def br_q():
            pb = pal[0 % nb_]; pA = bank[pb]; kA = bk[pb]
            T.op("pe", ["hT", "w_in"], [kA], zmm(pb, 0, 512)); yield
            T.op("act", [kA], ["Qaug"], lambda e: e.activation(out=Qaug[0:nt, :, 0:64], in_=pA[0:nt, :].rearrange("p (h d) -> p h d", h=8), func=AF.Copy, scale=0.125)); yield

        def br_k():
            pb = pal[1 % nb_]; pA = bank[pb]; kA = bk[pb]
            T.op("pe", ["hT", "w_in"], [kA], zmm(pb, 512, 512)); yield
            T.op("act", [kA], ["kout"], lambda e: e.copy(out=kout[0:nt, :], in_=pA[0:nt, :])); yield
            T.op("dve", [kA], ["Kaug"], lambda e: e.tensor_copy(out=Kaug[0:nt, :, 0:64], in_=pA[0:nt, :].rearrange("p (h d) -> p h d", h=8))); yield
            T.dma("sp", "kout", ["kout"], [], o_k, kout[0:nt, :])
            yield

        def br_v():
            pb = pal[2 % nb_]; pA = bank[pb]; kA = bk[pb]
            T.op("pe", ["hT", "w_in"], [kA], zmm(pb, 1024, 512)); yield
            T.op("act", [kA], ["xt"], lambda e: e.copy(out=xt[0:nt, 0:512], in_=pA[0:nt, :])); yield
            T.op("dve", [kA], [vkey], lambda e: e.tensor_copy(out=vf_dst, in_=pA[0:nt, :].rearrange("p (a c d) -> p a c d", a=4, c=2))); yield
            T.dma("sp", "vout", ["xt"], [], o_v, xt[0:nt, 0:512])
            yield

        def br_f():
            pb = pal[0 % nb_]; pA = bank[pb]; kA = bk[pb]
            T.op("pe", ["hT", "w_in"], [kA], zmm(pb, 1536, 392)); yield
            T.op("dve", [kA, "b_f"], ["r1"], lambda e: e.tensor_tensor(out=r1[0:nt, :], in0=pA[0:nt, 0:8], in1=b_f[0:nt, :], op=ALU.add)); yield
            yield from rmsn_stats(pA[0:nt, 8:392], kA, nt, 384, 1)
            T.op("dve", [kA, "ss1", "g_q"], ["qn"],
                 lambda e: e.scalar_tensor_tensor(out=qn[0:nt, :], in0=pA[0:nt, 8:392], scalar=ss[0:nt, 1:2], in1=g_q[0:nt, :], op0=ALU.mult, op1=ALU.mult)); yield
            yield from roundrobin([br_f1(), br_f2()])

        def br_f1():
            pb = pal[2 % nb_]; pA = bank[pb]; kA = bk[pb]
            T.op("act", ["r1"], ["Ft"], lambda e: e.activation(out=Ft[0:nt, :], in_=r1[0:nt, :], func=AF.Exp, scale=-1.0)); yield
            T.op("dve", ["Ft"], ["r1"], lambda e: e.tensor_scalar(out=r1[0:nt, :], in0=Ft[0:nt, :], scalar1=1.0, scalar2=None, op0=ALU.add)); yield
            T.op("act", ["r1"], ["Ft"], lambda e: e.activation(out=Ft[0:nt, :], in_=r1[0:nt, :], func=AF.Ln)); yield
            T.op("dve", ["Ft"], ["lf"], lambda e: e.tensor_scalar(out=lf[0:nt, :], in0=Ft[0:nt, :], scalar1=-1.0, scalar2=None, op0=ALU.mult)); yield
            T.dma("sp", "lf", ["lf"], [], o_lf, lf[0:nt, :])

            def fmm(e):
                e.matmul(pA[0:nt, 0:8], lhsT=tri_f[0:nt, 0:nt], rhs=lf[0:nt, :], start=True, stop=True)
                return e.matmul(pA[0:nt, 8:16], lhsT=ones_f[0:nt, 0:nt], rhs=lf[0:nt, :], start=True, stop=True)
            T.op("pe", ["lf", "cst_f"], [kA], fmm); yield
            T.op("dve", [kA, "tot"], ["Ft"], lambda e: e.tensor_tensor(out=Ft[0:nt, :], in0=pA[0:nt, 0:8], in1=tot[0:nt, :], op=ALU.add)); yield
            T.op("dve", [kA, "tot"], ["tot"], lambda e: e.tensor_tensor(out=tot[0:nt, :], in0=pA[0:nt, 8:16], in1=tot[0:nt, :], op=ALU.add)); yield
            T.op("dve", ["Ft"], ["Qaug"], lambda e: e.tensor_copy(out=Qaug[0:nt, :, 67], in_=Ft[0:nt, :])); yield
            T.op("dve", ["Ft", "Qaug"], ["r1"], lambda e: e.tensor_tensor(out=r1[0:nt, :], in0=Ft[0:nt, :], in1=Qaug[0:nt, :, 67], op=ALU.subtract)); yield
            T.op("dve", ["r1"], ["Qaug"], lambda e: e.tensor_copy(out=Qaug[0:nt, :, 68], in_=r1[0:nt, :])); yield
            T.op("dve", ["r1", "Qaug"], ["Ft"], lambda e: e.tensor_tensor(out=Ft[0:nt, :], in0=r1[0:nt, :], in1=Qaug[0:nt, :, 68], op=ALU.subtract)); yield
            T.op("dve", ["Ft"], ["Qaug"], lambda e: e.tensor_copy(out=Qaug[0:nt, :, 69], in_=Ft[0:nt, :])); yield
            T.op("dve", ["Qaug"], ["Kaug"], lambda e: e.tensor_scalar(out=Kaug[0:nt, :, 64:67], in0=Qaug[0:nt, :, 67:70], scalar1=-1.0, scalar2=None, op0=ALU.mult)); yield

        def br_f2():
            yield from tr8(qn.rearrange("p (k c) -> p k c", k=3), nt, 128, qnT[:, :, 0:nt], "qnT", "qn", tb, nh=3)
            sc = 96.0 ** -0.5
            for half in range(2):
                pb = pal[0]; pA = bank[pb]; kA = bk[pb]

                def qmm(e, half=half, pA=pA):
                    r = None
                    for k in range(3):
                        r = e.matmul(pA[0:nt, 0:384], lhsT=qnT[:, k, 0:nt], rhs=w_qup_sb[:, k, half * 384:(half + 1) * 384], start=(k == 0), stop=(k == 2))
                    return r
                T.op("pe", ["qnT", "w_qup"], [kA], qmm); yield
                pv = pA[0:nt, 0:384].rearrange("p (h d) -> p h d", h=4)
                hs = slice(half * 4, half * 4 + 4)
                T.op("act", [kA], ["Qmla"], lambda e, pv=pv, hs=hs: e.activation(out=Qmla[0:nt, hs, 0:64], in_=pv[:, :, 0:64], func=AF.Copy, scale=sc)); yield
                yield from rope2(pv[:, :, 64:96], Qmla[0:nt, hs, :], nt, 4, 0, 64, [kA], ["Qmla"], tmp16, tmp16b, "tmpa", "tmpb")

        def br_c():
            pb = pal[1 % nb_]; pA = bank[pb]; kA = bk[pb]
            T.op("pe", ["hT", "w_in"], [kA], zmm(pb, 1928, 288)); yield
            yield from rmsn_stats(pA[0:nt, 0:256], kA, nt, 256, 2)
            T.op("dve", [kA, "ss2", "g_kv"], ["cn"],
                 lambda e: e.scalar_tensor_tensor(out=cn[0:nt, :], in0=pA[0:nt, 0:256], scalar=ss[0:nt, 2:3], in1=g_kv[0:nt, :], op0=ALU.mult, op1=ALU.mult)); yield
            T.dma("sp", "cn", ["cn"], [], o_ml, cn[0:nt, :])
            T.op("act", ["cn"], ["cnb"], lambda e: e.copy(out=cnb[0:nt, :], in_=cn[0:nt, :])); yield
            yield from rope2(pA[0:nt, 256:288].unsqueeze(1), kro[0:nt, :].unsqueeze(1), nt, 1, 32, 0, [kA], ["kro"], tmp16c, tmp16d, "tmpc", "tmpd")
            T.dma("sp", "kro", ["kro"], [], o_kr, kro[0:nt, :])
            T.op("dve", ["kro"], ["Kmla"], lambda e: e.tensor_copy(out=Kmla[0:nt, :, 64:96], in_=kro[0:nt, :].unsqueeze(1).to_broadcast([nt, 8, 32]))); yield
            yield from tr8(cnb.rearrange("p (k c) -> p k c", k=2), nt, 128, cT[:, :, 0:nt], "cT", "cnb", tb, nh=2)
            yield from upproj(nt, Kmla, "Kmla", vm_dst, vkey, pal[1 % nb_])

        yield from roundrobin([br_q(), br_k(), br_v()])
        yield from roundrobin([br_f(), br_c()])

    def outproj(x_src, nt, ot_view, x1_dst, pa):
        T.dma("sp", "xt", [], ["xt"], xt[0:nt, :], x_src)
        yield
        for hf in range(2):
            def f(e, hf=hf):
                r = None
                for k in range(8):
                    r = e.matmul(bank[pa][0:nt, :], lhsT=ot_view[:, k, :], rhs=w_out_sb[:, k, hf * 512:(hf + 1) * 512], start=(k == 0), stop=(k == 7))
                return r
            T.op("pe", ["OT", "w_out"], [bk[pa]], f)
            yield
            T.op("dve", [bk[pa], "xt"], ["xt"], lambda e, hf=hf: e.tensor_tensor(out=xt[0:nt, hf * 512:(hf + 1) * 512], in0=bank[pa][0:nt, :], in1=xt[0:nt, hf * 512:(hf + 1) * 512], op=ALU.add))
            yield
        T.dma("sp", "x1o", ["xt"], ["x1d"], x1_dst, xt[0:nt, :])
        yield

    NG = T_P // QG

    def fe_group(g):
        for j in range(NQT):
            t = g * NQT + j
            r0 = t * 128
            yield from front_tile(xp[r0:r0 + 128, :], 128, r0, ok_p[r0:r0 + 128, :], ov_p[r0:r0 + 128, :], olf_p[r0:r0 + 128, :],
                                  oml_p[r0:r0 + 128, :], okr_p[r0:r0 + 128, :], vdst(Vf[:, t, :], 128), vdst(Vm[:, t, :], 128), "V%d" % t,
                                  QaugS[:, j], QmlaS[:, j], [0, 6, 7], 1)
            yield from tr8(Kaug, 128, 70, KTf[0:70, t, :, :], "KTf%d" % t, "Kaug", 1)
            yield from tr8(Kmla, 128, 96, KTm[0:96, t, :, :], "KTm%d" % t, "Kmla", 1)

    def att_group(g):
        q0 = g * QG
        nkt = (q0 + QG) // 128
        blocks = []
        for typ in range(2):
            for h in range(8):
                for kt in range(nkt):
                    blocks.append((typ, h, kt))
        nb = len(blocks)

        def qk_ins(e, i):
            typ, h, kt = blocks[i]
            KT, QT, dk, msk = (KTf, QTf, 70, mtri) if typ == 0 else (KTm, QTm, 96, mblk)
            c0 = max(0, kt * 128 - q0)
            n = QG - c0
            diag = kt * 128 >= q0
            sbk = bank[2 + (i % 2)]
            r = e.matmul(sbk[:, 0:n], lhsT=KT[0:dk, kt, h, :], rhs=QT[0:dk, h, c0:QG], start=True, stop=not diag)
            if diag:
                r = e.matmul(sbk[:, 0:128], lhsT=ident, rhs=msk, start=False, stop=True)
            return r

        def pv_ins(e, i):
            typ, h, kt = blocks[i]
            V = Vf if typ == 0 else Vm
            oi = 4 + (typ * 8 + h) % 2
            c0 = max(0, kt * 128 - q0)
            n = QG - c0
            return e.matmul(bank[oi][:, c0:QG], lhsT=vaug(V[:, kt, :], h), rhs=PT[i % 2][:, 0:n], start=(kt == 0), stop=(kt == nkt - 1))

        def keys_qk(i):
            typ, h, kt = blocks[i]
            return [("KTf%d" if typ == 0 else "KTm%d") % kt, "QTf" if typ == 0 else "QTm", "cst_b"], [bk[2 + (i % 2)]]

        def keys_pv(i):
            typ, h, kt = blocks[i]
            return ["PT%d" % (i % 2), "V%d" % kt], [bk[4 + (typ * 8 + h) % 2]]

        def emit_exp(i):
            typ, h, kt = blocks[i]
            n = QG - max(0, kt * 128 - q0)
            sbi = 2 + (i % 2)
            T.op("act", [bk[sbi]], ["PT%d" % (i % 2)], lambda e: e.activation(out=PT[i % 2][:, 0:n], in_=bank[sbi][:, 0:n], func=AF.Exp))

        def emit_norm(i):
            typ, h, kt = blocks[i]
            oi = 4 + (typ * 8 + h) % 2
            pair, half = h // 2, h % 2
            rs = slice(half * 64, half * 64 + 64)
            rd = slice((1 - half) * 64, (1 - half) * 64 + 64)
            T.op("dve", [bk[oi]], ["rden"], lambda e: e.reciprocal(out=rden[rd, 0:QG], in_=bank[oi][rd, 0:QG]))
            T.op("dve", [bk[oi], "rden"], ["OT"], lambda e: e.tensor_tensor(out=OT[rs, typ * 4 + pair, :], in0=bank[oi][rs, 0:QG], in1=rden[rd, 0:QG], op=ALU.mult))

        r_, w_ = keys_qk(0)
        T.op("pe", r_, w_, lambda e: qk_ins(e, 0))
        for i in range(nb + 1):
            if i < nb:
                emit_exp(i)
            rr, ww = [], []
            if i + 1 < nb:
                a, b2 = keys_qk(i + 1); rr += a; ww += b2
            if i >= 1:
                a, b2 = keys_pv(i - 1); rr += a; ww += b2

            def f(e, i=i):
                r = None
                if i + 1 < nb:
                    r = qk_ins(e, i + 1)
                if i >= 1:
                    r = pv_ins(e, i - 1)
                return r
            if rr:
                T.op("pe", rr, ww, f)
            if i >= 1 and blocks[i - 1][2] == nkt - 1:
                emit_norm(i - 1)
            yield

    def qtrans_gen():
        for j in range(NQT):
            yield from tr8(QaugS[:, j], 128, 70, QTf[0:70, :, j * 128:(j + 1) * 128], "QTf", "Qaug", 1)
            yield from tr8(QmlaS[:, j], 128, 96, QTm[0:96, :, j * 128:(j + 1) * 128], "QTm", "Qmla", 6)

    def outproj_gen(g):
        for j in range(NQT):
            r0 = g * QG + j * 128
            yield from outproj(xp[r0:r0 + 128, :], 128, OT[:, :, j * 128:(j + 1) * 128], x1_d[r0:r0 + 128, :], 0)

    run(fe_group(0))
    run(qtrans_gen())
    for g in range(NG if STAGE >= 3 else 1):
        if STAGE < 2:
            break
        nblk = 16 * ((g * QG + QG) // 128)
        side = fe_group(g + 1) if g + 1 < NG else None
        ratio = max(1, -(-NQT * 110 // nblk))
        interleave(att_group(g), side, ratio)
        run(roundrobin([outproj_gen(g)] + ([qtrans_gen()] if g + 1 < NG else [])))

    T.barrier()
    NQ = T_S
    T.op("dve", [], ["KaugC0", "KaugC1"], lambda e: (e.memset(KaugC[0][:], 1.0), e.memset(KaugC[1][:], 1.0))[1])
    for kc in range(2):
        run(tr8(w_uk_sb[:, kc, :].rearrange("p (h d) -> p h d", h=8), 128, 64, w_ukT[0:64, :, kc * 128:(kc + 1) * 128], "w_ukT", "w_uk", 1))
    def cumsum_gen(b):
        for q4 in range(4):
            T.dma("sp", "lfc%d" % q4, [], ["lfc"], lfc[:, q4 * 8:(q4 + 1) * 8, :], clf[b, q4 * 1024:(q4 + 1) * 1024, :].rearrange("(t p) h -> p t h", p=128))
        yield

        def fcm(e):
            e.matmul(bank[7][:, 0:256], lhsT=tri_f, rhs=lfc.rearrange("p t h -> p (t h)"), start=True, stop=True)
            return e.matmul(bank[7][:, 256:512], lhsT=ones_f, rhs=lfc.rearrange("p t h -> p (t h)"), start=True, stop=True)
        T.op("pe", ["lfc", "cst_f"], [bk[7]], fcm); yield
        T.op("dve", [], ["Ec"], lambda e: e.memset(Ec[:, 0, :], 0.0)); yield
        T.op("act", [bk[7]], ["rc"], lambda e: e.copy(out=rc, in_=bank[7][:, 256:512].rearrange("p (t h) -> p t h", t=32))); yield
        for t in range(32):
            T.op("dve", ["rc", "Ec"], ["Ec"], lambda e, t=t: e.tensor_tensor(out=Ec[:, t + 1, :], in0=Ec[:, t, :], in1=rc[:, t, :], op=ALU.add)); yield
        T.op("dve", [bk[7], "Ec"], ["Fc"], lambda e: e.tensor_tensor(out=Fc, in0=bank[7][:, 0:256].rearrange("p (t h) -> p t h", t=32), in1=Ec[:, 0:32, :], op=ALU.add)); yield
        T.op("dve", ["Ec"], ["tot"], lambda e: e.tensor_copy(out=tot[:], in_=Ec[:, 32, :])); yield
        T.op("dve", ["Fc"], ["pcs"], lambda e: e.tensor_copy(out=pcs[:, :, :, 0], in_=Fc)); yield
        T.op("dve", ["Fc", "pcs"], ["rc"], lambda e: e.tensor_tensor(out=rc, in0=Fc, in1=pcs[:, :, :, 0], op=ALU.subtract)); yield
        T.op("dve", ["rc"], ["pcs"], lambda e: e.tensor_copy(out=pcs[:, :, :, 1], in_=rc)); yield
        T.op("dve", ["rc", "pcs"], ["Fc"], lambda e: e.tensor_tensor(out=Fc, in0=rc, in1=pcs[:, :, :, 1], op=ALU.subtract)); yield
        T.op("dve", ["Fc"], ["pcs"], lambda e: e.tensor_copy(out=pcs[:, :, :, 2], in_=Fc)); yield
        T.op("dve", ["pcs"], ["pcs"], lambda e: e.tensor_scalar(out=pcs, in0=pcs, scalar1=-1.0, scalar2=None, op0=ALU.mult)); yield

    NJ = NS if STAGE >= 5 else (1 if STAGE == 4 else 0)
    if NJ > 0:
        run(cumsum_gen(0))
    for b in range(NJ):
        run(front_tile(xs[b], NQ, T_P, ok_s[b], ov_s[b], olf_s[b], oml_s[b], okr_s[b], vdst(Vf[:, 0, :], NQ), vdst(Vm[:, 0, :], NQ), "V0",
                       QaugS[:, 0], QmlaS[:, 0], [0, 6, 7], 1))
        run(tr8(QaugS[:, 0], NQ, 70, QTf[0:70, :, 0:NQ], "QTf", "Qaug", 1))
        run(tr8(QmlaS[:, 0][:, :, 0:64], NQ, 64, QTm[0:64, :, 0:NQ], "QTm", "Qmla", 1))
        run(tr8(QmlaS[:, 0][:, :, 64:96], NQ, 32, qrT[0:32, :, 0:NQ], "qrT", "Qmla", 1))
        run(tr8(Kaug, NQ, 70, KTf[0:70, 0, :, 0:NQ], "KTf0", "Kaug", 1))
        for typ in range(1):
            QT, dk, qk = (QTf, 70, "QTf") if typ == 0 else (QTm, 96, "QTm")

            def st_load(kt):
                r0 = kt * 128
                i2 = kt % 2
                T.dma("sp", "kc%d" % i2, [], ["KstF%d" % i2], KstF[i2], cfk[b, r0:r0 + 128, :])
                T.dma("pool", "vc%d" % (kt % 4), [], ["Vc%d" % (kt % 4)], Vfc[kt % 4], cfv[b, r0:r0 + 128, :])

            def st_prep(kt):
                i2 = kt % 2
                if typ == 0:
                    T.op("dve", ["KstF%d" % i2], ["KaugC%d" % i2], lambda e: e.tensor_copy(out=KaugC[i2][:, :, 0:64], in_=KstF[i2].rearrange("p (h d) -> p h d", h=8)))
                    T.op("dve", ["pcs"], ["KaugC%d" % i2], lambda e: e.tensor_copy(out=KaugC[i2][:, :, 64:67], in_=pcs[:, kt, :, :]))
                    run(tr8(KaugC[i2], 128, 70, KTc[i2][0:70, :, :], "KTc%d" % i2, "KaugC%d" % i2, 1 if i2 == 0 else 6))
                else:
                    run(tr8(latc[i2].rearrange("p (k c) -> p k c", k=2), 128, 128, cT[:, :, :], "cT", "latc%d" % i2, 1, nh=2))
                    pass
                    T.op("dve", ["krc%d" % i2], ["Kmla"], lambda e: e.tensor_copy(out=Kmla[:, :, 64:96], in_=krc[i2][:].unsqueeze(1).to_broadcast([128, 8, 32])))
                    run(tr8(Kmla, 128, 96, KTc[i2][0:96, :, :], "KTc%d" % i2, "Kmla", 6))

            def srcs(kt):
                if kt < 32:
                    i2 = kt % 2
                    return KTc[i2], "KTc%d" % i2, Vfc[kt % 4], "Vc%d" % (kt % 4), 128
                return KTf[:, 0], "KTf0", Vf[:, 0, :], "V0", NQ

            def st_qk(kt):
                KTsrc, kk, _, _, nk = srcs(kt)
                sbi = 2 + kt % 2
                sbk = bank[sbi]; ptb = PT[kt % 2]

                def qkf(e):
                    r = None
                    for h in range(8):
                        r = e.matmul(sbk[0:nk, h * NQ:(h + 1) * NQ], lhsT=KTsrc[0:dk, h, 0:nk], rhs=QT[0:dk, h, 0:NQ], start=(h == 0), stop=True, skip_group_check=True)
                        if kt == 32 and typ == 0:
                            r = e.matmul(sbk[0:nk, h * NQ:(h + 1) * NQ], lhsT=ident[0:NQ, 0:NQ], rhs=mtri[0:NQ, 0:NQ], start=False, stop=True, skip_group_check=True)
                    return r
                T.op("pe", [kk, qk, "cst_b"], [bk[sbi]], qkf)
                T.op("act", [bk[sbi]], ["PT%d" % (kt % 2)], lambda e: e.activation(out=ptb[0:nk, :], in_=sbk[0:nk, :], func=AF.Exp))

            def st_pv(kt):
                _, _, Vsrc, vk, nk = srcs(kt)
                ptb = PT[kt % 2]

                def pvf(e):
                    for h in range(8):
                        pair = h // 2
                        lt_ = Vsrc[0:nk, pair * 128:(pair + 1) * 128] if kt < 32 else vaug(Vsrc, h, nk)
                        e.matmul(bank[4][:, h * NQ:(h + 1) * NQ], lhsT=lt_, rhs=ptb[0:nk, h * NQ:(h + 1) * NQ],
                                 start=(kt == 0 and h == 0), stop=(kt == 32), skip_group_check=True)
                    return e.matmul(bank[5][:, :], lhsT=ones_b[0:nk, :], rhs=ptb[0:nk, :], start=(kt == 0), stop=(kt == 32))
                T.op("pe", ["PT%d" % (kt % 2), vk, "cst_b"], [bk[4], bk[5]], pvf)

            for it in range(33 + 3):
                if b == 0 and it < NFC:
                    T.dma("pool", "w%d" % (wcnt[0] % 4), [], ["w_dn%d" % it], w_dn_c[it], w_dn_d[it * 128:(it + 1) * 128, :])
                    wcnt[0] += 1
                if it < 32:
                    st_load(it)
                if 0 <= it - 1 < 32:
                    st_prep(it - 1)
                if 0 <= it - 2 < 33:
                    st_qk(it - 2)
                if 0 <= it - 3 < 33:
                    st_pv(it - 3)
            for half in range(2):
                rs = slice(half * 64, half * 64 + 64)
                dv = bank[5][rs, :].rearrange("p (a c q) -> p a c q", a=4, c=2)[:, :, half, :]
                ov = bank[4][rs, :].rearrange("p (a c q) -> p a c q", a=4, c=2)[:, :, half, :]
                rv = rden[rs, 0:4 * NQ].rearrange("p (a q) -> p a q", a=4)
                T.op("dve", [bk[5]], ["rden"], lambda e, dv=dv, rv=rv: e.reciprocal(out=rv, in_=dv))
                T.op("dve", [bk[4], "rden"], ["OT"], lambda e, ov=ov, rv=rv, rs=rs, typ=typ: e.tensor_tensor(out=OT[rs, typ * 4:typ * 4 + 4, 0:NQ], in0=ov, in1=rv, op=ALU.mult))

        for kc in range(2):
            bi = 0 if kc == 0 else 7

            def qlm(e, kc=kc, bi=bi):
                r = None
                for h in range(8):
                    r = e.matmul(bank[bi][:, h * NQ:(h + 1) * NQ], lhsT=w_ukT[0:64, h, kc * 128:(kc + 1) * 128], rhs=QTm[0:64, h, 0:NQ],
                                 start=(h == 0), stop=True, skip_group_check=True)
                return r
            T.op("pe", ["w_ukT", "QTm"], [bk[bi]], qlm)
            if kc == 0:
                T.op("act", [bk[bi]], ["qlatT"], lambda e: e.copy(out=qlatT[:, 0, :], in_=bank[0][:, :]))
            else:
                T.op("dve", [bk[bi]], ["qlatT"], lambda e: e.tensor_copy(out=qlatT[:, 1, :], in_=bank[7][:, :]))

        def ltn(e):
            ptv = bfv(1)
            e.transpose(out=ptv[:, 0, 0:NQ], in_=cnb[0:NQ, 0:128], identity=ident[0:NQ, 0:NQ])
            e.transpose(out=ptv[:, 1, 0:NQ], in_=cnb[0:NQ, 128:256], identity=ident[0:NQ, 0:NQ])
            return e.transpose(out=ptv[0:32, 2, 0:NQ], in_=Kmla[0:NQ, 0, 64:96], identity=ident[0:NQ, 0:NQ])
        T.op("pe", ["cnb", "Kmla", "cst_b"], [bk[1]], ltn)
        T.op("act", [bk[1]], ["LTn"], lambda e: e.copy(out=LTn[:, 0:2, :], in_=bfv(1)[:, 0:2, 0:NQ]))
        T.op("act", [bk[1]], ["LTn"], lambda e: e.copy(out=LTn[0:32, 2, :], in_=bfv(1)[0:32, 2, 0:NQ]))

        def a_load(kt):
            r0 = kt * 128
            i4 = kt % 4
            T.dma("sp", "lc%d" % (kt % 2), [], ["LstF%d" % (kt % 2)], LstF[kt % 2][:, 0:256], cml[b, r0:r0 + 128, :])
            T.dma("sp", "rc%d" % (kt % 2), [], ["LstF%d" % (kt % 2)], LstF[kt % 2][:, 256:288], ckr[b, r0:r0 + 128, :])

        def a_prep(kt):
            i4 = kt % 4; i2 = kt % 2
            tb = 1 if i2 == 0 else 6
            T.op("dve", ["LstF%d" % i2], ["Lb%d" % i4], lambda e: e.tensor_copy(out=Lb[i4], in_=LstF[i2]))

            def f(e):
                ptv = bfv(tb)
                e.transpose(out=ptv[:, 0, :], in_=Lb[i4][:, 0:128], identity=ident)
                e.transpose(out=ptv[:, 1, :], in_=Lb[i4][:, 128:256], identity=ident)
                return e.transpose(out=ptv[0:32, 2, :], in_=Lb[i4][:, 256:288], identity=ident)
            T.op("pe", ["Lb%d" % i4, "cst_b"], [bk[tb]], f)
            T.op("act", [bk[tb]], ["LT%d" % i2], lambda e: e.copy(out=LT[i2][:, 0:2, :], in_=bfv(tb)[:, 0:2, :]))
            T.op("dve", [bk[tb]], ["LT%d" % i2], lambda e: e.tensor_copy(out=LT[i2][0:32, 2, :], in_=bfv(tb)[0:32, 2, :]))

        def a_qk(kt):
            lt, ltk, nk = (LT[kt % 2], "LT%d" % (kt % 2), 128) if kt < 32 else (LTn, "LTn", NQ)
            sbi = 2 + kt % 2
            ptb = PT[kt % 2]

            def f(e):
                e.matmul(bank[sbi][0:nk, :], lhsT=lt[:, 0, 0:nk], rhs=qlatT[:, 0, :], start=True, stop=False)
                e.matmul(bank[sbi][0:nk, :], lhsT=lt[:, 1, 0:nk], rhs=qlatT[:, 1, :], start=False, stop=False)
                return e.matmul(bank[sbi][0:nk, :], lhsT=lt[0:32, 2, 0:nk], rhs=qrT[0:32, :, :].rearrange("p h s -> p (h s)"), start=False, stop=True)
            T.op("pe", [ltk, "qlatT", "qrT"], [bk[sbi]], f)
            T.op("act", [bk[sbi]], ["PT%d" % (kt % 2)], lambda e: e.activation(out=ptb[0:nk, :], in_=bank[sbi][0:nk, :], func=AF.Exp))

        def a_pv(kt):
            lsrc, lk, nk = (Lb[kt % 4], "Lb%d" % (kt % 4), 128) if kt < 32 else (cnb, "cnb", NQ)
            ptb = PT[kt % 2]

            def f(e):
                e.matmul(bank[4][:, :], lhsT=lsrc[0:nk, 0:128], rhs=ptb[0:nk, :], start=(kt == 0), stop=(kt == 32))
                e.matmul(bank[5][:, :], lhsT=lsrc[0:nk, 128:256], rhs=ptb[0:nk, :], start=(kt == 0), stop=(kt == 32))
                return e.matmul(bank[0][:, :], lhsT=ones_b[0:nk, :], rhs=ptb[0:nk, :], start=(kt == 0), stop=(kt == 32))
            T.op("pe", ["PT%d" % (kt % 2), lk, "cst_b"], [bk[4], bk[5], bk[0]], f)

        side = cumsum_gen(b + 1) if b + 1 < NJ else None
        for it in range(33 + 3):
            if it < 32:
                a_load(it)
            if 0 <= it - 1 < 32:
                a_prep(it - 1)
            if 0 <= it - 2 < 33:
                a_qk(it - 2)
            if 0 <= it - 3 < 33:
                a_pv(it - 3)
            if side is not None:
                for _i in range(2):
                    try:
                        next(side)
                    except StopIteration:
                        side = None
                        break
        if side is not None:
            run(side)
        T.op("dve", [bk[0]], ["rdenL"], lambda e: e.reciprocal(out=rdenL[:, :], in_=bank[0][:, :]))
        T.op("dve", [bk[4], "rdenL"], ["qlatT"], lambda e: e.tensor_tensor(out=qlatT[:, 0, :], in0=bank[4][:, :], in1=rdenL[:, :], op=ALU.mult))
        T.op("dve", [bk[5], "rdenL"], ["qlatT"], lambda e: e.tensor_tensor(out=qlatT[:, 1, :], in0=bank[5][:, :], in1=rdenL[:, :], op=ALU.mult))

        def fin(e):
            r = None
            first = True
            for h in range(8):
                pair = h // 2
                for kc in range(2):
                    r = e.matmul(bank[7][:, h * NQ:(h + 1) * NQ], lhsT=w_uv_sb[:, kc, pair * 128:(pair + 1) * 128], rhs=qlatT[:, kc, h * NQ:(h + 1) * NQ],
                                 start=first, stop=(kc == 1), skip_group_check=True)
                    first = False
            return r
        T.op("pe", ["qlatT", "w_uv"], [bk[7]], fin)
        for half in range(2):
            rs = slice(half * 64, half * 64 + 64)
            ov = bank[7][rs, :].rearrange("p (a c q) -> p a c q", a=4, c=2)[:, :, half, :]
            T.op("act", [bk[7]], ["OT"], lambda e, ov=ov, rs=rs: e.copy(out=OT[rs, 4:8, 0:NQ], in_=ov))
        run(outproj(xs[b], NQ, OT[:, :, 0:NQ], x1_d[T_P + b * NQ:T_P + (b + 1) * NQ, :], 0))

    if STAGE < 6:
        T.finish("sp")
        return nc
    T.barrier()
    off[0] = 0
    w_up_sb = carve(8 * 2 * DFF).rearrange("p (k n) -> p k n", k=8)
    assert off[0] <= _o_ktm
    _free = [[off[0], _o_ktm + 6 * 1024], [_o_vf, _o_vf + 5 * VW], [_o_vm, _o_vm + 5 * VW], [_o_vm + 16 * VW, ARENA]]

    def carve(n, dt=BF16):
        nb = 2 * n if dt == F32 else n
        na = (nb + 15) // 16 * 16
        for r_ in _free:
            if r_[1] - r_[0] >= na:
                a = r_[0]; r_[0] += na
                v = arena[:, a:a + nb]
                return v.bitcast(F32) if dt == F32 else v
        raise AssertionError("FFN arena full")
    aTb = [carve(NFC * FG).rearrange("p (k n) -> p k n", k=NFC) for _ in range(2)]
    h2Tb = [carve(8 * (FG + 8)).rearrange("p (k n) -> p k n", k=8) for _ in range(2)]
    T.dma("sp", "c2", [], ["g_big"], g_big[:], g_ffn_d.partition_broadcast(128))
    g_fin = carve(D, F32)
    T.dma("sp", "c6", [], ["g_fin"], g_fin, g_fin_d.partition_broadcast(128))
    cw = carve(3 * 44, F32).rearrange("p (j c) -> p j c", j=3); cb = carve(44, F32)
    cstage = carve(128, F32)
    cprev = carve(NS * 2 * 44, F32).rearrange("p (b j c) -> p b j c", b=NS, j=2)
    cnew = carve(NS * 2 * 44, F32).rearrange("p (b j c) -> p b j c", b=NS, j=2)
    cnewT = carve(128, F32)
    tgs = [carve(FG, F32) for _ in range(2)]; tvs = [carve(FG, F32) for _ in range(2)]
    yt = carve(D, F32)
    xe = carve(D, F32)

    def load_fm(src_rows, nrows, dst, dkey):
        T.dma("sp", "cst", [], ["cstage"], cstage[0:nrows, :], src_rows)
        T.op("pe", ["cstage", "cst_f"], [bk[0]], lambda e: e.transpose(out=bank[0][:, 0:nrows], in_=cstage[0:nrows, :], identity=ident_f[0:nrows, 0:nrows]))
        T.op("act", [bk[0]], [dkey], lambda e: e.copy(out=dst, in_=bank[0][:, 0:nrows]))
    for j in range(3):
        load_fm(cw_d[j].rearrange("(c p) -> c p", p=128), 44, cw[:, j, :], "cw")
    load_fm(cb_d.rearrange("(c p) -> c p", p=128), 44, cb[:, :], "cb")

    def ffn_prep(gi, x1_src, nt, halo, nseg):
        h2T = h2Tb[gi % 2]; hk = "h2T%d" % (gi % 2)
        ntile = (nt + 127) // 128
        L = nt // nseg
        W = nseg * (L + 2)
        h2v = h2T[:, :, 0:W].rearrange("p k (s c) -> p k s c", s=nseg)
        if halo == "prev":
            hp = h2Tb[(gi - 1) % 2]
            T.op("dve", ["h2T%d" % ((gi - 1) % 2)], [hk], lambda e: e.tensor_copy(out=h2T[:, :, 0:2], in_=hp[:, :, FG:FG + 2]))
        else:
            T.op("dve", [], [hk], lambda e: e.memset(h2v[:, :, :, 0:2], 0.0))
        yield
        for j in range(ntile):
            n = min(128, nt - j * 128)
            T.dma("sp", "xt", [], ["xt"], xt[0:n, :], x1_src[j * 128:j * 128 + n, :])
            yield from rmsn_stats(xt[0:n, :], "xt", n, D, 0)
            T.op("dve", ["xt", "ss0", "g_big"], ["hb"],
                 lambda e, n=n: e.scalar_tensor_tensor(out=hb[0:n, :], in0=xt[0:n, :], scalar=ss[0:n, 0:1], in1=g_big[0:n, :], op0=ALU.mult, op1=ALU.mult))
            yield
            if L >= 128:
                sg, c0 = (j * 128) // L, (j * 128) % L
                yield from tr8(hb.rearrange("p (k c) -> p k c", k=8), n, 128, h2v[:, :, sg, 2 + c0:2 + c0 + n], hk, "hb", 1)
            else:
                r = 128 // L
                yield from tr8(hb.rearrange("p (k c) -> p k c", k=8), n, 128, None, hk, "hb", 1,
                               evac=lambda e, ptv, j=j, r=r: e.copy(out=h2v[:, :, j * r:(j + 1) * r, 2:2 + L], in_=ptv[:, :, :].rearrange("p k (r l) -> p k r l", r=r)))

    def ffn_group(gi, x1_src, nt, y_dst_fn, last, ocv_dsts, halo, nseg=1, next_prep=None):
        h2T = h2Tb[gi % 2]; hk = "h2T%d" % (gi % 2)
        aT = aTb[gi % 2]; ak = "aT%d" % (gi % 2)
        ntile = (nt + 127) // 128
        L = nt // nseg
        W = nseg * (L + 2)

        def mm(c):
            i2 = c % 2
            for gv in range(2):
                cc = gv * NFC + c
                bi = 2 + 2 * gv + i2

                def um(e, cc=cc, bi=bi):
                    r = None
                    for k in range(8):
                        r = e.matmul(bank[bi][:, 0:W], lhsT=w_up_sb[:, k, cc * 128:(cc + 1) * 128], rhs=h2T[:, k, 0:W], start=(k == 0), stop=(k == 7))
                    return r
                T.op("pe", [hk, "wu%d" % cc], [bk[bi]], um)

        def ew(c):
            i2 = c % 2
            for gv in range(2):
                cc = gv * NFC + c
                bi = 2 + 2 * gv + i2
                pk = bk[bi]
                psv = bank[bi][:, 0:W].rearrange("p (s c) -> p s c", s=nseg)
                dk_ = ("tg%d" if gv == 0 else "tv%d") % i2
                dv = (tgs if gv == 0 else tvs)[i2][:, 0:nt].rearrange("p (s l) -> p s l", s=nseg)
                if halo == "state":
                    T.op("dve", [pk, "cprev"], [pk], lambda e, psv=psv, cc=cc: e.tensor_copy(out=psv[:, :, 0:2], in_=cprev[:, 0:nseg, :, cc]))
                T.op("act", [pk, "cw", "cb"], [dk_], lambda e, psv=psv, cc=cc, dv=dv: e.activation(out=dv, in_=psv[:, :, 2:2 + L], func=AF.Identity, scale=cw[:, 2, cc:cc + 1], bias=cb[:, cc:cc + 1]))
                T.op("dve", [pk, dk_, "cw"], [dk_], lambda e, psv=psv, cc=cc, dv=dv: e.scalar_tensor_tensor(out=dv, in0=psv[:, :, 1:1 + L], scalar=cw[:, 1, cc:cc + 1], in1=dv, op0=ALU.mult, op1=ALU.add))
                T.op("dve", [pk, dk_, "cw"], [dk_], lambda e, psv=psv, cc=cc, dv=dv: e.scalar_tensor_tensor(out=dv, in0=psv[:, :, 0:L], scalar=cw[:, 0, cc:cc + 1], in1=dv, op0=ALU.mult, op1=ALU.add))
                if last:
                    T.op("dve", [pk], ["cnew"], lambda e, psv=psv, cc=cc: e.tensor_copy(out=cnew[:, 0:nseg, :, cc], in_=psv[:, :, L:L + 2]))
            T.op("act", ["tg%d" % i2], ["tg%d" % i2], lambda e: e.activation(out=tgs[i2][:, 0:nt], in_=tgs[i2][:, 0:nt], func=AF.Silu))
            T.op("pool", ["tg%d" % i2, "tv%d" % i2], [ak], lambda e: e.tensor_tensor(out=aT[:, c, 0:nt], in0=tgs[i2][:, 0:nt], in1=tvs[i2][:, 0:nt], op=ALU.mult))

        def up_loop():
            mm(0)
            for c in range(NFC):
                if c + 1 < NFC:
                    mm(c + 1)
                ew(c)
                yield
        interleave(up_loop(), next_prep if (next_prep is not None) else None, 2)
        def epi():
            for j in range(ntile):
                n = min(128, nt - j * 128)
                T.dma("sp", "xe", [], ["xe"], xe[0:n, :], x1_src[j * 128:j * 128 + n, :])
                yield
                for hf in range(2):
                    bi = 6 + hf

                    def dm(e, bi=bi, hf=hf, j=j, n=n):
                        r = None
                        for c in range(NFC):
                            r = e.matmul(bank[bi][0:n, :], lhsT=aT[:, c, j * 128:j * 128 + n], rhs=w_dn_c[c][:, hf * 512:(hf + 1) * 512], start=(c == 0), stop=(c == NFC - 1))
                        return r
                    T.op("pe", [ak] + ["w_dn%d" % c_ for c_ in range(NFC)], [bk[bi]], dm)
                    yield
                    T.op("dve", [bk[bi], "xe"], ["xe"], lambda e, bi=bi, hf=hf, n=n: e.tensor_tensor(out=xe[0:n, hf * 512:(hf + 1) * 512], in0=bank[bi][0:n, :], in1=xe[0:n, hf * 512:(hf + 1) * 512], op=ALU.add))
                    yield
                T.op("act", ["xe"], ["yt", "ss3"], lambda e, n=n: e.activation(out=yt[0:n, :], in_=xe[0:n, :], func=AF.Square, accum_out=ss[0:n, 3:4]))
                yield
                T.op("dve", ["ss3"], ["ss3"], lambda e, n=n: e.tensor_scalar(out=ss[0:n, 3:4], in0=ss[0:n, 3:4], scalar1=1.0 / D, scalar2=1e-6, op0=ALU.mult, op1=ALU.add))
                yield
                T.op("pool", ["ss3", "mhalf"], ["ss3"], lambda e, n=n: e.tensor_tensor(out=ss[0:n, 3:4], in0=ss[0:n, 3:4], in1=mhalf[0:n, 0:1], op=ALU.pow))
                yield
                T.op("dve", ["xe", "ss3", "g_fin"], ["yt"],
                     lambda e, n=n: e.scalar_tensor_tensor(out=yt[0:n, :], in0=xe[0:n, :], scalar=ss[0:n, 3:4], in1=g_fin[0:n, :], op0=ALU.mult, op1=ALU.mult))
                yield
                T.dma("sp", "yt", ["yt"], [], y_dst_fn(j, n), yt[0:n, :])
                yield
        if last:
            for sg in range(nseg):
                T.op("pe", ["cnew", "cst_f"], [bk[0]], lambda e, sg=sg: e.transpose(out=bank[0][0:88, 0:128], in_=cnew[:, sg].rearrange("p j c -> p (j c)"), identity=ident_f))
                T.op("act", [bk[0]], ["cnewT"], lambda e: e.copy(out=cnewT[0:88, :], in_=bank[0][0:88, 0:128]))
                T.dma("sp", "cno", ["cnewT"], [], ocv_dsts[sg].rearrange("j (c p) -> (j c) p", p=128), cnewT[0:88, :])
        return epi()

    def chain_gens(gens):
        for g_ in gens:
            yield from g_

    for b in range(NS):
        for j in range(2):
            load_fm(scv[b, j].rearrange("(c p) -> c p", p=128), 44, cprev[:, b, j, :], "cprev")
    y_s_flat = y_s.rearrange("b t d -> (b t) d")
    ng = T_P // FG
    specs = []
    for g in range(ng):
        r0 = g * FG
        specs.append(dict(x1=x1_d[r0:r0 + FG, :], nt=FG, y=(lambda j, n, r0=r0: y_p[r0 + j * 128:r0 + j * 128 + n, :]), last=(g == ng - 1),
                          ocv=[ocv_p], halo=("zero" if g == 0 else "prev"), nseg=1))
    specs.append(dict(x1=x1_d[T_P:T_P + NS * T_S, :], nt=NS * T_S, y=(lambda j, n: y_s_flat[j * 128:j * 128 + n, :]), last=True,
                      ocv=[ocv_s[b] for b in range(NS)], halo="state", nseg=NS))
    run(ffn_prep(0, specs[0]["x1"], specs[0]["nt"], specs[0]["halo"], specs[0]["nseg"]))
    for c in range(NFC):
        for gv in range(2):
            cc = gv * NFC + c
            T.dma("pool", "w%d" % (wcnt[0] % 4), [], ["wu%d" % cc], w_up_sb[:, :, cc * 128:(cc + 1) * 128],
                  w_up_d[:, cc * 128:(cc + 1) * 128].rearrange("(k p) n -> p k n", p=128))
            wcnt[0] += 1
    pend = None
    for gi, sp_ in enumerate(specs):
        sides = []
        if gi + 1 < len(specs):
            n_ = specs[gi + 1]
            sides.append(ffn_prep(gi + 1, n_["x1"], n_["nt"], n_["halo"], n_["nseg"]))
        if pend is not None:
            sides.append(pend)
        pend = ffn_group(gi, sp_["x1"], sp_["nt"], sp_["y"], sp_["last"], sp_["ocv"], sp_["halo"], sp_["nseg"],
                         next_prep=chain_gens(sides) if sides else None)
    run(pend)

    T.finish("sp")
    return nc


def _consts():
    ident = np.eye(128, dtype=np.float32)
    s = np.arange(128)
    tri = (s[:, None] <= s[None, :]).astype(np.float32)
    ones = np.ones((128, 128), np.float32)
    mtri = np.where(s[:, None] <= s[None, :], 0.0, NEGM).astype(np.float32)
    mblk = np.where((s[:, None] // 64) <= (s[None, :] // 64), 0.0, NEGM).astype(np.float32)
    return np.concatenate([ident, tri, ones, mtri, mblk], axis=1)


def _rope_tab():
    half = 16
    inv = (np.float32(10000.0) ** (-np.arange(half, dtype=np.float32) / np.float32(half))).astype(np.float32)
    pos = np.concatenate([np.arange(T_P), PAST + np.arange(T_S)]).astype(np.float32)
    ang = (pos[:, None] * inv[None, :]).astype(np.float32)
    c = np.cos(ang).astype(np.float32); s = np.sin(ang).astype(np.float32)
    sc = np.float32(96.0 ** -0.5)
    return np.concatenate([c * sc, s * sc, c, s], axis=1).astype(np.float32)


_NC_CACHE = {}


def kernel(x_prompt, x_sample, cache_fox_k, cache_fox_v, cache_fox_logf, cache_mla_latent,
           cache_mla_krope, state_ffn_conv, attn_norm, w_in, b_forget, q_norm, w_q_up, kv_norm,
           w_uk, w_uv, w_out, ffn_norm, w_up, conv_w, conv_b, w_down, final_norm, _cores=None):
    f = lambda a: np.ascontiguousarray(np.asarray(a, dtype=np.float32))
    if "nc" not in _NC_CACHE:
        _NC_CACHE["nc"] = build_program()
    nc = _NC_CACHE["nc"]
    shared = {
        "attn_norm": f(attn_norm[0]), "w_in": f(w_in[0]), "b_forget": f(b_forget[0]), "q_norm": f(q_norm[0]),
        "w_q_up": f(w_q_up[0]), "kv_norm": f(kv_norm[0]), "w_uk": f(np.asarray(w_uk[0]).reshape(256, 512)),
        "w_uv": f(np.asarray(w_uv[0]).reshape(256, 512)), "w_out": f(w_out[0]), "ffn_norm": f(ffn_norm[0]),
        "w_up": f(w_up[0]), "conv_w": f(conv_w[0]), "conv_b": f(conv_b[0]), "w_down": f(w_down[0]),
        "final_norm": f(final_norm), "rope_tab": _rope_tab(), "consts": _consts(),
    }
    cores = list(range(NCORES)) if _cores is None else _cores
    in_maps = []
    for c in cores:
        sl = slice(NS * c, NS * (c + 1))
        m = dict(shared)
        m["xp"] = f(x_prompt[c]); m["xs"] = f(x_sample[sl])
        m["cfk"] = f(np.asarray(cache_fox_k[0, sl]).reshape(NS, PAST, 512))
        m["cfv"] = f(np.asarray(cache_fox_v[0, sl]).reshape(NS, PAST, 512))
        m["clf"] = f(cache_fox_logf[0, sl]); m["cml"] = f(cache_mla_latent[0, sl]); m["ckr"] = f(cache_mla_krope[0, sl])
        m["scv"] = f(state_ffn_conv[0, sl])
        in_maps.append(m)
    res = run_bass_kernel_spmd(nc, in_maps, core_ids=cores)
    R = res.results
    cat = lambda k: np.concatenate([np.asarray(r[k]) for r in R], axis=0)
    stk = lambda k: np.stack([np.asarray(r[k]) for r in R], axis=0)
    nb = len(cores)
    outs = (
        stk("y_p"), cat("y_s"),
        stk("ok_p").reshape(1, nb, T_P, 8, 64), stk("ov_p").reshape(1, nb, T_P, 8, 64), stk("olf_p").reshape(1, nb, T_P, 8),
        stk("oml_p").reshape(1, nb, T_P, 256), stk("okr_p").reshape(1, nb, T_P, 32), stk("ocv_p").reshape(1, nb, 2, 2 * DFF),
        cat("ok_s").reshape(1, nb * NS, T_S, 8, 64), cat("ov_s").reshape(1, nb * NS, T_S, 8, 64), cat("olf_s").reshape(1, nb * NS, T_S, 8),
        cat("oml_s").reshape(1, nb * NS, T_S, 256), cat("okr_s").reshape(1, nb * NS, T_S, 32), cat("ocv_s").reshape(1, nb * NS, 2, 2 * DFF),
    )
    return tuple(np.ascontiguousarray(o.astype(np.float32)) for o in outs)
```

```python
import os
import numpy as np
import concourse.bass as bass
import concourse.mybir as mybir
from concourse.bass_utils import run_bass_kernel_spmd

F32 = mybir.dt.float32
BF16 = mybir.dt.bfloat16
AF = mybir.ActivationFunctionType
ALU = mybir.AluOpType

NCORES = 8
D = 1024
T_P = 2048
NS = 4
T_S = 64
PAST = 4096
DIN = 2216
DFF = 2816
NFC = 22
QG = 256
FG = 256
NEGM = -30000.0


class Trk:
    def __init__(self, nc):
        self.nc = nc
        self.eng = {"pe": nc.tensor, "act": nc.scalar, "dve": nc.vector, "pool": nc.gpsimd, "sp": nc.sync}
        self.sem = {}
        self.cnt = {}
        for e in ("pe", "act", "dve", "pool"):
            self.sem[e] = nc.alloc_semaphore("sem_" + e)
            self.cnt[e] = 0
        self.dsem = {}
        self.seen = {e: {} for e in self.eng}
        self.lw = {}
        self.rd = {}

    def _wait(self, e, tok):
        name, sem, val = tok
        if name == "pe" and e == "pe":
            return
        s = self.seen[e]
        if s.get(name, 0) >= val:
            return
        s[name] = val
        self.eng[e].wait_ge(sem, val)

    def _deps(self, e, reads, writes):
        for k in list(reads) + list(writes):
            t = self.lw.get(k)
            if t is not None:
                self._wait(e, t)
        for k in writes:
            for t in self.rd.get(k, ()):
                self._wait(e, t)

    def _commit(self, tok, reads, writes):
        for k in reads:
            self.rd.setdefault(k, []).append(tok)
        for k in writes:
            self.lw[k] = tok
            self.rd[k] = []

    def op(self, e, reads, writes, fn):
        writes = list(writes) + [k for k in reads if k.startswith("ps") and k not in writes]
        reads = [k for k in reads if not k.startswith("ps")]
        self._deps(e, reads, writes)
        ins = fn(self.eng[e])
        self.cnt[e] += 1
        ins.then_inc(self.sem[e], 1)
        tok = (e, self.sem[e], self.cnt[e])
        self._commit(tok, reads, writes)
        return tok

    def dma(self, q, slot, reads, writes, out, in_, **kw):
        if slot not in self.dsem:
            self.dsem[slot] = [self.nc.alloc_semaphore("d_%d" % len(self.dsem)), 0]
        ds = self.dsem[slot]
        name = "d_" + str(slot)
        if ds[1] > 0:
            self._wait(q, (name, ds[0], ds[1]))
        self._deps(q, reads, writes)
        if q == "pool":
            kw.setdefault("max_dma_last_dim", 4096)
        ins = self.eng[q].dma_start(out=out, in_=in_, **kw)
        ds[1] += 16
        ins.then_inc(ds[0], 16)
        tok = (name, ds[0], ds[1])
        self._commit(tok, reads, writes)
        return tok

    def barrier(self):
        toks = []
        for slot, (sem, c) in self.dsem.items():
            if c > 0:
                toks.append(("d_" + str(slot), sem, c))
        for x in ("pe", "act", "dve", "pool"):
            if self.cnt[x] > 0:
                toks.append((x, self.sem[x], self.cnt[x]))
        for e in self.eng:
            for t in toks:
                if not (t[0] == e):
                    self._wait(e, t)

    def finish(self, e="sp"):
        for slot, (sem, c) in self.dsem.items():
            if c > 0:
                self._wait(e, ("d_" + str(slot), sem, c))
        for x in ("pe", "act", "dve", "pool"):
            if self.cnt[x] > 0:
                self._wait(e, (x, self.sem[x], self.cnt[x]))


def run(gen):
    for _ in gen:
        pass


def interleave(main, side, ratio):
    side_alive = side is not None
    for _ in main:
        if side_alive:
            for _i in range(ratio):
                try:
                    next(side)
                except StopIteration:
                    side_alive = False
                    break
    if side_alive:
        run(side)


def build_program():
    nc = bass.Bass("TRN2", target_bir_lowering=False, dynamic_dma_scratch_size=4096)
    T = Trk(nc)
    STAGE = int(os.environ.get("KSTAGE", "9"))

    def din(name, shape):
        return nc.dram_tensor(name, list(shape), F32, kind="ExternalInput").ap()

    def dout(name, shape):
        return nc.dram_tensor(name, list(shape), F32, kind="ExternalOutput").ap()

    xp = din("xp", [T_P, D]); xs = din("xs", [NS, T_S, D])
    cfk = din("cfk", [NS, PAST, 512]); cfv = din("cfv", [NS, PAST, 512]); clf = din("clf", [NS, PAST, 8])
    cml = din("cml", [NS, PAST, 256]); ckr = din("ckr", [NS, PAST, 32]); scv = din("scv", [NS, 2, 2 * DFF])
    g_attn_d = din("attn_norm", [D]); w_in_d = din("w_in", [D, DIN]); b_f_d = din("b_forget", [8])
    g_q_d = din("q_norm", [384]); w_qup_d = din("w_q_up", [384, 768]); g_kv_d = din("kv_norm", [256])
    w_uk_d = din("w_uk", [256, 512]); w_uv_d = din("w_uv", [256, 512]); w_out_d = din("w_out", [D, D])
    g_ffn_d = din("ffn_norm", [D]); w_up_d = din("w_up", [D, 2 * DFF]); cw_d = din("conv_w", [3, 2 * DFF])
    cb_d = din("conv_b", [2 * DFF]); w_dn_d = din("w_down", [DFF, D]); g_fin_d = din("final_norm", [D])
    rope_d = din("rope_tab", [T_P + T_S, 64])
    cst_d = din("consts", [128, 5 * 128])

    y_p = dout("y_p", [T_P, D]); y_s = dout("y_s", [NS, T_S, D])
    ok_p = dout("ok_p", [T_P, 512]); ov_p = dout("ov_p", [T_P, 512]); olf_p = dout("olf_p", [T_P, 8])
    oml_p = dout("oml_p", [T_P, 256]); okr_p = dout("okr_p", [T_P, 32]); ocv_p = dout("ocv_p", [2, 2 * DFF])
    ok_s = dout("ok_s", [NS, T_S, 512]); ov_s = dout("ov_s", [NS, T_S, 512]); olf_s = dout("olf_s", [NS, T_S, 8])
    oml_s = dout("oml_s", [NS, T_S, 256]); okr_s = dout("okr_s", [NS, T_S, 32]); ocv_s = dout("ocv_s", [NS, 2, 2 * DFF])
    x1_d = nc.dram_tensor("x1_scratch", [T_P + NS * T_S, D], F32).ap()

    sb = nc.alloc_sbuf_tensor
    cst_f = sb("cst_f", [128, 3 * 128], F32)
    cst_b = sb("cst_b", [128, 5 * 128], BF16)
    ident = cst_b[:, 0:128]; ones_b = cst_b[:, 256:384]; mtri = cst_b[:, 384:512]; mblk = cst_b[:, 512:640]
    ident_f = cst_f[:, 0:128]; tri_f = cst_f[:, 128:256]; ones_f = cst_f[:, 256:384]
    T.dma("sp", "c0", [], ["cst_f"], cst_f[:], cst_d[:, 0:384])
    T.dma("pool", "c1", [], ["cst_b"], cst_b[:], cst_d)
    g_big = sb("g_big", [128, D], F32)
    g_q = sb("g_q", [128, 384], F32); g_kv = sb("g_kv", [128, 256], F32)
    b_f = sb("b_f", [128, 8], F32)
    T.dma("sp", "c2", [], ["g_big"], g_big[:], g_attn_d.partition_broadcast(128))
    T.dma("sp", "c3", [], ["g_q"], g_q[:], g_q_d.partition_broadcast(128))
    T.dma("sp", "c4", [], ["g_kv"], g_kv[:], g_kv_d.partition_broadcast(128))
    T.dma("sp", "c5", [], ["b_f"], b_f[:], b_f_d.partition_broadcast(128))
    xt = sb("xt", [128, D], F32)
    hb = sb("hb", [128, D], BF16)
    ss = sb("ss", [128, 8], F32)
    tot = sb("tot", [128, 8], F32)
    mhalf = sb("mhalf", [128, 8], F32)
    T.op("dve", [], ["mhalf"], lambda e: e.memset(mhalf[:], -0.5))

    ARENA = (nc.sbuf_bytes_remaining - 256) // 2
    arena = sb("arena", [128, ARENA], BF16)
    off = [0]

    def carve(n, dt=BF16):
        nb = 2 * n if dt == F32 else n
        a = off[0]; off[0] += (nb + 15) // 16 * 16
        assert off[0] <= ARENA, (off[0], ARENA)
        v = arena[:, a:a + nb]
        return v.bitcast(F32) if dt == F32 else v

    NQT = QG // 128
    w_in_sb = carve(8 * DIN).rearrange("p (k n) -> p k n", k=8)
    w_qup_sb = carve(3 * 768).rearrange("p (k n) -> p k n", k=3)
    w_uk_sb = carve(2 * 512).rearrange("p (k n) -> p k n", k=2)
    w_uv_sb = carve(2 * 512).rearrange("p (k n) -> p k n", k=2)
    w_out_sb = carve(8 * D).rearrange("p (k n) -> p k n", k=8)
    KTf = carve(16 * 8 * 128).rearrange("p (t h s) -> p t h s", t=16, h=8)
    _o_ktm = off[0]
    KTm = carve(16 * 8 * 128).rearrange("p (t h s) -> p t h s", t=16, h=8)
    VW = 768
    _o_vf = off[0]
    Vf = carve(16 * VW).rearrange("p (t n) -> p t n", t=16)
    _o_vm = off[0]
    Vm = carve(16 * VW).rearrange("p (t n) -> p t n", t=16)
    _wd_slots = [_o_vf + 5 * VW + i * 1024 for i in range(8)] + [_o_vm + 5 * VW + i * 1024 for i in range(8)] + [_o_ktm + 6 * 1024 + i * 1024 for i in range(6)]
    w_dn_c = [arena[:, a:a + 1024] for a in _wd_slots]
    APcls = type(arena[:, 0:1])

    def vaug(vtile, h, nk=128):
        a = (h // 2) * 192 + (h % 2) * 64
        return vtile[0:nk, a:a + 128]

    def vdst(vtile, nt):
        a = vtile[0:nt, 0:64]
        return APcls(a.tensor, a.offset, [list(a.ap[0]), [192, 4], [128, 2], [1, 64]])
    hT = carve(8 * 128).rearrange("p (k s) -> p k s", k=8)
    kout = carve(512, F32)
    cn = carve(256, F32); cnb = carve(256); cT = carve(256).rearrange("p (k s) -> p k s", k=2)
    qn = carve(384); qnT = carve(384).rearrange("p (k s) -> p k s", k=3)
    Kaug = carve(8 * 70).rearrange("p (h d) -> p h d", h=8)
    Kmla = carve(8 * 96).rearrange("p (h d) -> p h d", h=8)
    QaugS = carve(NQT * 8 * 70).rearrange("p (t h d) -> p t h d", t=NQT, h=8)
    QmlaS = carve(NQT * 8 * 96).rearrange("p (t h d) -> p t h d", t=NQT, h=8)
    rp = carve(64, F32); kro = carve(32, F32)
    tmp16 = carve(8 * 16, F32).rearrange("p (h d) -> p h d", h=8)
    tmp16b = carve(8 * 16, F32).rearrange("p (h d) -> p h d", h=8)
    tmp16c = carve(16, F32).rearrange("p (h d) -> p h d", h=1)
    tmp16d = carve(16, F32).rearrange("p (h d) -> p h d", h=1)
    lf = carve(8, F32); Ft = carve(8, F32); r1 = carve(8, F32)
    QTf = carve(8 * QG).rearrange("p (h s) -> p h s", h=8)
    QTm = carve(8 * QG).rearrange("p (h s) -> p h s", h=8)
    PT = [carve(512) for _ in range(2)]
    OT = carve(8 * QG).rearrange("p (k s) -> p k s", k=8)
    rden = carve(512, F32)
    _kf = KTf[:, 3:16, :, :].rearrange("p t h s -> p (t h s)")
    _ko = [0]

    def kcarve(n, dt=BF16):
        nb = 2 * n if dt == F32 else n
        a = _ko[0]; _ko[0] += (nb + 15) // 16 * 16
        assert _ko[0] <= 13 * 1024
        v = _kf[:, a:a + nb]
        return v.bitcast(F32) if dt == F32 else v
    KaugC = [kcarve(8 * 70).rearrange("p (h d) -> p h d", h=8) for _ in range(2)]
    w_ukT = kcarve(8 * 256).rearrange("p (h c) -> p h c", h=8)
    qlatT = kcarve(2 * 512).rearrange("p (k n) -> p k n", k=2)
    qrT = kcarve(512).rearrange("p (h s) -> p h s", h=8)
    LT = [kcarve(3 * 128).rearrange("p (k s) -> p k s", k=3) for _ in range(2)]
    LTn = kcarve(3 * 64).rearrange("p (k s) -> p k s", k=3)
    rdenL = kcarve(512, F32)
    KstF = [kcarve(512, F32) for _ in range(2)]
    VstF = [kcarve(512, F32) for _ in range(2)]
    LstF = [kcarve(288, F32) for _ in range(2)]
    KTc = [KTf[:, 1, :, :], KTf[:, 2, :, :]]
    Vfc = [Vf[:, 1 + i, 0:512] for i in range(4)]
    Lb = [Vm[:, 1 + i, 0:288] for i in range(4)]
    _km = KTm[:, 2:16, :, :].rearrange("p t h s -> p (t h s)")
    lfc = _km[:, 0:512].bitcast(F32).rearrange("p (t h) -> p t h", t=32)
    Fc = _km[:, 512:1024].bitcast(F32).rearrange("p (t h) -> p t h", t=32)
    Ec = _km[:, 1024:1024 + 528].bitcast(F32).rearrange("p (t h) -> p t h", t=33)
    rc = _km[:, 2048:2560].bitcast(F32).rearrange("p (t h) -> p t h", t=32)
    pcs = _km[:, 3072:3072 + 768].rearrange("p (t h c) -> p t h c", t=32, h=8)

    wcnt = [0]

    def wload(dst3, src, K, key):
        for k in range(K):
            T.dma("pool", "w%d" % (wcnt[0] % 4), [], [key], dst3[:, k, :], src[k * 128:(k + 1) * 128, :])
            wcnt[0] += 1
    for bi_, (c0_, n_) in enumerate([(0, 512), (512, 512), (1024, 512), (1536, 392), (1928, 288)]):
        T.dma("pool", "w%d" % (wcnt[0] % 4), [], ["w_in%d" % bi_], w_in_sb[:, :, c0_:c0_ + n_],
              w_in_d[:, c0_:c0_ + n_].rearrange("(k p) n -> p k n", p=128))
        wcnt[0] += 1
    wload(w_qup_sb, w_qup_d, 3, "w_qup")
    wload(w_uk_sb, w_uk_d, 2, "w_uk")
    wload(w_uv_sb, w_uv_d, 2, "w_uv")
    wload(w_out_sb, w_out_d, 8, "w_out")

    bank = [nc.alloc_psum_tensor("bank%d" % i, [128, 512], F32) for i in range(8)]
    bk = ["ps%d" % i for i in range(8)]

    def bfv(i):
        return bank[i][:, :].bitcast(BF16).rearrange("p (h s) -> p h s", h=8)

    def preset(e):
        e.memset(Kaug[:], 1.0)
        e.memset(Vf[:], 1.0)
        e.memset(Vm[:], 1.0)
        return e.memset(QaugS[:], 1.0)
    T.op("dve", [], ["Kaug", "Qaug"] + ["V%d" % t for t in range(16)], preset)
    T.op("dve", [], ["tot"], lambda e: e.memset(tot[:], 0.0))

    def tr8(src, nt, w, dst, dkey, skey, tb, nh=8, evac=None):
        ptv = bfv(tb)

        def f(e):
            r = None
            for h in range(nh):
                r = e.transpose(out=ptv[0:w, h, 0:nt], in_=src[0:nt, h, 0:w], identity=ident[0:nt, 0:nt])
            return r
        T.op("pe", [skey, "cst_b"], [bk[tb]], f)
        yield
        if evac is None:
            T.op("act", [bk[tb]], [dkey], lambda e: e.copy(out=dst, in_=ptv[0:w, 0:nh, 0:nt]))
        else:
            T.op("act", [bk[tb]], [dkey], lambda e: evac(e, ptv))
        yield

    def rmsn_stats(src, skey, nt, width, col):
        c = "ss%d" % col
        T.op("act", [skey], ["hb", c],
             lambda e: e.activation(out=hb[0:nt, 0:width], in_=src, func=AF.Square, accum_out=ss[0:nt, col:col + 1]))
        yield
        T.op("dve", [c], [c],
             lambda e: e.tensor_scalar(out=ss[0:nt, col:col + 1], in0=ss[0:nt, col:col + 1], scalar1=1.0 / width, scalar2=1e-6, op0=ALU.mult, op1=ALU.add))
        yield
        T.op("pool", [c, "mhalf"], [c], lambda e: e.tensor_tensor(out=ss[0:nt, col:col + 1], in0=ss[0:nt, col:col + 1], in1=mhalf[0:nt, 0:1], op=ALU.pow))
        yield

    def rope(src, dst, nt, nh, c0, dcol, rk, wk):
        cs = rp[0:nt, c0:c0 + 16].unsqueeze(1).to_broadcast([nt, nh, 16])
        sn = rp[0:nt, c0 + 16:c0 + 32].unsqueeze(1).to_broadcast([nt, nh, 16])
        x1 = src[:, :, 0:16]; x2 = src[:, :, 16:32]
        a = tmp16[0:nt, 0:nh, :]; b = tmp16b[0:nt, 0:nh, :]
        R = list(rk) + ["rp"]
        T.op("dve", R, ["tmpa"], lambda e: e.tensor_tensor(out=a, in0=x1, in1=cs, op=ALU.mult)); yield
        T.op("dve", R, ["tmpb"], lambda e: e.tensor_tensor(out=b, in0=x2, in1=sn, op=ALU.mult)); yield
        T.op("dve", ["tmpa", "tmpb"], wk, lambda e: e.tensor_tensor(out=dst[:, :, dcol:dcol + 16], in0=a, in1=b, op=ALU.subtract)); yield
        T.op("dve", R, ["tmpa"], lambda e: e.tensor_tensor(out=a, in0=x1, in1=sn, op=ALU.mult)); yield
        T.op("dve", R, ["tmpb"], lambda e: e.tensor_tensor(out=b, in0=x2, in1=cs, op=ALU.mult)); yield
        T.op("dve", ["tmpa", "tmpb"], wk, lambda e: e.tensor_tensor(out=dst[:, :, dcol + 16:dcol + 32], in0=a, in1=b, op=ALU.add)); yield

    def upproj(nt, kmla_dst, kmkey, vm_dst, vkey, pa):
        def um(w):
            def f(e):
                r = None
                for k in range(2):
                    r = e.matmul(bank[pa][0:nt, :], lhsT=cT[:, k, 0:nt], rhs=w[:, k, :], start=(k == 0), stop=(k == 1))
                return r
            return f
        T.op("pe", ["cT", "w_uk"], [bk[pa]], um(w_uk_sb)); yield
        T.op("act", [bk[pa]], [kmkey], lambda e: e.copy(out=kmla_dst[0:nt, :, 0:64], in_=bank[pa][0:nt, :].rearrange("p (h d) -> p h d", h=8))); yield
        T.op("pe", ["cT", "w_uv"], [bk[pa]], um(w_uv_sb)); yield
        T.op("dve", [bk[pa]], [vkey], lambda e: e.tensor_copy(out=vm_dst, in_=bank[pa][0:nt, :].rearrange("p (a c d) -> p a c d", a=4, c=2))); yield

    def roundrobin(gens):
        gens = list(gens)
        while gens:
            alive = []
            for g_ in gens:
                try:
                    next(g_)
                    alive.append(g_)
                except StopIteration:
                    pass
            gens = alive
            yield

    def rope2(src, dst, nt, nh, c0, dcol, rk, wk, ta, tbb, ka, kb):
        cs = rp[0:nt, c0:c0 + 16].unsqueeze(1).to_broadcast([nt, nh, 16])
        sn = rp[0:nt, c0 + 16:c0 + 32].unsqueeze(1).to_broadcast([nt, nh, 16])
        x1 = src[:, :, 0:16]; x2 = src[:, :, 16:32]
        a = ta[0:nt, 0:nh, :]; b = tbb[0:nt, 0:nh, :]
        R = list(rk) + ["rp"]
        T.op("dve", R, [ka], lambda e: e.tensor_tensor(out=a, in0=x1, in1=cs, op=ALU.mult)); yield
        T.op("dve", R, [kb], lambda e: e.tensor_tensor(out=b, in0=x2, in1=sn, op=ALU.mult)); yield
        T.op("dve", [ka, kb], wk, lambda e: e.tensor_tensor(out=dst[:, :, dcol:dcol + 16], in0=a, in1=b, op=ALU.subtract)); yield
        T.op("dve", R, [ka], lambda e: e.tensor_tensor(out=a, in0=x1, in1=sn, op=ALU.mult)); yield
        T.op("dve", R, [kb], lambda e: e.tensor_tensor(out=b, in0=x2, in1=cs, op=ALU.mult)); yield
        T.op("dve", [ka, kb], wk, lambda e: e.tensor_tensor(out=dst[:, :, dcol + 16:dcol + 32], in0=a, in1=b, op=ALU.add)); yield

    def front_tile(x_src, nt, rope_row, o_k, o_v, o_lf, o_ml, o_kr, vf_dst, vm_dst, vkey, Qaug, Qmla, pa, tb):
        pal = list(pa) if isinstance(pa, (list, tuple)) else [pa]
        nb_ = len(pal)
        T.dma("sp", "xt", [], ["xt"], xt[0:nt, :], x_src)
        T.dma("sp", "rp", [], ["rp"], rp[0:nt, :], rope_d[rope_row:rope_row + nt, :])
        yield
        yield from rmsn_stats(xt[0:nt, :], "xt", nt, D, 0)
        T.op("dve", ["xt", "ss0", "g_big"], ["hb"],
             lambda e: e.scalar_tensor_tensor(out=hb[0:nt, :], in0=xt[0:nt, :], scalar=ss[0:nt, 0:1], in1=g_big[0:nt, :], op0=ALU.mult, op1=ALU.mult))
        yield
        yield from tr8(hb.rearrange("p (k c) -> p k c", k=8), nt, 128, hT[:, :, 0:nt], "hT", "hb", tb)

        def zmm(pb, c0, n):
            def f(e):
                r = None
                for k in range(8):
                    r = e.matmul(bank[pb][0:nt, 0:n], lhsT=hT[:, k, 0:nt], rhs=w_in_sb[:, k, c0:c0 + n], start=(k == 0), stop=(k == 7))
                return r
            return f

        def br_q():
            pb = pal[0 % nb_]; pA = bank[pb]; kA = bk[pb]
            T.op("pe", ["hT", "w_in0"], [kA], zmm(pb, 0, 512)); yield
            T.op("act", [kA], ["Qaug"], lambda e: e.activation(out=Qaug[0:nt, :, 0:64], in_=pA[0:nt, :].rearrange("p (h d) -> p h d", h=8), func=AF.Copy, scale=0.125)); yield

        def br_k():
            pb = pal[1 % nb_]; pA = bank[pb]; kA = bk[pb]
            T.op("pe", ["hT", "w_in1"], [kA], zmm(pb, 512, 512)); yield
            T.op("act", [kA], ["kout"], lambda e: e.copy(out=kout[0:nt, :], in_=pA[0:nt, :])); yield
            T.op("dve", [kA], ["Kaug"], lambda e: e.tensor_copy(out=Kaug[0:nt, :, 0:64], in_=pA[0:nt, :].rearrange("p (h d) -> p h d", h=8))); yield
            T.dma("sp", "kout", ["kout"], [], o_k, kout[0:nt, :])
            yield

        def br_v():
            pb = pal[2 % nb_]; pA = bank[pb]; kA = bk[pb]
            T.op("pe", ["hT", "w_in2"], [kA], zmm(pb, 1024, 512)); yield
            T.op("act", [kA], ["xt"], lambda e: e.copy(out=xt[0:nt, 0:512], in_=pA[0:nt, :])); yield
            T.op("dve", [kA], [vkey], lambda e: e.tensor_copy(out=vf_dst, in_=pA[0:nt, :].rearrange("p (a c d) -> p a c d", a=4, c=2))); yield
            T.dma("sp", "vout", ["xt"], [], o_v, xt[0:nt, 0:512])
            yield

        def br_f():
            pb = pal[0 % nb_]; pA = bank[pb]; kA = bk[pb]
            T.op("pe", ["hT", "w_in3"], [kA], zmm(pb, 1536, 392)); yield
            T.op("dve", [kA, "b_f"], ["r1"], lambda e: e.tensor_tensor(out=r1[0:nt, :], in0=pA[0:nt, 0:8], in1=b_f[0:nt, :], op=ALU.add)); yield
            yield from rmsn_stats(pA[0:nt, 8:392], kA, nt, 384, 1)
            T.op("dve", [kA, "ss1", "g_q"], ["qn"],
                 lambda e: e.scalar_tensor_tensor(out=qn[0:nt, :], in0=pA[0:nt, 8:392], scalar=ss[0:nt, 1:2], in1=g_q[0:nt, :], op0=ALU.mult, op1=ALU.mult)); yield
            yield from roundrobin([br_f1(), br_f2()])

        def br_f1():
            pb = pal[2 % nb_]; pA = bank[pb]; kA = bk[pb]
            T.op("act", ["r1"], ["Ft"], lambda e: e.activation(out=Ft[0:nt, :], in_=r1[0:nt, :], func=AF.Exp, scale=-1.0)); yield
            T.op("dve", ["Ft"], ["r1"], lambda e: e.tensor_scalar(out=r1[0:nt, :], in0=Ft[0:nt, :], scalar1=1.0, scalar2=None, op0=ALU.add)); yield
            T.op("act", ["r1"], ["Ft"], lambda e: e.activation(out=Ft[0:nt, :], in_=r1[0:nt, :], func=AF.Ln)); yield
            T.op("dve", ["Ft"], ["lf"], lambda e: e.tensor_scalar(out=lf[0:nt, :], in0=Ft[0:nt, :], scalar1=-1.0, scalar2=None, op0=ALU.mult)); yield
            T.dma("sp", "lf", ["lf"], [], o_lf, lf[0:nt, :])

            def fmm(e):
                e.matmul(pA[0:nt, 0:8], lhsT=tri_f[0:nt, 0:nt], rhs=lf[0:nt, :], start=True, stop=True)
                return e.matmul(pA[0:nt, 8:16], lhsT=ones_f[0:nt, 0:nt], rhs=lf[0:nt, :], start=True, stop=True)
            T.op("pe", ["lf", "cst_f"], [kA], fmm); yield
            T.op("dve", [kA, "tot"], ["Ft"], lambda e: e.tensor_tensor(out=Ft[0:nt, :], in0=pA[0:nt, 0:8], in1=tot[0:nt, :], op=ALU.add)); yield
            T.op("dve", [kA, "tot"], ["tot"], lambda e: e.tensor_tensor(out=tot[0:nt, :], in0=pA[0:nt, 8:16], in1=tot[0:nt, :], op=ALU.add)); yield
            T.op("dve", ["Ft"], ["Qaug"], lambda e: e.tensor_copy(out=Qaug[0:nt, :, 67], in_=Ft[0:nt, :])); yield
            T.op("dve", ["Ft", "Qaug"], ["r1"], lambda e: e.tensor_tensor(out=r1[0:nt, :], in0=Ft[0:nt, :], in1=Qaug[0:nt, :, 67], op=ALU.subtract)); yield
            T.op("dve", ["r1"], ["Qaug"], lambda e: e.tensor_copy(out=Qaug[0:nt, :, 68], in_=r1[0:nt, :])); yield
            T.op("dve", ["r1", "Qaug"], ["Ft"], lambda e: e.tensor_tensor(out=Ft[0:nt, :], in0=r1[0:nt, :], in1=Qaug[0:nt, :, 68], op=ALU.subtract)); yield
            T.op("dve", ["Ft"], ["Qaug"], lambda e: e.tensor_copy(out=Qaug[0:nt, :, 69], in_=Ft[0:nt, :])); yield
            T.op("dve", ["Qaug"], ["Kaug"], lambda e: e.tensor_scalar(out=Kaug[0:nt, :, 64:67], in0=Qaug[0:nt, :, 67:70], scalar1=-1.0, scalar2=None, op0=ALU.mult)); yield

        def br_f2():
            yield from tr8(qn.rearrange("p (k c) -> p k c", k=3), nt, 128, qnT[:, :, 0:nt], "qnT", "qn", tb, nh=3)
            sc = 96.0 ** -0.5
            for half in range(2):
                pb = pal[0]; pA = bank[pb]; kA = bk[pb]

                def qmm(e, half=half, pA=pA):
                    r = None
                    for k in range(3):
                        r = e.matmul(pA[0:nt, 0:384], lhsT=qnT[:, k, 0:nt], rhs=w_qup_sb[:, k, half * 384:(half + 1) * 384], start=(k == 0), stop=(k == 2))
                    return r
                T.op("pe", ["qnT", "w_qup"], [kA], qmm); yield
                pv = pA[0:nt, 0:384].rearrange("p (h d) -> p h d", h=4)
                hs = slice(half * 4, half * 4 + 4)
                T.op("act", [kA], ["Qmla"], lambda e, pv=pv, hs=hs: e.activation(out=Qmla[0:nt, hs, 0:64], in_=pv[:, :, 0:64], func=AF.Copy, scale=sc)); yield
                yield from rope2(pv[:, :, 64:96], Qmla[0:nt, hs, :], nt, 4, 0, 64, [kA], ["Qmla"], tmp16, tmp16b, "tmpa", "tmpb")

        def br_c():
            pb = pal[1 % nb_]; pA = bank[pb]; kA = bk[pb]
            T.op("pe", ["hT", "w_in4"], [kA], zmm(pb, 1928, 288)); yield
            yield from rmsn_stats(pA[0:nt, 0:256], kA, nt, 256, 2)
            T.op("dve", [kA, "ss2", "g_kv"], ["cn"],
                 lambda e: e.scalar_tensor_tensor(out=cn[0:nt, :], in0=pA[0:nt, 0:256], scalar=ss[0:nt, 2:3], in1=g_kv[0:nt, :], op0=ALU.mult, op1=ALU.mult)); yield
            T.dma("sp", "cn", ["cn"], [], o_ml, cn[0:nt, :])
            T.op("act", ["cn"], ["cnb"], lambda e: e.copy(out=cnb[0:nt, :], in_=cn[0:nt, :])); yield
            yield from rope2(pA[0:nt, 256:288].unsqueeze(1), kro[0:nt, :].unsqueeze(1), nt, 1, 32, 0, [kA], ["kro"], tmp16c, tmp16d, "tmpc", "tmpd")
            T.dma("sp", "kro", ["kro"], [], o_kr, kro[0:nt, :])
            T.op("dve", ["kro"], ["Kmla"], lambda e: e.tensor_copy(out=Kmla[0:nt, :, 64:96], in_=kro[0:nt, :].unsqueeze(1).to_broadcast([nt, 8, 32]))); yield
            yield from tr8(cnb.rearrange("p (k c) -> p k c", k=2), nt, 128, cT[:, :, 0:nt], "cT", "cnb", tb, nh=2)
            yield from upproj(nt, Kmla, "Kmla", vm_dst, vkey, pal[1 % nb_])

        yield from roundrobin([br_q(), br_k(), br_v()])
        yield from roundrobin([br_f(), br_c()])

    def outproj(x_src, nt, ot_view, x1_dst, pa):
        T.dma("sp", "xt", [], ["xt"], xt[0:nt, :], x_src)
        yield
        for hf in range(2):
            def f(e, hf=hf):
                r = None
                for k in range(8):
                    r = e.matmul(bank[pa][0:nt, :], lhsT=ot_view[:, k, :], rhs=w_out_sb[:, k, hf * 512:(hf + 1) * 512], start=(k == 0), stop=(k == 7))
                return r
            T.op("pe", ["OT", "w_out"], [bk[pa]], f)
            yield
            T.op("dve", [bk[pa], "xt"], ["xt"], lambda e, hf=hf: e.tensor_tensor(out=xt[0:nt, hf * 512:(hf + 1) * 512], in0=bank[pa][0:nt, :], in1=xt[0:nt, hf * 512:(hf + 1) * 512], op=ALU.add))
            yield
        T.dma("sp", "x1o", ["xt"], ["x1d"], x1_dst, xt[0:nt, :])
        yield

    NG = T_P // QG

    def fe_group(g):
        for j in range(NQT):
            t = g * NQT + j
            r0 = t * 128
            yield from front_tile(xp[r0:r0 + 128, :], 128, r0, ok_p[r0:r0 + 128, :], ov_p[r0:r0 + 128, :], olf_p[r0:r0 + 128, :],
                                  oml_p[r0:r0 + 128, :], okr_p[r0:r0 + 128, :], vdst(Vf[:, t, :], 128), vdst(Vm[:, t, :], 128), "V%d" % t,
                                  QaugS[:, j], QmlaS[:, j], [0, 6, 7], 1)
            yield from tr8(Kaug, 128, 70, KTf[0:70, t, :, :], "KTf%d" % t, "Kaug", 1)
            yield from tr8(Kmla, 128, 96, KTm[0:96, t, :, :], "KTm%d" % t, "Kmla", 1)

    def att_group(g):
        q0 = g * QG
        nkt = (q0 + QG) // 128
        blocks = []
        for typ in range(2):
            for h in range(8):
                for kt in range(nkt):
                    blocks.append((typ, h, kt))
        nb = len(blocks)

        def qk_ins(e, i):
            typ, h, kt = blocks[i]
            KT, QT, dk, msk = (KTf, QTf, 70, mtri) if typ == 0 else (KTm, QTm, 96, mblk)
            c0 = max(0, kt * 128 - q0)
            n = QG - c0
            diag = kt * 128 >= q0
            sbk = bank[2 + (i % 2)]
            r = e.matmul(sbk[:, 0:n], lhsT=KT[0:dk, kt, h, :], rhs=QT[0:dk, h, c0:QG], start=True, stop=not diag)
            if diag:
                r = e.matmul(sbk[:, 0:128], lhsT=ident, rhs=msk, start=False, stop=True)
            return r

        def pv_ins(e, i):
            typ, h, kt = blocks[i]
            V = Vf if typ == 0 else Vm
            oi = 4 + (typ * 8 + h) % 2
            c0 = max(0, kt * 128 - q0)
            n = QG - c0
            return e.matmul(bank[oi][:, c0:QG], lhsT=vaug(V[:, kt, :], h), rhs=PT[i % 2][:, 0:n], start=(kt == 0), stop=(kt == nkt - 1))

        def keys_qk(i):
            typ, h, kt = blocks[i]
            return [("KTf%d" if typ == 0 else "KTm%d") % kt, "QTf" if typ == 0 else "QTm", "cst_b"], [bk[2 + (i % 2)]]

        def keys_pv(i):
            typ, h, kt = blocks[i]
            return ["PT%d" % (i % 2), "V%d" % kt], [bk[4 + (typ * 8 + h) % 2]]

        def emit_exp(i):
            typ, h, kt = blocks[i]
            n = QG - max(0, kt * 128 - q0)
            sbi = 2 + (i % 2)
            T.op("act", [bk[sbi]], ["PT%d" % (i % 2)], lambda e: e.activation(out=PT[i % 2][:, 0:n], in_=bank[sbi][:, 0:n], func=AF.Exp))

        def emit_norm(i):
            typ, h, kt = blocks[i]
            oi = 4 + (typ * 8 + h) % 2
            pair, half = h // 2, h % 2
            rs = slice(half * 64, half * 64 + 64)
            rd = slice((1 - half) * 64, (1 - half) * 64 + 64)
            T.op("dve", [bk[oi]], ["rden"], lambda e: e.reciprocal(out=rden[rd, 0:QG], in_=bank[oi][rd, 0:QG]))
            T.op("dve", [bk[oi], "rden"], ["OT"], lambda e: e.tensor_tensor(out=OT[rs, typ * 4 + pair, :], in0=bank[oi][rs, 0:QG], in1=rden[rd, 0:QG], op=ALU.mult))

        r_, w_ = keys_qk(0)
        T.op("pe", r_, w_, lambda e: qk_ins(e, 0))
        for i in range(nb + 1):
            if i < nb:
                emit_exp(i)
            rr, ww = [], []
            if i + 1 < nb:
                a, b2 = keys_qk(i + 1); rr += a; ww += b2
            if i >= 1:
                a, b2 = keys_pv(i - 1); rr += a; ww += b2

            def f(e, i=i):
                r = None
                if i + 1 < nb:
                    r = qk_ins(e, i + 1)
                if i >= 1:
                    r = pv_ins(e, i - 1)
                return r
            if rr:
                T.op("pe", rr, ww, f)
            if i >= 1 and blocks[i - 1][2] == nkt - 1:
                emit_norm(i - 1)
            yield

    def qtrans_gen():
        for j in range(NQT):
            yield from tr8(QaugS[:, j], 128, 70, QTf[0:70, :, j * 128:(j + 1) * 128], "QTf", "Qaug", 1)
            yield from tr8(QmlaS[:, j], 128, 96, QTm[0:96, :, j * 128:(j + 1) * 128], "QTm", "Qmla", 6)

    def outproj_gen(g):
        for j in range(NQT):
            r0 = g * QG + j * 128
            yield from outproj(xp[r0:r0 + 128, :], 128, OT[:, :, j * 128:(j + 1) * 128], x1_d[r0:r0 + 128, :], 0)

    run(fe_group(0))
    run(qtrans_gen())
    for g in range(NG if STAGE >= 3 else 1):
        if STAGE < 2:
            break
        nblk = 16 * ((g * QG + QG) // 128)
        side = fe_group(g + 1) if g + 1 < NG else None
        ratio = max(1, -(-NQT * 110 // nblk))
        interleave(att_group(g), side, ratio)
        run(roundrobin([outproj_gen(g)] + ([qtrans_gen()] if g + 1 < NG else [])))

    T.barrier()
    NQ = T_S
    T.op("dve", [], ["KaugC0", "KaugC1"], lambda e: (e.memset(KaugC[0][:], 1.0), e.memset(KaugC[1][:], 1.0))[1])
    for kc in range(2):
        run(tr8(w_uk_sb[:, kc, :].rearrange("p (h d) -> p h d", h=8), 128, 64, w_ukT[0:64, :, kc * 128:(kc + 1) * 128], "w_ukT", "w_uk", 1))
    def cumsum_gen(b):
        for q4 in range(4):
            T.dma("sp", "lfc%d" % q4, [], ["lfc"], lfc[:, q4 * 8:(q4 + 1) * 8, :], clf[b, q4 * 1024:(q4 + 1) * 1024, :].rearrange("(t p) h -> p t h", p=128))
        yield

        def fcm(e):
            e.matmul(bank[7][:, 0:256], lhsT=tri_f, rhs=lfc.rearrange("p t h -> p (t h)"), start=True, stop=True)
            return e.matmul(bank[7][:, 256:512], lhsT=ones_f, rhs=lfc.rearrange("p t h -> p (t h)"), start=True, stop=True)
        T.op("pe", ["lfc", "cst_f"], [bk[7]], fcm); yield
        T.op("dve", [], ["Ec"], lambda e: e.memset(Ec[:, 0, :], 0.0)); yield
        T.op("act", [bk[7]], ["rc"], lambda e: e.copy(out=rc, in_=bank[7][:, 256:512].rearrange("p (t h) -> p t h", t=32))); yield
        for t in range(32):
            T.op("dve", ["rc", "Ec"], ["Ec"], lambda e, t=t: e.tensor_tensor(out=Ec[:, t + 1, :], in0=Ec[:, t, :], in1=rc[:, t, :], op=ALU.add)); yield
        T.op("dve", [bk[7], "Ec"], ["Fc"], lambda e: e.tensor_tensor(out=Fc, in0=bank[7][:, 0:256].rearrange("p (t h) -> p t h", t=32), in1=Ec[:, 0:32, :], op=ALU.add)); yield
        T.op("dve", ["Ec"], ["tot"], lambda e: e.tensor_copy(out=tot[:], in_=Ec[:, 32, :])); yield
        T.op("dve", ["Fc"], ["pcs"], lambda e: e.tensor_copy(out=pcs[:, :, :, 0], in_=Fc)); yield
        T.op("dve", ["Fc", "pcs"], ["rc"], lambda e: e.tensor_tensor(out=rc, in0=Fc, in1=pcs[:, :, :, 0], op=ALU.subtract)); yield
        T.op("dve", ["rc"], ["pcs"], lambda e: e.tensor_copy(out=pcs[:, :, :, 1], in_=rc)); yield
        T.op("dve", ["rc", "pcs"], ["Fc"], lambda e: e.tensor_tensor(out=Fc, in0=rc, in1=pcs[:, :, :, 1], op=ALU.subtract)); yield
        T.op("dve", ["Fc"], ["pcs"], lambda e: e.tensor_copy(out=pcs[:, :, :, 2], in_=Fc)); yield
        T.op("dve", ["pcs"], ["pcs"], lambda e: e.tensor_scalar(out=pcs, in0=pcs, scalar1=-1.0, scalar2=None, op0=ALU.mult)); yield

    NJ = NS if STAGE >= 5 else (1 if STAGE == 4 else 0)
    if NJ > 0:
        run(cumsum_gen(0))
    for b in range(NJ):
        run(front_tile(xs[b], NQ, T_P, ok_s[b], ov_s[b], olf_s[b], oml_s[b], okr_s[b], vdst(Vf[:, 0, :], NQ), vdst(Vm[:, 0, :], NQ), "V0",
                       QaugS[:, 0], QmlaS[:, 0], [0, 6, 7], 1))
        run(tr8(QaugS[:, 0], NQ, 70, QTf[0:70, :, 0:NQ], "QTf", "Qaug", 1))
        run(tr8(QmlaS[:, 0][:, :, 0:64], NQ, 64, QTm[0:64, :, 0:NQ], "QTm", "Qmla", 1))
        run(tr8(QmlaS[:, 0][:, :, 64:96], NQ, 32, qrT[0:32, :, 0:NQ], "qrT", "Qmla", 1))
        run(tr8(Kaug, NQ, 70, KTf[0:70, 0, :, 0:NQ], "KTf0", "Kaug", 1))
        for typ in range(1):
            QT, dk, qk = (QTf, 70, "QTf") if typ == 0 else (QTm, 96, "QTm")

            def st_load(kt):
                r0 = kt * 128
                i2 = kt % 2
                T.dma("sp", "kc%d" % i2, [], ["KstF%d" % i2], KstF[i2], cfk[b, r0:r0 + 128, :])
                T.dma("pool", "vc%d" % (kt % 4), [], ["Vc%d" % (kt % 4)], Vfc[kt % 4], cfv[b, r0:r0 + 128, :])

            def st_prep(kt):
                i2 = kt % 2
                if typ == 0:
                    T.op("dve", ["KstF%d" % i2], ["KaugC%d" % i2], lambda e: e.tensor_copy(out=KaugC[i2][:, :, 0:64], in_=KstF[i2].rearrange("p (h d) -> p h d", h=8)))
                    T.op("dve", ["pcs"], ["KaugC%d" % i2], lambda e: e.tensor_copy(out=KaugC[i2][:, :, 64:67], in_=pcs[:, kt, :, :]))
                    run(tr8(KaugC[i2], 128, 70, KTc[i2][0:70, :, :], "KTc%d" % i2, "KaugC%d" % i2, 1 if i2 == 0 else 6))
                else:
                    run(tr8(latc[i2].rearrange("p (k c) -> p k c", k=2), 128, 128, cT[:, :, :], "cT", "latc%d" % i2, 1, nh=2))
                    pass
                    T.op("dve", ["krc%d" % i2], ["Kmla"], lambda e: e.tensor_copy(out=Kmla[:, :, 64:96], in_=krc[i2][:].unsqueeze(1).to_broadcast([128, 8, 32])))
                    run(tr8(Kmla, 128, 96, KTc[i2][0:96, :, :], "KTc%d" % i2, "Kmla", 6))

            def srcs(kt):
                if kt < 32:
                    i2 = kt % 2
                    return KTc[i2], "KTc%d" % i2, Vfc[kt % 4], "Vc%d" % (kt % 4), 128
                return KTf[:, 0], "KTf0", Vf[:, 0, :], "V0", NQ

            def st_qk(kt):
                KTsrc, kk, _, _, nk = srcs(kt)
                sbi = 2 + kt % 2
                sbk = bank[sbi]; ptb = PT[kt % 2]

                def qkf(e):
                    r = None
                    for h in range(8):
                        r = e.matmul(sbk[0:nk, h * NQ:(h + 1) * NQ], lhsT=KTsrc[0:dk, h, 0:nk], rhs=QT[0:dk, h, 0:NQ], start=(h == 0), stop=True, skip_group_check=True)
                        if kt == 32 and typ == 0:
                            r = e.matmul(sbk[0:nk, h * NQ:(h + 1) * NQ], lhsT=ident[0:NQ, 0:NQ], rhs=mtri[0:NQ, 0:NQ], start=False, stop=True, skip_group_check=True)
                    return r
                T.op("pe", [kk, qk, "cst_b"], [bk[sbi]], qkf)
                T.op("act", [bk[sbi]], ["PT%d" % (kt % 2)], lambda e: e.activation(out=ptb[0:nk, :], in_=sbk[0:nk, :], func=AF.Exp))

            def st_pv(kt):
                _, _, Vsrc, vk, nk = srcs(kt)
                ptb = PT[kt % 2]

                def pvf(e):
                    for h in range(8):
                        pair = h // 2
                        lt_ = Vsrc[0:nk, pair * 128:(pair + 1) * 128] if kt < 32 else vaug(Vsrc, h, nk)
                        e.matmul(bank[4][:, h * NQ:(h + 1) * NQ], lhsT=lt_, rhs=ptb[0:nk, h * NQ:(h + 1) * NQ],
                                 start=(kt == 0 and h == 0), stop=(kt == 32), skip_group_check=True)
                    return e.matmul(bank[5][:, :], lhsT=ones_b[0:nk, :], rhs=ptb[0:nk, :], start=(kt == 0), stop=(kt == 32))
                T.op("pe", ["PT%d" % (kt % 2), vk, "cst_b"], [bk[4], bk[5]], pvf)

            for it in range(33 + 3):
                if b == 0 and it < NFC:
                    T.dma("pool", "w%d" % (wcnt[0] % 4), [], ["w_dn%d" % it], w_dn_c[it], w_dn_d[it * 128:(it + 1) * 128, :])
                    wcnt[0] += 1
                if it < 32:
                    st_load(it)
                if 0 <= it - 1 < 32:
                    st_prep(it - 1)
                if 0 <= it - 2 < 33:
                    st_qk(it - 2)
                if 0 <= it - 3 < 33:
                    st_pv(it - 3)
            for half in range(2):
                rs = slice(half * 64, half * 64 + 64)
                dv = bank[5][rs, :].rearrange("p (a c q) -> p a c q", a=4, c=2)[:, :, half, :]
                ov = bank[4][rs, :].rearrange("p (a c q) -> p a c q", a=4, c=2)[:, :, half, :]
                rv = rden[rs, 0:4 * NQ].rearrange("p (a q) -> p a q", a=4)
                T.op("dve", [bk[5]], ["rden"], lambda e, dv=dv, rv=rv: e.reciprocal(out=rv, in_=dv))
                T.op("dve", [bk[4], "rden"], ["OT"], lambda e, ov=ov, rv=rv, rs=rs, typ=typ: e.tensor_tensor(out=OT[rs, typ * 4:typ * 4 + 4, 0:NQ], in0=ov, in1=rv, op=ALU.mult))

        for kc in range(2):
            bi = 0 if kc == 0 else 7

            def qlm(e, kc=kc, bi=bi):
                r = None
                for h in range(8):
                    r = e.matmul(bank[bi][:, h * NQ:(h + 1) * NQ], lhsT=w_ukT[0:64, h, kc * 128:(kc + 1) * 128], rhs=QTm[0:64, h, 0:NQ],
                                 start=(h == 0), stop=True, skip_group_check=True)
                return r
            T.op("pe", ["w_ukT", "QTm"], [bk[bi]], qlm)
            if kc == 0:
                T.op("act", [bk[bi]], ["qlatT"], lambda e: e.copy(out=qlatT[:, 0, :], in_=bank[0][:, :]))
            else:
                T.op("dve", [bk[bi]], ["qlatT"], lambda e: e.tensor_copy(out=qlatT[:, 1, :], in_=bank[7][:, :]))

        def ltn(e):
            ptv = bfv(1)
            e.transpose(out=ptv[:, 0, 0:NQ], in_=cnb[0:NQ, 0:128], identity=ident[0:NQ, 0:NQ])
            e.transpose(out=ptv[:, 1, 0:NQ], in_=cnb[0:NQ, 128:256], identity=ident[0:NQ, 0:NQ])
            return e.transpose(out=ptv[0:32, 2, 0:NQ], in_=Kmla[0:NQ, 0, 64:96], identity=ident[0:NQ, 0:NQ])
        T.op("pe", ["cnb", "Kmla", "cst_b"], [bk[1]], ltn)
        T.op("act", [bk[1]], ["LTn"], lambda e: e.copy(out=LTn[:, 0:2, :], in_=bfv(1)[:, 0:2, 0:NQ]))
        T.op("act", [bk[1]], ["LTn"], lambda e: e.copy(out=LTn[0:32, 2, :], in_=bfv(1)[0:32, 2, 0:NQ]))

        def a_load(kt):
            r0 = kt * 128
            i4 = kt % 4
            T.dma("sp", "lc%d" % (kt % 2), [], ["LstF%d" % (kt % 2)], LstF[kt % 2][:, 0:256], cml[b, r0:r0 + 128, :])
            T.dma("sp", "rc%d" % (kt % 2), [], ["LstF%d" % (kt % 2)], LstF[kt % 2][:, 256:288], ckr[b, r0:r0 + 128, :])

        def a_prep(kt):
            i4 = kt % 4; i2 = kt % 2
            tb = 1 if i2 == 0 else 6
            T.op("dve", ["LstF%d" % i2], ["Lb%d" % i4], lambda e: e.tensor_copy(out=Lb[i4], in_=LstF[i2]))

            def f(e):
                ptv = bfv(tb)
                e.transpose(out=ptv[:, 0, :], in_=Lb[i4][:, 0:128], identity=ident)
                e.transpose(out=ptv[:, 1, :], in_=Lb[i4][:, 128:256], identity=ident)
                return e.transpose(out=ptv[0:32, 2, :], in_=Lb[i4][:, 256:288], identity=ident)
            T.op("pe", ["Lb%d" % i4, "cst_b"], [bk[tb]], f)
            T.op("act", [bk[tb]], ["LT%d" % i2], lambda e: e.copy(out=LT[i2][:, 0:2, :], in_=bfv(tb)[:, 0:2, :]))
            T.op("dve", [bk[tb]], ["LT%d" % i2], lambda e: e.tensor_copy(out=LT[i2][0:32, 2, :], in_=bfv(tb)[0:32, 2, :]))

        def a_qk(kt):
            lt, ltk, nk = (LT[kt % 2], "LT%d" % (kt % 2), 128) if kt < 32 else (LTn, "LTn", NQ)
            sbi = 2 + kt % 2
            ptb = PT[kt % 2]

            def f(e):
                e.matmul(bank[sbi][0:nk, :], lhsT=lt[:, 0, 0:nk], rhs=qlatT[:, 0, :], start=True, stop=False)
                e.matmul(bank[sbi][0:nk, :], lhsT=lt[:, 1, 0:nk], rhs=qlatT[:, 1, :], start=False, stop=False)
                return e.matmul(bank[sbi][0:nk, :], lhsT=lt[0:32, 2, 0:nk], rhs=qrT[0:32, :, :].rearrange("p h s -> p (h s)"), start=False, stop=True)
            T.op("pe", [ltk, "qlatT", "qrT"], [bk[sbi]], f)
            T.op("act", [bk[sbi]], ["PT%d" % (kt % 2)], lambda e: e.activation(out=ptb[0:nk, :], in_=bank[sbi][0:nk, :], func=AF.Exp))

        def a_pv(kt):
            lsrc, lk, nk = (Lb[kt % 4], "Lb%d" % (kt % 4), 128) if kt < 32 else (cnb, "cnb", NQ)
            ptb = PT[kt % 2]

            def f(e):
                e.matmul(bank[4][:, :], lhsT=lsrc[0:nk, 0:128], rhs=ptb[0:nk, :], start=(kt == 0), stop=(kt == 32))
                e.matmul(bank[5][:, :], lhsT=lsrc[0:nk, 128:256], rhs=ptb[0:nk, :], start=(kt == 0), stop=(kt == 32))
                return e.matmul(bank[0][:, :], lhsT=ones_b[0:nk, :], rhs=ptb[0:nk, :], start=(kt == 0), stop=(kt == 32))
            T.op("pe", ["PT%d" % (kt % 2), lk, "cst_b"], [bk[4], bk[5], bk[0]], f)

        side = cumsum_gen(b + 1) if b + 1 < NJ else None
        for it in range(33 + 3):
            if it < 32:
                a_load(it)
            if 0 <= it - 1 < 32:
                a_prep(it - 1)
            if 0 <= it - 2 < 33:
                a_qk(it - 2)
            if 0 <= it - 3 < 33:
                a_pv(it - 3)
            if side is not None:
                for _i in range(2):
                    try:
                        next(side)
                    except StopIteration:
                        side = None
                        break
        if side is not None:
            run(side)
        T.op("dve", [bk[0]], ["rdenL"], lambda e: e.reciprocal(out=rdenL[:, :], in_=bank[0][:, :]))
        T.op("dve", [bk[4], "rdenL"], ["qlatT"], lambda e: e.tensor_tensor(out=qlatT[:, 0, :], in0=bank[4][:, :], in1=rdenL[:, :], op=ALU.mult))
        T.op("dve", [bk[5], "rdenL"], ["qlatT"], lambda e: e.tensor_tensor(out=qlatT[:, 1, :], in0=bank[5][:, :], in1=rdenL[:, :], op=ALU.mult))

        def fin(e):
            r = None
            first = True
            for h in range(8):
                pair = h // 2
                for kc in range(2):
                    r = e.matmul(bank[7][:, h * NQ:(h + 1) * NQ], lhsT=w_uv_sb[:, kc, pair * 128:(pair + 1) * 128], rhs=qlatT[:, kc, h * NQ:(h + 1) * NQ],
                                 start=first, stop=(kc == 1), skip_group_check=True)
                    first = False
            return r
        T.op("pe", ["qlatT", "w_uv"], [bk[7]], fin)
        for half in range(2):
            rs = slice(half * 64, half * 64 + 64)
            ov = bank[7][rs, :].rearrange("p (a c q) -> p a c q", a=4, c=2)[:, :, half, :]
            T.op("act", [bk[7]], ["OT"], lambda e, ov=ov, rs=rs: e.copy(out=OT[rs, 4:8, 0:NQ], in_=ov))
        run(outproj(xs[b], NQ, OT[:, :, 0:NQ], x1_d[T_P + b * NQ:T_P + (b + 1) * NQ, :], 0))

    if STAGE < 6:
        T.finish("sp")
        return nc
    T.barrier()
    off[0] = 0
    w_up_sb = carve(8 * 2 * DFF).rearrange("p (k n) -> p k n", k=8)
    assert off[0] <= _o_ktm
    _free = [[off[0], _o_ktm + 6 * 1024], [_o_vf, _o_vf + 5 * VW], [_o_vm, _o_vm + 5 * VW], [_o_vm + 16 * VW, ARENA]]

    def carve(n, dt=BF16):
        nb = 2 * n if dt == F32 else n
        na = (nb + 15) // 16 * 16
        for r_ in _free:
            if r_[1] - r_[0] >= na:
                a = r_[0]; r_[0] += na
                v = arena[:, a:a + nb]
                return v.bitcast(F32) if dt == F32 else v
        raise AssertionError("FFN arena full")
    aTb = [carve(NFC * FG).rearrange("p (k n) -> p k n", k=NFC) for _ in range(2)]
    h2Tb = [carve(8 * (FG + 8)).rearrange("p (k n) -> p k n", k=8) for _ in range(2)]
    T.dma("sp", "c2", [], ["g_big"], g_big[:], g_ffn_d.partition_broadcast(128))
    g_fin = carve(D, F32)
    T.dma("sp", "c6", [], ["g_fin"], g_fin, g_fin_d.partition_broadcast(128))
    cw = carve(3 * 44, F32).rearrange("p (j c) -> p j c", j=3); cb = carve(44, F32)
    cstage = carve(128, F32)
    cprev = carve(NS * 2 * 44, F32).rearrange("p (b j c) -> p b j c", b=NS, j=2)
    cnew = carve(NS * 2 * 44, F32).rearrange("p (b j c) -> p b j c", b=NS, j=2)
    cnewT = carve(128, F32)
    tgs = [carve(FG, F32) for _ in range(2)]; tvs = [carve(FG, F32) for _ in range(2)]
    yt = carve(D, F32)
    xe = carve(D, F32)

    def load_fm(src_rows, nrows, dst, dkey):
        T.dma("sp", "cst", [], ["cstage"], cstage[0:nrows, :], src_rows)
        T.op("pe", ["cstage", "cst_f"], [bk[0]], lambda e: e.transpose(out=bank[0][:, 0:nrows], in_=cstage[0:nrows, :], identity=ident_f[0:nrows, 0:nrows]))
        T.op("act", [bk[0]], [dkey], lambda e: e.copy(out=dst, in_=bank[0][:, 0:nrows]))
    for j in range(3):
        load_fm(cw_d[j].rearrange("(c p) -> c p", p=128), 44, cw[:, j, :], "cw")
    load_fm(cb_d.rearrange("(c p) -> c p", p=128), 44, cb[:, :], "cb")

    def ffn_prep(gi, x1_src, nt, halo, nseg):
        h2T = h2Tb[gi % 2]; hk = "h2T%d" % (gi % 2)
        ntile = (nt + 127) // 128
        L = nt // nseg
        W = nseg * (L + 2)
        h2v = h2T[:, :, 0:W].rearrange("p k (s c) -> p k s c", s=nseg)
        if halo == "prev":
            hp = h2Tb[(gi - 1) % 2]
            T.op("dve", ["h2T%d" % ((gi - 1) % 2)], [hk], lambda e: e.tensor_copy(out=h2T[:, :, 0:2], in_=hp[:, :, FG:FG + 2]))
        else:
            T.op("dve", [], [hk], lambda e: e.memset(h2v[:, :, :, 0:2], 0.0))
        yield
        for j in range(ntile):
            n = min(128, nt - j * 128)
            T.dma("sp", "xt", [], ["xt"], xt[0:n, :], x1_src[j * 128:j * 128 + n, :])
            yield from rmsn_stats(xt[0:n, :], "xt", n, D, 0)
            T.op("dve", ["xt", "ss0", "g_big"], ["hb"],
                 lambda e, n=n: e.scalar_tensor_tensor(out=hb[0:n, :], in0=xt[0:n, :], scalar=ss[0:n, 0:1], in1=g_big[0:n, :], op0=ALU.mult, op1=ALU.mult))
            yield
            if L >= 128:
                sg, c0 = (j * 128) // L, (j * 128) % L
                yield from tr8(hb.rearrange("p (k c) -> p k c", k=8), n, 128, h2v[:, :, sg, 2 + c0:2 + c0 + n], hk, "hb", 1)
            else:
                r = 128 // L
                yield from tr8(hb.rearrange("p (k c) -> p k c", k=8), n, 128, None, hk, "hb", 1,
                               evac=lambda e, ptv, j=j, r=r: e.copy(out=h2v[:, :, j * r:(j + 1) * r, 2:2 + L], in_=ptv[:, :, :].rearrange("p k (r l) -> p k r l", r=r)))

    def ffn_group(gi, x1_src, nt, y_dst_fn, last, ocv_dsts, halo, nseg=1, next_prep=None):
        h2T = h2Tb[gi % 2]; hk = "h2T%d" % (gi % 2)
        aT = aTb[gi % 2]; ak = "aT%d" % (gi % 2)
        ntile = (nt + 127) // 128
        L = nt // nseg
        W = nseg * (L + 2)

        def mm(c):
            i2 = c % 2
            for gv in range(2):
                cc = gv * NFC + c
                bi = 2 + 2 * gv + i2

                def um(e, cc=cc, bi=bi):
                    r = None
                    for k in range(8):
                        r = e.matmul(bank[bi][:, 0:W], lhsT=w_up_sb[:, k, cc * 128:(cc + 1) * 128], rhs=h2T[:, k, 0:W], start=(k == 0), stop=(k == 7))
                    return r
                T.op("pe", [hk, "wu%d" % cc], [bk[bi]], um)

        def ew(c):
            i2 = c % 2
            for gv in range(2):
                cc = gv * NFC + c
                bi = 2 + 2 * gv + i2
                pk = bk[bi]
                psv = bank[bi][:, 0:W].rearrange("p (s c) -> p s c", s=nseg)
                dk_ = ("tg%d" if gv == 0 else "tv%d") % i2
                dv = (tgs if gv == 0 else tvs)[i2][:, 0:nt].rearrange("p (s l) -> p s l", s=nseg)
                if halo == "state":
                    T.op("dve", [pk, "cprev"], [pk], lambda e, psv=psv, cc=cc: e.tensor_copy(out=psv[:, :, 0:2], in_=cprev[:, 0:nseg, :, cc]))
                T.op("act", [pk, "cw", "cb"], [dk_], lambda e, psv=psv, cc=cc, dv=dv: e.activation(out=dv, in_=psv[:, :, 2:2 + L], func=AF.Identity, scale=cw[:, 2, cc:cc + 1], bias=cb[:, cc:cc + 1]))
                T.op("dve", [pk, dk_, "cw"], [dk_], lambda e, psv=psv, cc=cc, dv=dv: e.scalar_tensor_tensor(out=dv, in0=psv[:, :, 1:1 + L], scalar=cw[:, 1, cc:cc + 1], in1=dv, op0=ALU.mult, op1=ALU.add))
                T.op("dve", [pk, dk_, "cw"], [dk_], lambda e, psv=psv, cc=cc, dv=dv: e.scalar_tensor_tensor(out=dv, in0=psv[:, :, 0:L], scalar=cw[:, 0, cc:cc + 1], in1=dv, op0=ALU.mult, op1=ALU.add))
                if last:
                    T.op("dve", [pk], ["cnew"], lambda e, psv=psv, cc=cc: e.tensor_copy(out=cnew[:, 0:nseg, :, cc], in_=psv[:, :, L:L + 2]))
            T.op("act", ["tg%d" % i2], ["tg%d" % i2], lambda e: e.activation(out=tgs[i2][:, 0:nt], in_=tgs[i2][:, 0:nt], func=AF.Silu))
            T.op("pool", ["tg%d" % i2, "tv%d" % i2], [ak], lambda e: e.tensor_tensor(out=aT[:, c, 0:nt], in0=tgs[i2][:, 0:nt], in1=tvs[i2][:, 0:nt], op=ALU.mult))

        def up_loop():
            mm(0)
            for c in range(NFC):
                if c + 1 < NFC:
                    mm(c + 1)
                ew(c)
                yield
        interleave(up_loop(), next_prep if (next_prep is not None) else None, 2)
        def epi():
            for j in range(ntile):
                n = min(128, nt - j * 128)
                T.dma("sp", "xe", [], ["xe"], xe[0:n, :], x1_src[j * 128:j * 128 + n, :])
                yield
                for hf in range(2):
                    bi = 6 + hf

                    def dm(e, bi=bi, hf=hf, j=j, n=n):
                        r = None
                        for c in range(NFC):
                            r = e.matmul(bank[bi][0:n, :], lhsT=aT[:, c, j * 128:j * 128 + n], rhs=w_dn_c[c][:, hf * 512:(hf + 1) * 512], start=(c == 0), stop=(c == NFC - 1))
                        return r
                    T.op("pe", [ak] + ["w_dn%d" % c_ for c_ in range(NFC)], [bk[bi]], dm)
                    yield
                    T.op("dve", [bk[bi], "xe"], ["xe"], lambda e, bi=bi, hf=hf, n=n: e.tensor_tensor(out=xe[0:n, hf * 512:(hf + 1) * 512], in0=bank[bi][0:n, :], in1=xe[0:n, hf * 512:(hf + 1) * 512], op=ALU.add))
                    yield
                T.op("act", ["xe"], ["yt", "ss3"], lambda e, n=n: e.activation(out=yt[0:n, :], in_=xe[0:n, :], func=AF.Square, accum_out=ss[0:n, 3:4]))
                yield
                T.op("dve", ["ss3"], ["ss3"], lambda e, n=n: e.tensor_scalar(out=ss[0:n, 3:4], in0=ss[0:n, 3:4], scalar1=1.0 / D, scalar2=1e-6, op0=ALU.mult, op1=ALU.add))
                yield
                T.op("pool", ["ss3", "mhalf"], ["ss3"], lambda e, n=n: e.tensor_tensor(out=ss[0:n, 3:4], in0=ss[0:n, 3:4], in1=mhalf[0:n, 0:1], op=ALU.pow))
                yield
                T.op("dve", ["xe", "ss3", "g_fin"], ["yt"],
                     lambda e, n=n: e.scalar_tensor_tensor(out=yt[0:n, :], in0=xe[0:n, :], scalar=ss[0:n, 3:4], in1=g_fin[0:n, :], op0=ALU.mult, op1=ALU.mult))
                yield
                T.dma("sp", "yt", ["yt"], [], y_dst_fn(j, n), yt[0:n, :])
                yield
        if last:
            for sg in range(nseg):
                T.op("pe", ["cnew", "cst_f"], [bk[0]], lambda e, sg=sg: e.transpose(out=bank[0][0:88, 0:128], in_=cnew[:, sg].rearrange("p j c -> p (j c)"), identity=ident_f))
                T.op("act", [bk[0]], ["cnewT"], lambda e: e.copy(out=cnewT[0:88, :], in_=bank[0][0:88, 0:128]))
                T.dma("sp", "cno", ["cnewT"], [], ocv_dsts[sg].rearrange("j (c p) -> (j c) p", p=128), cnewT[0:88, :])
        return epi()

    def chain_gens(gens):
        for g_ in gens:
            yield from g_

    for b in range(NS):
        for j in range(2):
            load_fm(scv[b, j].rearrange("(c p) -> c p", p=128), 44, cprev[:, b, j, :], "cprev")
    y_s_flat = y_s.rearrange("b t d -> (b t) d")
    ng = T_P // FG
    specs = []
    for g in range(ng):
        r0 = g * FG
        specs.append(dict(x1=x1_d[r0:r0 + FG, :], nt=FG, y=(lambda j, n, r0=r0: y_p[r0 + j * 128:r0 + j * 128 + n, :]), last=(g == ng - 1),
                          ocv=[ocv_p], halo=("zero" if g == 0 else "prev"), nseg=1))
    specs.append(dict(x1=x1_d[T_P:T_P + NS * T_S, :], nt=NS * T_S, y=(lambda j, n: y_s_flat[j * 128:j * 128 + n, :]), last=True,
                      ocv=[ocv_s[b] for b in range(NS)], halo="state", nseg=NS))
    run(ffn_prep(0, specs[0]["x1"], specs[0]["nt"], specs[0]["halo"], specs[0]["nseg"]))
    for c in range(NFC):
        for gv in range(2):
            cc = gv * NFC + c
            T.dma("pool", "w%d" % (wcnt[0] % 4), [], ["wu%d" % cc], w_up_sb[:, :, cc * 128:(cc + 1) * 128],
                  w_up_d[:, cc * 128:(cc + 1) * 128].rearrange("(k p) n -> p k n", p=128))
            wcnt[0] += 1
    pend = None
    for gi, sp_ in enumerate(specs):
        sides = []
        if gi + 1 < len(specs):
            n_ = specs[gi + 1]
            sides.append(ffn_prep(gi + 1, n_["x1"], n_["nt"], n_["halo"], n_["nseg"]))
        if pend is not None:
            sides.append(pend)
        pend = ffn_group(gi, sp_["x1"], sp_["nt"], sp_["y"], sp_["last"], sp_["ocv"], sp_["halo"], sp_["nseg"],
                         next_prep=chain_gens(sides) if sides else None)
    run(pend)

    T.finish("sp")
    return nc


def _consts():
    ident = np.eye(128, dtype=np.float32)
    s = np.arange(128)
    tri = (s[:, None] <= s[None, :]).astype(np.float32)
    ones = np.ones((128, 128), np.float32)
    mtri = np.where(s[:, None] <= s[None, :], 0.0, NEGM).astype(np.float32)
    mblk = np.where((s[:, None] // 64) <= (s[None, :] // 64), 0.0, NEGM).astype(np.float32)
    return np.concatenate([ident, tri, ones, mtri, mblk], axis=1)


def _rope_tab():
    half = 16
    inv = (np.float32(10000.0) ** (-np.arange(half, dtype=np.float32) / np.float32(half))).astype(np.float32)
    pos = np.concatenate([np.arange(T_P), PAST + np.arange(T_S)]).astype(np.float32)
    ang = (pos[:, None] * inv[None, :]).astype(np.float32)
    c = np.cos(ang).astype(np.float32); s = np.sin(ang).astype(np.float32)
    sc = np.float32(96.0 ** -0.5)
    return np.concatenate([c * sc, s * sc, c, s], axis=1).astype(np.float32)


_NC_CACHE = {}


def kernel(x_prompt, x_sample, cache_fox_k, cache_fox_v, cache_fox_logf, cache_mla_latent,
           cache_mla_krope, state_ffn_conv, attn_norm, w_in, b_forget, q_norm, w_q_up, kv_norm,
           w_uk, w_uv, w_out, ffn_norm, w_up, conv_w, conv_b, w_down, final_norm, _cores=None):
    f = lambda a: np.ascontiguousarray(np.asarray(a, dtype=np.float32))
    if "nc" not in _NC_CACHE:
        _NC_CACHE["nc"] = build_program()
    nc = _NC_CACHE["nc"]
    shared = {
        "attn_norm": f(attn_norm[0]), "w_in": f(w_in[0]), "b_forget": f(b_forget[0]), "q_norm": f(q_norm[0]),
        "w_q_up": f(w_q_up[0]), "kv_norm": f(kv_norm[0]), "w_uk": f(np.asarray(w_uk[0]).reshape(256, 512)),
        "w_uv": f(np.asarray(w_uv[0]).reshape(256, 512)), "w_out": f(w_out[0]), "ffn_norm": f(ffn_norm[0]),
        "w_up": f(w_up[0]), "conv_w": f(conv_w[0]), "conv_b": f(conv_b[0]), "w_down": f(w_down[0]),
        "final_norm": f(final_norm), "rope_tab": _rope_tab(), "consts": _consts(),
    }
    cores = list(range(NCORES)) if _cores is None else _cores
    in_maps = []
    for c in cores:
        sl = slice(NS * c, NS * (c + 1))
        m = dict(shared)
        m["xp"] = f(x_prompt[c]); m["xs"] = f(x_sample[sl])
        m["cfk"] = f(np.asarray(cache_fox_k[0, sl]).reshape(NS, PAST, 512))
        m["cfv"] = f(np.asarray(cache_fox_v[0, sl]).reshape(NS, PAST, 512))
        m["clf"] = f(cache_fox_logf[0, sl]); m["cml"] = f(cache_mla_latent[0, sl]); m["ckr"] = f(cache_mla_krope[0, sl])
        m["scv"] = f(state_ffn_conv[0, sl])
        in_maps.append(m)
    res = run_bass_kernel_spmd(nc, in_maps, core_ids=cores)
    R = res.results
    cat = lambda k: np.concatenate([np.asarray(r[k]) for r in R], axis=0)
    stk = lambda k: np.stack([np.asarray(r[k]) for r in R], axis=0)
    nb = len(cores)
    outs = (
        stk("y_p"), cat("y_s"),
        stk("ok_p").reshape(1, nb, T_P, 8, 64), stk("ov_p").reshape(1, nb, T_P, 8, 64), stk("olf_p").reshape(1, nb, T_P, 8),
        stk("oml_p").reshape(1, nb, T_P, 256), stk("okr_p").reshape(1, nb, T_P, 32), stk("ocv_p").reshape(1, nb, 2, 2 * DFF),
        cat("ok_s").reshape(1, nb * NS, T_S, 8, 64), cat("ov_s").reshape(1, nb * NS, T_S, 8, 64), cat("olf_s").reshape(1, nb * NS, T_S, 8),
        cat("oml_s").reshape(1, nb * NS, T_S, 256), cat("okr_s").reshape(1, nb * NS, T_S, 32), cat("ocv_s").reshape(1, nb * NS, 2, 2 * DFF),
    )
    return tuple(np.ascontiguousarray(o.astype(np.float32)) for o in outs)
```

```python
import os
import numpy as np
import concourse.bass as bass
import concourse.mybir as mybir
from concourse.bass_utils import run_bass_kernel_spmd

F32 = mybir.dt.float32
BF16 = mybir.dt.bfloat16
AF = mybir.ActivationFunctionType
ALU = mybir.AluOpType

NCORES = 8
D = 1024
T_P = 2048
NS = 4
T_S = 64
PAST = 4096
DIN = 2216
DFF = 2816
NFC = 22
QG = 256
FG = 256
NEGM = -30000.0


class Trk:
    def __init__(self, nc):
        self.nc = nc
        self.eng = {"pe": nc.tensor, "act": nc.scalar, "dve": nc.vector, "pool": nc.gpsimd, "sp": nc.sync}
        self.sem = {}
        self.cnt = {}
        for e in ("pe", "act", "dve", "pool"):
            self.sem[e] = nc.alloc_semaphore("sem_" + e)
            self.cnt[e] = 0
        self.dsem = {}
        self.seen = {e: {} for e in self.eng}
        self.lw = {}
        self.rd = {}

    def _wait(self, e, tok):
        name, sem, val = tok
        if name == "pe" and e == "pe":
            return
        s = self.seen[e]
        if s.get(name, 0) >= val:
            return
        s[name] = val
        self.eng[e].wait_ge(sem, val)

    def _deps(self, e, reads, writes):
        for k in list(reads) + list(writes):
            t = self.lw.get(k)
            if t is not None:
                self._wait(e, t)
        for k in writes:
            for t in self.rd.get(k, ()):
                self._wait(e, t)

    def _commit(self, tok, reads, writes):
        for k in reads:
            self.rd.setdefault(k, []).append(tok)
        for k in writes:
            self.lw[k] = tok
            self.rd[k] = []

    def op(self, e, reads, writes, fn):
        writes = list(writes) + [k for k in reads if k.startswith("ps") and k not in writes]
        reads = [k for k in reads if not k.startswith("ps")]
        self._deps(e, reads, writes)
        ins = fn(self.eng[e])
        self.cnt[e] += 1
        ins.then_inc(self.sem[e], 1)
        tok = (e, self.sem[e], self.cnt[e])
        self._commit(tok, reads, writes)
        return tok

    def dma(self, q, slot, reads, writes, out, in_, **kw):
        if slot not in self.dsem:
            self.dsem[slot] = [self.nc.alloc_semaphore("d_%d" % len(self.dsem)), 0]
        ds = self.dsem[slot]
        name = "d_" + str(slot)
        if ds[1] > 0:
            self._wait(q, (name, ds[0], ds[1]))
        self._deps(q, reads, writes)
        if q == "pool":
            kw.setdefault("max_dma_last_dim", 4096)
        ins = self.eng[q].dma_start(out=out, in_=in_, **kw)
        ds[1] += 16
        ins.then_inc(ds[0], 16)
        tok = (name, ds[0], ds[1])
        self._commit(tok, reads, writes)
        return tok

    def barrier(self):
        toks = []
        for slot, (sem, c) in self.dsem.items():
            if c > 0:
                toks.append(("d_" + str(slot), sem, c))
        for x in ("pe", "act", "dve", "pool"):
            if self.cnt[x] > 0:
                toks.append((x, self.sem[x], self.cnt[x]))
        for e in self.eng:
            for t in toks:
                if not (t[0] == e):
                    self._wait(e, t)

    def finish(self, e="sp"):
        for slot, (sem, c) in self.dsem.items():
            if c > 0:
                self._wait(e, ("d_" + str(slot), sem, c))
        for x in ("pe", "act", "dve", "pool"):
            if self.cnt[x] > 0:
                self._wait(e, (x, self.sem[x], self.cnt[x]))


def run(gen):
    for _ in gen:
        pass


def interleave(main, side, ratio):
    side_alive = side is not None
    for _ in main:
        if side_alive:
            for _i in range(ratio):
                try:
                    next(side)
                except StopIteration:
                    side_alive = False
                    break
    if side_alive:
        run(side)


def build_program():
    nc = bass.Bass("TRN2", target_bir_lowering=False, dynamic_dma_scratch_size=4096)
    T = Trk(nc)
    STAGE = int(os.environ.get("KSTAGE", "9"))

    def din(name, shape):
        return nc.dram_tensor(name, list(shape), F32, kind="ExternalInput").ap()

    def dout(name, shape):
        return nc.dram_tensor(name, list(shape), F32, kind="ExternalOutput").ap()

    xp = din("xp", [T_P, D]); xs = din("xs", [NS, T_S, D])
    cfk = din("cfk", [NS, PAST, 512]); cfv = din("cfv", [NS, PAST, 512]); clf = din("clf", [NS, PAST, 8])
    cml = din("cml", [NS, PAST, 256]); ckr = din("ckr", [NS, PAST, 32]); scv = din("scv", [NS, 2, 2 * DFF])
    g_attn_d = din("attn_norm", [D]); w_in_d = din("w_in", [D, DIN]); b_f_d = din("b_forget", [8])
    g_q_d = din("q_norm", [384]); w_qup_d = din("w_q_up", [384, 768]); g_kv_d = din("kv_norm", [256])
    w_uk_d = din("w_uk", [256, 512]); w_uv_d = din("w_uv", [256, 512]); w_out_d = din("w_out", [D, D])
    g_ffn_d = din("ffn_norm", [D]); w_up_d = din("w_up", [D, 2 * DFF]); cw_d = din("conv_w", [3, 2 * DFF])
    cb_d = din("conv_b", [2 * DFF]); w_dn_d = din("w_down", [DFF, D]); g_fin_d = din("final_norm", [D])
    rope_d = din("rope_tab", [T_P + T_S, 64])
    cst_d = din("consts", [128, 5 * 128])

    y_p = dout("y_p", [T_P, D]); y_s = dout("y_s", [NS, T_S, D])
    ok_p = dout("ok_p", [T_P, 512]); ov_p = dout("ov_p", [T_P, 512]); olf_p = dout("olf_p", [T_P, 8])
    oml_p = dout("oml_p", [T_P, 256]); okr_p = dout("okr_p", [T_P, 32]); ocv_p = dout("ocv_p", [2, 2 * DFF])
    ok_s = dout("ok_s", [NS, T_S, 512]); ov_s = dout("ov_s", [NS, T_S, 512]); olf_s = dout("olf_s", [NS, T_S, 8])
    oml_s = dout("oml_s", [NS, T_S, 256]); okr_s = dout("okr_s", [NS, T_S, 32]); ocv_s = dout("ocv_s", [NS, 2, 2 * DFF])
    x1_d = nc.dram_tensor("x1_scratch", [T_P + NS * T_S, D], F32).ap()

    sb = nc.alloc_sbuf_tensor
    cst_f = sb("cst_f", [128, 3 * 128], F32)
    cst_b = sb("cst_b", [128, 5 * 128], BF16)
    ident = cst_b[:, 0:128]; ones_b = cst_b[:, 256:384]; mtri = cst_b[:, 384:512]; mblk = cst_b[:, 512:640]
    ident_f = cst_f[:, 0:128]; tri_f = cst_f[:, 128:256]; ones_f = cst_f[:, 256:384]
    T.dma("sp", "c0", [], ["cst_f"], cst_f[:], cst_d[:, 0:384])
    T.dma("pool", "c1", [], ["cst_b"], cst_b[:], cst_d)
    g_big = sb("g_big", [128, D], F32)
    g_q = sb("g_q", [128, 384], F32); g_kv = sb("g_kv", [128, 256], F32)
    b_f = sb("b_f", [128, 8], F32)
    T.dma("sp", "c2", [], ["g_big"], g_big[:], g_attn_d.partition_broadcast(128))
    T.dma("sp", "c3", [], ["g_q"], g_q[:], g_q_d.partition_broadcast(128))
    T.dma("sp", "c4", [], ["g_kv"], g_kv[:], g_kv_d.partition_broadcast(128))
    T.dma("sp", "c5", [], ["b_f"], b_f[:], b_f_d.partition_broadcast(128))
    xt = sb("xt", [128, D], F32)
    hb = sb("hb", [128, D], BF16)
    ss = sb("ss", [128, 8], F32)
    tot = sb("tot", [128, 8], F32)
    mhalf = sb("mhalf", [128, 8], F32)
    T.op("dve", [], ["mhalf"], lambda e: e.memset(mhalf[:], -0.5))

    ARENA = (nc.sbuf_bytes_remaining - 256) // 2
    arena = sb("arena", [128, ARENA], BF16)
    off = [0]

    def carve(n, dt=BF16):
        nb = 2 * n if dt == F32 else n
        a = off[0]; off[0] += (nb + 15) // 16 * 16
        assert off[0] <= ARENA, (off[0], ARENA)
        v = arena[:, a:a + nb]
        return v.bitcast(F32) if dt == F32 else v

    NQT = QG // 128
    w_in_sb = carve(8 * DIN).rearrange("p (k n) -> p k n", k=8)
    w_qup_sb = carve(3 * 768).rearrange("p (k n) -> p k n", k=3)
    w_uk_sb = carve(2 * 512).rearrange("p (k n) -> p k n", k=2)
    w_uv_sb = carve(2 * 512).rearrange("p (k n) -> p k n", k=2)
    w_out_sb = carve(8 * D).rearrange("p (k n) -> p k n", k=8)
    KTf = carve(16 * 8 * 128).rearrange("p (t h s) -> p t h s", t=16, h=8)
    _o_ktm = off[0]
    KTm = carve(16 * 8 * 128).rearrange("p (t h s) -> p t h s", t=16, h=8)
    VW = 768
    _o_vf = off[0]
    Vf = carve(16 * VW).rearrange("p (t n) -> p t n", t=16)
    _o_vm = off[0]
    Vm = carve(16 * VW).rearrange("p (t n) -> p t n", t=16)
    _wd_slots = [_o_vf + 5 * VW + i * 1024 for i in range(8)] + [_o_vm + 5 * VW + i * 1024 for i in range(8)] + [_o_ktm + 6 * 1024 + i * 1024 for i in range(6)]
    w_dn_c = [arena[:, a:a + 1024] for a in _wd_slots]
    APcls = type(arena[:, 0:1])

    def vaug(vtile, h, nk=128):
        a = (h // 2) * 192 + (h % 2) * 64
        return vtile[0:nk, a:a + 128]

    def vdst(vtile, nt):
        a = vtile[0:nt, 0:64]
        return APcls(a.tensor, a.offset, [list(a.ap[0]), [192, 4], [128, 2], [1, 64]])
    hT = carve(8 * 128).rearrange("p (k s) -> p k s", k=8)
    kout = carve(512, F32)
    cn = carve(256, F32); cnb = carve(256); cT = carve(256).rearrange("p (k s) -> p k s", k=2)
    qn = carve(384); qnT = carve(384).rearrange("p (k s) -> p k s", k=3)
    Kaug = carve(8 * 70).rearrange("p (h d) -> p h d", h=8)
    Kmla = carve(8 * 96).rearrange("p (h d) -> p h d", h=8)
    QaugS = carve(NQT * 8 * 70).rearrange("p (t h d) -> p t h d", t=NQT, h=8)
    QmlaS = carve(NQT * 8 * 96).rearrange("p (t h d) -> p t h d", t=NQT, h=8)
    rp = carve(64, F32); kro = carve(32, F32)
    tmp16 = carve(8 * 16, F32).rearrange("p (h d) -> p h d", h=8)
    tmp16b = carve(8 * 16, F32).rearrange("p (h d) -> p h d", h=8)
    tmp16c = carve(16, F32).rearrange("p (h d) -> p h d", h=1)
    tmp16d = carve(16, F32).rearrange("p (h d) -> p h d", h=1)
    lf = carve(8, F32); Ft = carve(8, F32); r1 = carve(8, F32)
    QTf = carve(8 * QG).rearrange("p (h s) -> p h s", h=8)
    QTm = carve(8 * QG).rearrange("p (h s) -> p h s", h=8)
    PT = [carve(512) for _ in range(2)]
    OT = carve(8 * QG).rearrange("p (k s) -> p k s", k=8)
    rden = carve(512, F32)
    _kf = KTf[:, 3:16, :, :].rearrange("p t h s -> p (t h s)")
    _ko = [0]

    def kcarve(n, dt=BF16):
        nb = 2 * n if dt == F32 else n
        a = _ko[0]; _ko[0] += (nb + 15) // 16 * 16
        assert _ko[0] <= 13 * 1024
        v = _kf[:, a:a + nb]
        return v.bitcast(F32) if dt == F32 else v
    KaugC = [kcarve(8 * 70).rearrange("p (h d) -> p h d", h=8) for _ in range(2)]
    w_ukT = kcarve(8 * 256).rearrange("p (h c) -> p h c", h=8)
    qlatT = kcarve(2 * 512).rearrange("p (k n) -> p k n", k=2)
    qrT = kcarve(512).rearrange("p (h s) -> p h s", h=8)
    LT = [kcarve(3 * 128).rearrange("p (k s) -> p k s", k=3) for _ in range(2)]
    LTn = kcarve(3 * 64).rearrange("p (k s) -> p k s", k=3)
    rdenL = kcarve(512, F32)
    KstF = [kcarve(512, F32) for _ in range(2)]
    VstF = [kcarve(512, F32) for _ in range(2)]
    LstF = [kcarve(288, F32) for _ in range(2)]
    KTc = [KTf[:, 1, :, :], KTf[:, 2, :, :]]
    Vfc = [Vf[:, 1 + i, 0:512] for i in range(4)]
    Lb = [Vm[:, 1 + i, 0:288] for i in range(4)]
    _km = KTm[:, 2:16, :, :].rearrange("p t h s -> p (t h s)")
    lfc = _km[:, 0:512].bitcast(F32).rearrange("p (t h) -> p t h", t=32)
    Fc = _km[:, 512:1024].bitcast(F32).rearrange("p (t h) -> p t h", t=32)
    Ec = _km[:, 1024:1024 + 528].bitcast(F32).rearrange("p (t h) -> p t h", t=33)
    rc = _km[:, 2048:2560].bitcast(F32).rearrange("p (t h) -> p t h", t=32)
    pcs = _km[:, 3072:3072 + 768].rearrange("p (t h c) -> p t h c", t=32, h=8)

    wcnt = [0]

    def wload(dst3, src, K, key):
        for k in range(K):
            T.dma("pool", "w%d" % (wcnt[0] % 4), [], [key], dst3[:, k, :], src[k * 128:(k + 1) * 128, :])
            wcnt[0] += 1
    for bi_, (c0_, n_) in enumerate([(0, 512), (512, 512), (1024, 512), (1536, 392), (1928, 288)]):
        T.dma("pool", "w%d" % (wcnt[0] % 4), [], ["w_in%d" % bi_], w_in_sb[:, :, c0_:c0_ + n_],
              w_in_d[:, c0_:c0_ + n_].rearrange("(k p) n -> p k n", p=128))
        wcnt[0] += 1
    wload(w_qup_sb, w_qup_d, 3, "w_qup")
    wload(w_uk_sb, w_uk_d, 2, "w_uk")
    wload(w_uv_sb, w_uv_d, 2, "w_uv")
    wload(w_out_sb, w_out_d, 8, "w_out")

    bank = [nc.alloc_psum_tensor("bank%d" % i, [128, 512], F32) for i in range(8)]
    bk = ["ps%d" % i for i in range(8)]

    def bfv(i):
        return bank[i][:, :].bitcast(BF16).rearrange("p (h s) -> p h s", h=8)

    def preset(e):
        e.memset(Kaug[:], 1.0)
        e.memset(Vf[:], 1.0)
        e.memset(Vm[:], 1.0)
        return e.memset(QaugS[:], 1.0)
    T.op("dve", [], ["Kaug", "Qaug"] + ["V%d" % t for t in range(16)], preset)
    T.op("dve", [], ["tot"], lambda e: e.memset(tot[:], 0.0))

    def tr8(src, nt, w, dst, dkey, skey, tb, nh=8, evac=None):
        ptv = bfv(tb)

        def f(e):
            r = None
            for h in range(nh):
                r = e.transpose(out=ptv[0:w, h, 0:nt], in_=src[0:nt, h, 0:w], identity=ident[0:nt, 0:nt])
            return r
        T.op("pe", [skey, "cst_b"], [bk[tb]], f)
        yield
        if evac is None:
            T.op("act", [bk[tb]], [dkey], lambda e: e.copy(out=dst, in_=ptv[0:w, 0:nh, 0:nt]))
        else:
            T.op("act", [bk[tb]], [dkey], lambda e: evac(e, ptv))
        yield

    def rmsn_stats(src, skey, nt, width, col):
        c = "ss%d" % col
        T.op("act", [skey], ["hb", c],
             lambda e: e.activation(out=hb[0:nt, 0:width], in_=src, func=AF.Square, accum_out=ss[0:nt, col:col + 1]))
        yield
        T.op("dve", [c], [c],
             lambda e: e.tensor_scalar(out=ss[0:nt, col:col + 1], in0=ss[0:nt, col:col + 1], scalar1=1.0 / width, scalar2=1e-6, op0=ALU.mult, op1=ALU.add))
        yield
        T.op("pool", [c, "mhalf"], [c], lambda e: e.tensor_tensor(out=ss[0:nt, col:col + 1], in0=ss[0:nt, col:col + 1], in1=mhalf[0:nt, 0:1], op=ALU.pow))
        yield

    def rope(src, dst, nt, nh, c0, dcol, rk, wk):
        cs = rp[0:nt, c0:c0 + 16].unsqueeze(1).to_broadcast([nt, nh, 16])
        sn = rp[0:nt, c0 + 16:c0 + 32].unsqueeze(1).to_broadcast([nt, nh, 16])
        x1 = src[:, :, 0:16]; x2 = src[:, :, 16:32]
        a = tmp16[0:nt, 0:nh, :]; b = tmp16b[0:nt, 0:nh, :]
        R = list(rk) + ["rp"]
        T.op("dve", R, ["tmpa"], lambda e: e.tensor_tensor(out=a, in0=x1, in1=cs, op=ALU.mult)); yield
        T.op("dve", R, ["tmpb"], lambda e: e.tensor_tensor(out=b, in0=x2, in1=sn, op=ALU.mult)); yield
        T.op("dve", ["tmpa", "tmpb"], wk, lambda e: e.tensor_tensor(out=dst[:, :, dcol:dcol + 16], in0=a, in1=b, op=ALU.subtract)); yield
        T.op("dve", R, ["tmpa"], lambda e: e.tensor_tensor(out=a, in0=x1, in1=sn, op=ALU.mult)); yield
        T.op("dve", R, ["tmpb"], lambda e: e.tensor_tensor(out=b, in0=x2, in1=cs, op=ALU.mult)); yield
        T.op("dve", ["tmpa", "tmpb"], wk, lambda e: e.tensor_tensor(out=dst[:, :, dcol + 16:dcol + 32], in0=a, in1=b, op=ALU.add)); yield

    def upproj(nt, kmla_dst, kmkey, vm_dst, vkey, pa):
        def um(w):
            def f(e):
                r = None
                for k in range(2):
                    r = e.matmul(bank[pa][0:nt, :], lhsT=cT[:, k, 0:nt], rhs=w[:, k, :], start=(k == 0), stop=(k == 1))
                return r
            return f
        T.op("pe", ["cT", "w_uk"], [bk[pa]], um(w_uk_sb)); yield
        T.op("act", [bk[pa]], [kmkey], lambda e: e.copy(out=kmla_dst[0:nt, :, 0:64], in_=bank[pa][0:nt, :].rearrange("p (h d) -> p h d", h=8))); yield
        T.op("pe", ["cT", "w_uv"], [bk[pa]], um(w_uv_sb)); yield
        T.op("dve", [bk[pa]], [vkey], lambda e: e.tensor_copy(out=vm_dst, in_=bank[pa][0:nt, :].rearrange("p (a c d) -> p a c d", a=4, c=2))); yield

    def roundrobin(gens):
        gens = list(gens)
        while gens:
            alive = []
            for g_ in gens:
                try:
                    next(g_)
                    alive.append(g_)
                except StopIteration:
                    pass
            gens = alive
            yield

    def rope2(src, dst, nt, nh, c0, dcol, rk, wk, ta, tbb, ka, kb):
        cs = rp[0:nt, c0:c0 + 16].unsqueeze(1).to_broadcast([nt, nh, 16])
        sn = rp[0:nt, c0 + 16:c0 + 32].unsqueeze(1).to_broadcast([nt, nh, 16])
        x1 = src[:, :, 0:16]; x2 = src[:, :, 16:32]
        a = ta[0:nt, 0:nh, :]; b = tbb[0:nt, 0:nh, :]
        R = list(rk) + ["rp"]
        T.op("dve", R, [ka], lambda e: e.tensor_tensor(out=a, in0=x1, in1=cs, op=ALU.mult)); yield
        T.op("dve", R, [kb], lambda e: e.tensor_tensor(out=b, in0=x2, in1=sn, op=ALU.mult)); yield
        T.op("dve", [ka, kb], wk, lambda e: e.tensor_tensor(out=dst[:, :, dcol:dcol + 16], in0=a, in1=b, op=ALU.subtract)); yield
        T.op("dve", R, [ka], lambda e: e.tensor_tensor(out=a, in0=x1, in1=sn, op=ALU.mult)); yield
        T.op("dve", R, [kb], lambda e: e.tensor_tensor(out=b, in0=x2, in1=cs, op=ALU.mult)); yield
        T.op("dve", [ka, kb], wk, lambda e: e.tensor_tensor(out=dst[:, :, dcol + 16:dcol + 32], in0=a, in1=b, op=ALU.add)); yield

    def front_tile(x_src, nt, rope_row, o_k, o_v, o_lf, o_ml, o_kr, vf_dst, vm_dst, vkey, Qaug, Qmla, pa, tb):
        pal = list(pa) if isinstance(pa, (list, tuple)) else [pa]
        nb_ = len(pal)
        T.dma("sp", "xt", [], ["xt"], xt[0:nt, :], x_src)
        T.dma("sp", "rp", [], ["rp"], rp[0:nt, :], rope_d[rope_row:rope_row + nt, :])
        yield
        yield from rmsn_stats(xt[0:nt, :], "xt", nt, D, 0)
        T.op("dve", ["xt", "ss0", "g_big"], ["hb"],
             lambda e: e.scalar_tensor_tensor(out=hb[0:nt, :], in0=xt[0:nt, :], scalar=ss[0:nt, 0:1], in1=g_big[0:nt, :], op0=ALU.mult, op1=ALU.mult))
        yield
        yield from tr8(hb.rearrange("p (k c) -> p k c", k=8), nt, 128, hT[:, :, 0:nt], "hT", "hb", tb)

        def zmm(pb, c0, n):
            def f(e):
                r = None
                for k in range(8):
                    r = e.matmul(bank[pb][0:nt, 0:n], lhsT=hT[:, k, 0:nt], rhs=w_in_sb[:, k, c0:c0 + n], start=(k == 0), stop=(k == 7))
                return r
            return f

        def br_q():
            pb = pal[0 % nb_]; pA = bank[pb]; kA = bk[pb]
            T.op("pe", ["hT", "w_in0"], [kA], zmm(pb, 0, 512)); yield
            T.op("act", [kA], ["Qaug"], lambda e: e.activation(out=Qaug[0:nt, :, 0:64], in_=pA[0:nt, :].rearrange("p (h d) -> p h d", h=8), func=AF.Copy, scale=0.125)); yield

        def br_k():
            pb = pal[1 % nb_]; pA = bank[pb]; kA = bk[pb]
            T.op("pe", ["hT", "w_in1"], [kA], zmm(pb, 512, 512)); yield
            T.op("act", [kA], ["kout"], lambda e: e.copy(out=kout[0:nt, :], in_=pA[0:nt, :])); yield
            T.op("dve", [kA], ["Kaug"], lambda e: e.tensor_copy(out=Kaug[0:nt, :, 0:64], in_=pA[0:nt, :].rearrange("p (h d) -> p h d", h=8))); yield
            T.dma("sp", "kout", ["kout"], [], o_k, kout[0:nt, :])
            yield

        def br_v():
            pb = pal[2 % nb_]; pA = bank[pb]; kA = bk[pb]
            T.op("pe", ["hT", "w_in2"], [kA], zmm(pb, 1024, 512)); yield
            T.op("act", [kA], ["xt"], lambda e: e.copy(out=xt[0:nt, 0:512], in_=pA[0:nt, :])); yield
            T.op("dve", [kA], [vkey], lambda e: e.tensor_copy(out=vf_dst, in_=pA[0:nt, :].rearrange("p (a c d) -> p a c d", a=4, c=2))); yield
            T.dma("sp", "vout", ["xt"], [], o_v, xt[0:nt, 0:512])
            yield

        def br_f():
            pb = pal[0 % nb_]; pA = bank[pb]; kA = bk[pb]
            T.op("pe", ["hT", "w_in3"], [kA], zmm(pb, 1536, 392)); yield
            T.op("dve", [kA, "b_f"], ["r1"], lambda e: e.tensor_tensor(out=r1[0:nt, :], in0=pA[0:nt, 0:8], in1=b_f[0:nt, :], op=ALU.add)); yield
            yield from rmsn_stats(pA[0:nt, 8:392], kA, nt, 384, 1)
            T.op("dve", [kA, "ss1", "g_q"], ["qn"],
                 lambda e: e.scalar_tensor_tensor(out=qn[0:nt, :], in0=pA[0:nt, 8:392], scalar=ss[0:nt, 1:2], in1=g_q[0:nt, :], op0=ALU.mult, op1=ALU.mult)); yield
            yield from roundrobin([br_f1(), br_f2()])

        def br_f1():
            pb = pal[2 % nb_]; pA = bank[pb]; kA = bk[pb]
            T.op("act", ["r1"], ["Ft"], lambda e: e.activation(out=Ft[0:nt, :], in_=r1[0:nt, :], func=AF.Exp, scale=-1.0)); yield
            T.op("dve", ["Ft"], ["r1"], lambda e: e.tensor_scalar(out=r1[0:nt, :], in0=Ft[0:nt, :], scalar1=1.0, scalar2=None, op0=ALU.add)); yield
            T.op("act", ["r1"], ["Ft"], lambda e: e.activation(out=Ft[0:nt, :], in_=r1[0:nt, :], func=AF.Ln)); yield
            T.op("dve", ["Ft"], ["lf"], lambda e: e.tensor_scalar(out=lf[0:nt, :], in0=Ft[0:nt, :], scalar1=-1.0, scalar2=None, op0=ALU.mult)); yield
            T.dma("sp", "lf", ["lf"], [], o_lf, lf[0:nt, :])

            def fmm(e):
                e.matmul(pA[0:nt, 0:8], lhsT=tri_f[0:nt, 0:nt], rhs=lf[0:nt, :], start=True, stop=True)
                return e.matmul(pA[0:nt, 8:16], lhsT=ones_f[0:nt, 0:nt], rhs=lf[0:nt, :], start=True, stop=True)
            T.op("pe", ["lf", "cst_f"], [kA], fmm); yield
            T.op("dve", [kA, "tot"], ["Ft"], lambda e: e.tensor_tensor(out=Ft[0:nt, :], in0=pA[0:nt, 0:8], in1=tot[0:nt, :], op=ALU.add)); yield
            T.op("dve", [kA, "tot"], ["tot"], lambda e: e.tensor_tensor(out=tot[0:nt, :], in0=pA[0:nt, 8:16], in1=tot[0:nt, :], op=ALU.add)); yield
            T.op("dve", ["Ft"], ["Qaug"], lambda e: e.tensor_copy(out=Qaug[0:nt, :, 67], in_=Ft[0:nt, :])); yield
            T.op("dve", ["Ft", "Qaug"], ["r1"], lambda e: e.tensor_tensor(out=r1[0:nt, :], in0=Ft[0:nt, :], in1=Qaug[0:nt, :, 67], op=ALU.subtract)); yield
            T.op("dve", ["r1"], ["Qaug"], lambda e: e.tensor_copy(out=Qaug[0:nt, :, 68], in_=r1[0:nt, :])); yield
            T.op("dve", ["r1", "Qaug"], ["Ft"], lambda e: e.tensor_tensor(out=Ft[0:nt, :], in0=r1[0:nt, :], in1=Qaug[0:nt, :, 68], op=ALU.subtract)); yield
            T.op("dve", ["Ft"], ["Qaug"], lambda e: e.tensor_copy(out=Qaug[0:nt, :, 69], in_=Ft[0:nt, :])); yield
            T.op("dve", ["Qaug"], ["Kaug"], lambda e: e.tensor_scalar(out=Kaug[0:nt, :, 64:67], in0=Qaug[0:nt, :, 67:70], scalar1=-1.0, scalar2=None, op0=ALU.mult)); yield

        def br_f2():
            yield from tr8(qn.rearrange("p (k c) -> p k c", k=3), nt, 128, qnT[:, :, 0:nt], "qnT", "qn", tb, nh=3)
            sc = 96.0 ** -0.5
            for half in range(2):
                pb = pal[0]; pA = bank[pb]; kA = bk[pb]

                def qmm(e, half=half, pA=pA):
                    r = None
                    for k in range(3):
                        r = e.matmul(pA[0:nt, 0:384], lhsT=qnT[:, k, 0:nt], rhs=w_qup_sb[:, k, half * 384:(half + 1) * 384], start=(k == 0), stop=(k == 2))
                    return r
                T.op("pe", ["qnT", "w_qup"], [kA], qmm); yield
                pv = pA[0:nt, 0:384].rearrange("p (h d) -> p h d", h=4)
                hs = slice(half * 4, half * 4 + 4)
                T.op("act", [kA], ["Qmla"], lambda e, pv=pv, hs=hs: e.activation(out=Qmla[0:nt, hs, 0:64], in_=pv[:, :, 0:64], func=AF.Copy, scale=sc)); yield
                yield from rope2(pv[:, :, 64:96], Qmla[0:nt, hs, :], nt, 4, 0, 64, [kA], ["Qmla"], tmp16, tmp16b, "tmpa", "tmpb")

        def br_c():
            pb = pal[1 % nb_]; pA = bank[pb]; kA = bk[pb]
            T.op("pe", ["hT", "w_in4"], [kA], zmm(pb, 1928, 288)); yield
            yield from rmsn_stats(pA[0:nt, 0:256], kA, nt, 256, 2)
            T.op("dve", [kA, "ss2", "g_kv"], ["cn"],
                 lambda e: e.scalar_tensor_tensor(out=cn[0:nt, :], in0=pA[0:nt, 0:256], scalar=ss[0:nt, 2:3], in1=g_kv[0:nt, :], op0=ALU.mult, op1=ALU.mult)); yield
            T.dma("sp", "cn", ["cn"], [], o_ml, cn[0:nt, :])
            T.op("act", ["cn"], ["cnb"], lambda e: e.copy(out=cnb[0:nt, :], in_=cn[0:nt, :])); yield
            yield from rope2(pA[0:nt, 256:288].unsqueeze(1), kro[0:nt, :].unsqueeze(1), nt, 1, 32, 0, [kA], ["kro"], tmp16c, tmp16d, "tmpc", "tmpd")
            T.dma("sp", "kro", ["kro"], [], o_kr, kro[0:nt, :])
            T.op("dve", ["kro"], ["Kmla"], lambda e: e.tensor_copy(out=Kmla[0:nt, :, 64:96], in_=kro[0:nt, :].unsqueeze(1).to_broadcast([nt, 8, 32]))); yield
            yield from tr8(cnb.rearrange("p (k c) -> p k c", k=2), nt, 128, cT[:, :, 0:nt], "cT", "cnb", tb, nh=2)
            yield from upproj(nt, Kmla, "Kmla", vm_dst, vkey, pal[1 % nb_])

        yield from roundrobin([br_q(), br_k(), br_v()])
        yield from roundrobin([br_f(), br_c()])

    def outproj(x_src, nt, ot_view, x1_dst, pa):
        T.dma("sp", "xt", [], ["xt"], xt[0:nt, :], x_src)
        yield
        for hf in range(2):
            def f(e, hf=hf):
                r = None
                for k in range(8):
                    r = e.matmul(bank[pa][0:nt, :], lhsT=ot_view[:, k, :], rhs=w_out_sb[:, k, hf * 512:(hf + 1) * 512], start=(k == 0), stop=(k == 7))
                return r
            T.op("pe", ["OT", "w_out"], [bk[pa]], f)
            yield
            T.op("dve", [bk[pa], "xt"], ["xt"], lambda e, hf=hf: e.tensor_tensor(out=xt[0:nt, hf * 512:(hf + 1) * 512], in0=bank[pa][0:nt, :], in1=xt[0:nt, hf * 512:(hf + 1) * 512], op=ALU.add))
            yield
        T.dma("sp", "x1o", ["xt"], ["x1d"], x1_dst, xt[0:nt, :])
        yield

    NG = T_P // QG

    def fe_group(g):
        for j in range(NQT):
            t = g * NQT + j
            r0 = t * 128
            yield from front_tile(xp[r0:r0 + 128, :], 128, r0, ok_p[r0:r0 + 128, :], ov_p[r0:r0 + 128, :], olf_p[r0:r0 + 128, :],
                                  oml_p[r0:r0 + 128, :], okr_p[r0:r0 + 128, :], vdst(Vf[:, t, :], 128), vdst(Vm[:, t, :], 128), "V%d" % t,
                                  QaugS[:, j], QmlaS[:, j], [0, 6, 7], 1)
            yield from tr8(Kaug, 128, 70, KTf[0:70, t, :, :], "KTf%d" % t, "Kaug", 1)
            yield from tr8(Kmla, 128, 96, KTm[0:96, t, :, :], "KTm%d" % t, "Kmla", 1)

    def att_group(g):
        q0 = g * QG
        nkt = (q0 + QG) // 128
        blocks = []
        for typ in range(2):
            for h in range(8):
                for kt in range(nkt):
                    blocks.append((typ, h, kt))
        nb = len(blocks)

        def qk_ins(e, i):
            typ, h, kt = blocks[i]
            KT, QT, dk, msk = (KTf, QTf, 70, mtri) if typ == 0 else (KTm, QTm, 96, mblk)
            c0 = max(0, kt * 128 - q0)
            n = QG - c0
            diag = kt * 128 >= q0
            sbk = bank[2 + (i % 2)]
            r = e.matmul(sbk[:, 0:n], lhsT=KT[0:dk, kt, h, :], rhs=QT[0:dk, h, c0:QG], start=True, stop=not diag)
            if diag:
                r = e.matmul(sbk[:, 0:128], lhsT=ident, rhs=msk, start=False, stop=True)
            return r

        def pv_ins(e, i):
            typ, h, kt = blocks[i]
            V = Vf if typ == 0 else Vm
            oi = 4 + (typ * 8 + h) % 2
            c0 = max(0, kt * 128 - q0)
            n = QG - c0
            return e.matmul(bank[oi][:, c0:QG], lhsT=vaug(V[:, kt, :], h), rhs=PT[i % 2][:, 0:n], start=(kt == 0), stop=(kt == nkt - 1))

        def keys_qk(i):
            typ, h, kt = blocks[i]
            return [("KTf%d" if typ == 0 else "KTm%d") % kt, "QTf" if typ == 0 else "QTm", "cst_b"], [bk[2 + (i % 2)]]

        def keys_pv(i):
            typ, h, kt = blocks[i]
            return ["PT%d" % (i % 2), "V%d" % kt], [bk[4 + (typ * 8 + h) % 2]]

        def emit_exp(i):
            typ, h, kt = blocks[i]
            n = QG - max(0, kt * 128 - q0)
            sbi = 2 + (i % 2)
            T.op("act", [bk[sbi]], ["PT%d" % (i % 2)], lambda e: e.activation(out=PT[i % 2][:, 0:n], in_=bank[sbi][:, 0:n], func=AF.Exp))

        def emit_norm(i):
            typ, h, kt = blocks[i]
            oi = 4 + (typ * 8 + h) % 2
            pair, half = h // 2, h % 2
            rs = slice(half * 64, half * 64 + 64)
            rd = slice((1 - half) * 64, (1 - half) * 64 + 64)
            T.op("dve", [bk[oi]], ["rden"], lambda e: e.reciprocal(out=rden[rd, 0:QG], in_=bank[oi][rd, 0:QG]))
            T.op("dve", [bk[oi], "rden"], ["OT"], lambda e: e.tensor_tensor(out=OT[rs, typ * 4 + pair, :], in0=bank[oi][rs, 0:QG], in1=rden[rd, 0:QG], op=ALU.mult))

        r_, w_ = keys_qk(0)
        T.op("pe", r_, w_, lambda e: qk_ins(e, 0))
        for i in range(nb + 1):
            if i < nb:
                emit_exp(i)
            rr, ww = [], []
            if i + 1 < nb:
                a, b2 = keys_qk(i + 1); rr += a; ww += b2
            if i >= 1:
                a, b2 = keys_pv(i - 1); rr += a; ww += b2

            def f(e, i=i):
                r = None
                if i + 1 < nb:
                    r = qk_ins(e, i + 1)
                if i >= 1:
                    r = pv_ins(e, i - 1)
                return r
            if rr:
                T.op("pe", rr, ww, f)
            if i >= 1 and blocks[i - 1][2] == nkt - 1:
                emit_norm(i - 1)
            yield

    def qtrans_gen():
        for j in range(NQT):
            yield from tr8(QaugS[:, j], 128, 70, QTf[0:70, :, j * 128:(j + 1) * 128], "QTf", "Qaug", 1)
            yield from tr8(QmlaS[:, j], 128, 96, QTm[0:96, :, j * 128:(j + 1) * 128], "QTm", "Qmla", 6)

    def outproj_gen(g):
        for j in range(NQT):
            r0 = g * QG + j * 128
            yield from outproj(xp[r0:r0 + 128, :], 128, OT[:, :, j * 128:(j + 1) * 128], x1_d[r0:r0 + 128, :], 0)

    run(fe_group(0))
    run(qtrans_gen())
    for g in range(NG if STAGE >= 3 else 1):
        if STAGE < 2:
            break
        nblk = 16 * ((g * QG + QG) // 128)
        side = fe_group(g + 1) if g + 1 < NG else None
        ratio = max(1, -(-NQT * 110 // nblk))
        interleave(att_group(g), side, ratio)
        run(roundrobin([outproj_gen(g)] + ([qtrans_gen()] if g + 1 < NG else [])))

    T.barrier()
    NQ = T_S
    T.op("dve", [], ["KaugC0", "KaugC1"], lambda e: (e.memset(KaugC[0][:], 1.0), e.memset(KaugC[1][:], 1.0))[1])
    for kc in range(2):
        run(tr8(w_uk_sb[:, kc, :].rearrange("p (h d) -> p h d", h=8), 128, 64, w_ukT[0:64, :, kc * 128:(kc + 1) * 128], "w_ukT", "w_uk", 1))
    def cumsum_gen(b):
        for q4 in range(4):
            T.dma("sp", "lfc%d" % q4, [], ["lfc"], lfc[:, q4 * 8:(q4 + 1) * 8, :], clf[b, q4 * 1024:(q4 + 1) * 1024, :].rearrange("(t p) h -> p t h", p=128))
        yield

        def fcm(e):
            e.matmul(bank[7][:, 0:256], lhsT=tri_f, rhs=lfc.rearrange("p t h -> p (t h)"), start=True, stop=True)
            return e.matmul(bank[7][:, 256:512], lhsT=ones_f, rhs=lfc.rearrange("p t h -> p (t h)"), start=True, stop=True)
        T.op("pe", ["lfc", "cst_f"], [bk[7]], fcm); yield
        T.op("dve", [], ["Ec"], lambda e: e.memset(Ec[:, 0, :], 0.0)); yield
        T.op("act", [bk[7]], ["rc"], lambda e: e.copy(out=rc, in_=bank[7][:, 256:512].rearrange("p (t h) -> p t h", t=32))); yield
        for t in range(32):
            T.op("dve", ["rc", "Ec"], ["Ec"], lambda e, t=t: e.tensor_tensor(out=Ec[:, t + 1, :], in0=Ec[:, t, :], in1=rc[:, t, :], op=ALU.add)); yield
        T.op("dve", [bk[7], "Ec"], ["Fc"], lambda e: e.tensor_tensor(out=Fc, in0=bank[7][:, 0:256].rearrange("p (t h) -> p t h", t=32), in1=Ec[:, 0:32, :], op=ALU.add)); yield
        T.op("dve", ["Ec"], ["tot"], lambda e: e.tensor_copy(out=tot[:], in_=Ec[:, 32, :])); yield
        T.op("dve", ["Fc"], ["pcs"], lambda e: e.tensor_copy(out=pcs[:, :, :, 0], in_=Fc)); yield
        T.op("dve", ["Fc", "pcs"], ["rc"], lambda e: e.tensor_tensor(out=rc, in0=Fc, in1=pcs[:, :, :, 0], op=ALU.subtract)); yield
        T.op("dve", ["rc"], ["pcs"], lambda e: e.tensor_copy(out=pcs[:, :, :, 1], in_=rc)); yield
        T.op("dve", ["rc", "pcs"], ["Fc"], lambda e: e.tensor_tensor(out=Fc, in0=rc, in1=pcs[:, :, :, 1], op=ALU.subtract)); yield
        T.op("dve", ["Fc"], ["pcs"], lambda e: e.tensor_copy(out=pcs[:, :, :, 2], in_=Fc)); yield
        T.op("dve", ["pcs"], ["pcs"], lambda e: e.tensor_scalar(out=pcs, in0=pcs, scalar1=-1.0, scalar2=None, op0=ALU.mult)); yield

    NJ = NS if STAGE >= 5 else (1 if STAGE == 4 else 0)
    wuk_done = [False]
    if NJ > 0:
        run(cumsum_gen(0))
    for b in range(NJ):
        run(front_tile(xs[b], NQ, T_P, ok_s[b], ov_s[b], olf_s[b], oml_s[b], okr_s[b], vdst(Vf[:, 0, :], NQ), vdst(Vm[:, 0, :], NQ), "V0",
                       QaugS[:, 0], QmlaS[:, 0], [0, 6, 7], 1))
        run(tr8(QaugS[:, 0], NQ, 70, QTf[0:70, :, 0:NQ], "QTf", "Qaug", 1))
        run(tr8(QmlaS[:, 0][:, :, 0:64], NQ, 64, QTm[0:64, :, 0:NQ], "QTm", "Qmla", 1))
        run(tr8(QmlaS[:, 0][:, :, 64:96], NQ, 32, qrT[0:32, :, 0:NQ], "qrT", "Qmla", 1))
        run(tr8(Kaug, NQ, 70, KTf[0:70, 0, :, 0:NQ], "KTf0", "Kaug", 1))
        for typ in range(1):
            QT, dk, qk = (QTf, 70, "QTf") if typ == 0 else (QTm, 96, "QTm")

            def st_load(kt):
                r0 = kt * 128
                i2 = kt % 2
                T.dma("sp", "kc%d" % i2, [], ["KstF%d" % i2], KstF[i2], cfk[b, r0:r0 + 128, :])
                T.dma("pool", "vc%d" % (kt % 4), [], ["Vc%d" % (kt % 4)], Vfc[kt % 4], cfv[b, r0:r0 + 128, :])

            def st_prep(kt):
                i2 = kt % 2
                if typ == 0:
                    T.op("dve", ["KstF%d" % i2], ["KaugC%d" % i2], lambda e: e.tensor_copy(out=KaugC[i2][:, :, 0:64], in_=KstF[i2].rearrange("p (h d) -> p h d", h=8)))
                    T.op("dve", ["pcs"], ["KaugC%d" % i2], lambda e: e.tensor_copy(out=KaugC[i2][:, :, 64:67], in_=pcs[:, kt, :, :]))
                    run(tr8(KaugC[i2], 128, 70, KTc[i2][0:70, :, :], "KTc%d" % i2, "KaugC%d" % i2, 1 if i2 == 0 else 6))
                else:
                    run(tr8(latc[i2].rearrange("p (k c) -> p k c", k=2), 128, 128, cT[:, :, :], "cT", "latc%d" % i2, 1, nh=2))
                    pass
                    T.op("dve", ["krc%d" % i2], ["Kmla"], lambda e: e.tensor_copy(out=Kmla[:, :, 64:96], in_=krc[i2][:].unsqueeze(1).to_broadcast([128, 8, 32])))
                    run(tr8(Kmla, 128, 96, KTc[i2][0:96, :, :], "KTc%d" % i2, "Kmla", 6))

            def srcs(kt):
                if kt < 32:
                    i2 = kt % 2
                    return KTc[i2], "KTc%d" % i2, Vfc[kt % 4], "Vc%d" % (kt % 4), 128
                return KTf[:, 0], "KTf0", Vf[:, 0, :], "V0", NQ

            def st_qk(kt):
                KTsrc, kk, _, _, nk = srcs(kt)
                sbi = 2 + kt % 2
                sbk = bank[sbi]; ptb = PT[kt % 2]

                def qkf(e):
                    r = None
                    for h in range(8):
                        r = e.matmul(sbk[0:nk, h * NQ:(h + 1) * NQ], lhsT=KTsrc[0:dk, h, 0:nk], rhs=QT[0:dk, h, 0:NQ], start=(h == 0), stop=True, skip_group_check=True)
                        if kt == 32 and typ == 0:
                            r = e.matmul(sbk[0:nk, h * NQ:(h + 1) * NQ], lhsT=ident[0:NQ, 0:NQ], rhs=mtri[0:NQ, 0:NQ], start=False, stop=True, skip_group_check=True)
                    return r
                T.op("pe", [kk, qk, "cst_b"], [bk[sbi]], qkf)
                T.op("act", [bk[sbi]], ["PT%d" % (kt % 2)], lambda e: e.activation(out=ptb[0:nk, :], in_=sbk[0:nk, :], func=AF.Exp))

            def st_pv(kt):
                _, _, Vsrc, vk, nk = srcs(kt)
                ptb = PT[kt % 2]

                def pvf(e):
                    for h in range(8):
                        pair = h // 2
                        lt_ = Vsrc[0:nk, pair * 128:(pair + 1) * 128] if kt < 32 else vaug(Vsrc, h, nk)
                        e.matmul(bank[4][:, h * NQ:(h + 1) * NQ], lhsT=lt_, rhs=ptb[0:nk, h * NQ:(h + 1) * NQ],
                                 start=(kt == 0 and h == 0), stop=(kt == 32), skip_group_check=True)
                    return e.matmul(bank[5][:, :], lhsT=ones_b[0:nk, :], rhs=ptb[0:nk, :], start=(kt == 0), stop=(kt == 32))
                T.op("pe", ["PT%d" % (kt % 2), vk, "cst_b"], [bk[4], bk[5]], pvf)

            for it in range(33 + 3):
                if b == 0 and it < NFC:
                    T.dma("pool", "w%d" % (wcnt[0] % 4), [], ["w_dn%d" % it], w_dn_c[it], w_dn_d[it * 128:(it + 1) * 128, :])
                    wcnt[0] += 1
                if it < 32:
                    st_load(it)
                if 0 <= it - 1 < 32:
                    st_prep(it - 1)
                if 0 <= it - 2 < 33:
                    st_qk(it - 2)
                if 0 <= it - 3 < 33:
                    st_pv(it - 3)
            for half in range(2):
                rs = slice(half * 64, half * 64 + 64)
                dv = bank[5][rs, :].rearrange("p (a c q) -> p a c q", a=4, c=2)[:, :, half, :]
                ov = bank[4][rs, :].rearrange("p (a c q) -> p a c q", a=4, c=2)[:, :, half, :]
                rv = rden[rs, 0:4 * NQ].rearrange("p (a q) -> p a q", a=4)
                T.op("dve", [bk[5]], ["rden"], lambda e, dv=dv, rv=rv: e.reciprocal(out=rv, in_=dv))
                T.op("dve", [bk[4], "rden"], ["OT"], lambda e, ov=ov, rv=rv, rs=rs, typ=typ: e.tensor_tensor(out=OT[rs, typ * 4:typ * 4 + 4, 0:NQ], in0=ov, in1=rv, op=ALU.mult))

        for kc in range(2):
            bi = 0 if kc == 0 else 7

            def qlm(e, kc=kc, bi=bi):
                r = None
                for h in range(8):
                    r = e.matmul(bank[bi][:, h * NQ:(h + 1) * NQ], lhsT=w_ukT[0:64, h, kc * 128:(kc + 1) * 128], rhs=QTm[0:64, h, 0:NQ],
                                 start=(h == 0), stop=True, skip_group_check=True)
                return r
            T.op("pe", ["w_ukT", "QTm"], [bk[bi]], qlm)
            if kc == 0:
                T.op("act", [bk[bi]], ["qlatT"], lambda e: e.copy(out=qlatT[:, 0, :], in_=bank[0][:, :]))
            else:
                T.op("dve", [bk[bi]], ["qlatT"], lambda e: e.tensor_copy(out=qlatT[:, 1, :], in_=bank[7][:, :]))

        def ltn(e):
            ptv = bfv(1)
            e.transpose(out=ptv[:, 0, 0:NQ], in_=cnb[0:NQ, 0:128], identity=ident[0:NQ, 0:NQ])
            e.transpose(out=ptv[:, 1, 0:NQ], in_=cnb[0:NQ, 128:256], identity=ident[0:NQ, 0:NQ])
            return e.transpose(out=ptv[0:32, 2, 0:NQ], in_=Kmla[0:NQ, 0, 64:96], identity=ident[0:NQ, 0:NQ])
        T.op("pe", ["cnb", "Kmla", "cst_b"], [bk[1]], ltn)
        T.op("act", [bk[1]], ["LTn"], lambda e: e.copy(out=LTn[:, 0:2, :], in_=bfv(1)[:, 0:2, 0:NQ]))
        T.op("act", [bk[1]], ["LTn"], lambda e: e.copy(out=LTn[0:32, 2, :], in_=bfv(1)[0:32, 2, 0:NQ]))

        def a_load(kt):
            r0 = kt * 128
            i4 = kt % 4
            T.dma("sp", "lc%d" % (kt % 2), [], ["LstF%d" % (kt % 2)], LstF[kt % 2][:, 0:256], cml[b, r0:r0 + 128, :])
            T.dma("sp", "rc%d" % (kt % 2), [], ["LstF%d" % (kt % 2)], LstF[kt % 2][:, 256:288], ckr[b, r0:r0 + 128, :])

        def a_prep(kt):
            i4 = kt % 4; i2 = kt % 2
            tb = 1 if i2 == 0 else 6
            T.op("dve", ["LstF%d" % i2], ["Lb%d" % i4], lambda e: e.tensor_copy(out=Lb[i4], in_=LstF[i2]))

            def f(e):
                ptv = bfv(tb)
                e.transpose(out=ptv[:, 0, :], in_=Lb[i4][:, 0:128], identity=ident)
                e.transpose(out=ptv[:, 1, :], in_=Lb[i4][:, 128:256], identity=ident)
                return e.transpose(out=ptv[0:32, 2, :], in_=Lb[i4][:, 256:288], identity=ident)
            T.op("pe", ["Lb%d" % i4, "cst_b"], [bk[tb]], f)
            T.op("act", [bk[tb]], ["LT%d" % i2], lambda e: e.copy(out=LT[i2][:, 0:2, :], in_=bfv(tb)[:, 0:2, :]))
            T.op("dve", [bk[tb]], ["LT%d" % i2], lambda e: e.tensor_copy(out=LT[i2][0:32, 2, :], in_=bfv(tb)[0:32, 2, :]))

        def a_qk(kt):
            lt, ltk, nk = (LT[kt % 2], "LT%d" % (kt % 2), 128) if kt < 32 else (LTn, "LTn", NQ)
            sbi = 2 + kt % 2
            ptb = PT[kt % 2]

            def f(e):
                e.matmul(bank[sbi][0:nk, :], lhsT=lt[:, 0, 0:nk], rhs=qlatT[:, 0, :], start=True, stop=False)
                e.matmul(bank[sbi][0:nk, :], lhsT=lt[:, 1, 0:nk], rhs=qlatT[:, 1, :], start=False, stop=False)
                return e.matmul(bank[sbi][0:nk, :], lhsT=lt[0:32, 2, 0:nk], rhs=qrT[0:32, :, :].rearrange("p h s -> p (h s)"), start=False, stop=True)
            T.op("pe", [ltk, "qlatT", "qrT"], [bk[sbi]], f)
            T.op("act", [bk[sbi]], ["PT%d" % (kt % 2)], lambda e: e.activation(out=ptb[0:nk, :], in_=bank[sbi][0:nk, :], func=AF.Exp))

        def a_pv(kt):
            lsrc, lk, nk = (Lb[kt % 4], "Lb%d" % (kt % 4), 128) if kt < 32 else (cnb, "cnb", NQ)
            ptb = PT[kt % 2]

            def f(e):
                e.matmul(bank[4][:, :], lhsT=lsrc[0:nk, 0:128], rhs=ptb[0:nk, :], start=(kt == 0), stop=(kt == 32))
                e.matmul(bank[5][:, :], lhsT=lsrc[0:nk, 128:256], rhs=ptb[0:nk, :], start=(kt == 0), stop=(kt == 32))
                return e.matmul(bank[0][:, :], lhsT=ones_b[0:nk, :], rhs=ptb[0:nk, :], start=(kt == 0), stop=(kt == 32))
            T.op("pe", ["PT%d" % (kt % 2), lk, "cst_b"], [bk[4], bk[5], bk[0]], f)

        side = cumsum_gen(b + 1) if b + 1 < NJ else None
        if b == NJ - 1 and NJ == NS:
            for k in range(3):
                T.dma("pool", "w%d" % (wcnt[0] % 4), [], ["wuk%d" % k] + ["w_in%d" % i_ for i_ in range(5)],
                      arena[:, k * 2 * DFF:(k + 1) * 2 * DFF], w_up_d[k * 128:(k + 1) * 128, :])
                wcnt[0] += 1
            wuk_done[0] = True
        for it in range(33 + 3):
            if it < 32:
                a_load(it)
            if 0 <= it - 1 < 32:
                a_prep(it - 1)
            if 0 <= it - 2 < 33:
                a_qk(it - 2)
            if 0 <= it - 3 < 33:
                a_pv(it - 3)
            if side is not None:
                for _i in range(2):
                    try:
                        next(side)
                    except StopIteration:
                        side = None
                        break
        if side is not None:
            run(side)
        T.op("dve", [bk[0]], ["rdenL"], lambda e: e.reciprocal(out=rdenL[:, :], in_=bank[0][:, :]))
        T.op("dve", [bk[4], "rdenL"], ["qlatT"], lambda e: e.tensor_tensor(out=qlatT[:, 0, :], in0=bank[4][:, :], in1=rdenL[:, :], op=ALU.mult))
        T.op("dve", [bk[5], "rdenL"], ["qlatT"], lambda e: e.tensor_tensor(out=qlatT[:, 1, :], in0=bank[5][:, :], in1=rdenL[:, :], op=ALU.mult))

        def fin(e):
            r = None
            first = True
            for h in range(8):
                pair = h // 2
                for kc in range(2):
                    r = e.matmul(bank[7][:, h * NQ:(h + 1) * NQ], lhsT=w_uv_sb[:, kc, pair * 128:(pair + 1) * 128], rhs=qlatT[:, kc, h * NQ:(h + 1) * NQ],
                                 start=first, stop=(kc == 1), skip_group_check=True)
                    first = False
            return r
        T.op("pe", ["qlatT", "w_uv"], [bk[7]], fin)
        for half in range(2):
            rs = slice(half * 64, half * 64 + 64)
            ov = bank[7][rs, :].rearrange("p (a c q) -> p a c q", a=4, c=2)[:, :, half, :]
            T.op("act", [bk[7]], ["OT"], lambda e, ov=ov, rs=rs: e.copy(out=OT[rs, 4:8, 0:NQ], in_=ov))
        run(outproj(xs[b], NQ, OT[:, :, 0:NQ], x1_d[T_P + b * NQ:T_P + (b + 1) * NQ, :], 0))

    if STAGE < 6:
        T.finish("sp")
        return nc
    T.barrier()
    off[0] = 0
    w_up_sb = carve(8 * 2 * DFF).rearrange("p (k n) -> p k n", k=8)
    assert off[0] <= _o_ktm
    _free = [[off[0], _o_ktm + 6 * 1024], [_o_vf, _o_vf + 5 * VW], [_o_vm, _o_vm + 5 * VW], [_o_vm + 16 * VW, ARENA]]

    def carve(n, dt=BF16):
        nb = 2 * n if dt == F32 else n
        na = (nb + 15) // 16 * 16
        for r_ in _free:
            if r_[1] - r_[0] >= na:
                a = r_[0]; r_[0] += na
                v = arena[:, a:a + nb]
                return v.bitcast(F32) if dt == F32 else v
        raise AssertionError("FFN arena full")
    aTb = [carve(NFC * FG).rearrange("p (k n) -> p k n", k=NFC) for _ in range(2)]
    h2Tb = [carve(8 * (FG + 8)).rearrange("p (k n) -> p k n", k=8) for _ in range(2)]
    T.dma("sp", "c2", [], ["g_big"], g_big[:], g_ffn_d.partition_broadcast(128))
    g_fin = carve(D, F32)
    T.dma("sp", "c6", [], ["g_fin"], g_fin, g_fin_d.partition_broadcast(128))
    cw = carve(3 * 44, F32).rearrange("p (j c) -> p j c", j=3); cb = carve(44, F32)
    cstage = carve(128, F32)
    cprev = carve(NS * 2 * 44, F32).rearrange("p (b j c) -> p b j c", b=NS, j=2)
    cnew = carve(NS * 2 * 44, F32).rearrange("p (b j c) -> p b j c", b=NS, j=2)
    cnewT = carve(128, F32)
    tgs = [carve(FG, F32) for _ in range(2)]; tvs = [carve(FG, F32) for _ in range(2)]
    yt = carve(D, F32)
    xe = carve(D, F32)

    def load_fm(src_rows, nrows, dst, dkey):
        T.dma("sp", "cst", [], ["cstage"], cstage[0:nrows, :], src_rows)
        T.op("pe", ["cstage", "cst_f"], [bk[0]], lambda e: e.transpose(out=bank[0][:, 0:nrows], in_=cstage[0:nrows, :], identity=ident_f[0:nrows, 0:nrows]))
        T.op("act", [bk[0]], [dkey], lambda e: e.copy(out=dst, in_=bank[0][:, 0:nrows]))
    for j in range(3):
        load_fm(cw_d[j].rearrange("(c p) -> c p", p=128), 44, cw[:, j, :], "cw")
    load_fm(cb_d.rearrange("(c p) -> c p", p=128), 44, cb[:, :], "cb")

    def ffn_prep(gi, x1_src, nt, halo, nseg):
        h2T = h2Tb[gi % 2]; hk = "h2T%d" % (gi % 2)
        ntile = (nt + 127) // 128
        L = nt // nseg
        W = nseg * (L + 2)
        h2v = h2T[:, :, 0:W].rearrange("p k (s c) -> p k s c", s=nseg)
        if halo == "prev":
            hp = h2Tb[(gi - 1) % 2]
            T.op("dve", ["h2T%d" % ((gi - 1) % 2)], [hk], lambda e: e.tensor_copy(out=h2T[:, :, 0:2], in_=hp[:, :, FG:FG + 2]))
        else:
            T.op("dve", [], [hk], lambda e: e.memset(h2v[:, :, :, 0:2], 0.0))
        yield
        for j in range(ntile):
            n = min(128, nt - j * 128)
            T.dma("sp", "xt", [], ["xt"], xt[0:n, :], x1_src[j * 128:j * 128 + n, :])
            yield from rmsn_stats(xt[0:n, :], "xt", n, D, 0)
            T.op("dve", ["xt", "ss0", "g_big"], ["hb"],
                 lambda e, n=n: e.scalar_tensor_tensor(out=hb[0:n, :], in0=xt[0:n, :], scalar=ss[0:n, 0:1], in1=g_big[0:n, :], op0=ALU.mult, op1=ALU.mult))
            yield
            if L >= 128:
                sg, c0 = (j * 128) // L, (j * 128) % L
                yield from tr8(hb.rearrange("p (k c) -> p k c", k=8), n, 128, h2v[:, :, sg, 2 + c0:2 + c0 + n], hk, "hb", 1)
            else:
                r = 128 // L
                yield from tr8(hb.rearrange("p (k c) -> p k c", k=8), n, 128, None, hk, "hb", 1,
                               evac=lambda e, ptv, j=j, r=r: e.copy(out=h2v[:, :, j * r:(j + 1) * r, 2:2 + L], in_=ptv[:, :, :].rearrange("p k (r l) -> p k r l", r=r)))

    def ffn_group(gi, x1_src, nt, y_dst_fn, last, ocv_dsts, halo, nseg=1, next_prep=None):
        h2T = h2Tb[gi % 2]; hk = "h2T%d" % (gi % 2)
        aT = aTb[gi % 2]; ak = "aT%d" % (gi % 2)
        ntile = (nt + 127) // 128
        L = nt // nseg
        W = nseg * (L + 2)

        def mm(c):
            i2 = c % 2
            for gv in range(2):
                cc = gv * NFC + c
                bi = 2 + 2 * gv + i2

                def um(e, cc=cc, bi=bi):
                    r = None
                    for k in range(8):
                        r = e.matmul(bank[bi][:, 0:W], lhsT=w_up_sb[:, k, cc * 128:(cc + 1) * 128], rhs=h2T[:, k, 0:W], start=(k == 0), stop=(k == 7))
                    return r
                T.op("pe", [hk, "wu%d" % cc, "wuk0", "wuk1", "wuk2"], [bk[bi]], um)

        def ew(c):
            i2 = c % 2
            for gv in range(2):
                cc = gv * NFC + c
                bi = 2 + 2 * gv + i2
                pk = bk[bi]
                psv = bank[bi][:, 0:W].rearrange("p (s c) -> p s c", s=nseg)
                dk_ = ("tg%d" if gv == 0 else "tv%d") % i2
                dv = (tgs if gv == 0 else tvs)[i2][:, 0:nt].rearrange("p (s l) -> p s l", s=nseg)
                if halo == "state":
                    T.op("dve", [pk, "cprev"], [pk], lambda e, psv=psv, cc=cc: e.tensor_copy(out=psv[:, :, 0:2], in_=cprev[:, 0:nseg, :, cc]))
                T.op("act", [pk, "cw", "cb"], [dk_], lambda e, psv=psv, cc=cc, dv=dv: e.activation(out=dv, in_=psv[:, :, 2:2 + L], func=AF.Identity, scale=cw[:, 2, cc:cc + 1], bias=cb[:, cc:cc + 1]))
                T.op("dve", [pk, dk_, "cw"], [dk_], lambda e, psv=psv, cc=cc, dv=dv: e.scalar_tensor_tensor(out=dv, in0=psv[:, :, 1:1 + L], scalar=cw[:, 1, cc:cc + 1], in1=dv, op0=ALU.mult, op1=ALU.add))
                T.op("dve", [pk, dk_, "cw"], [dk_], lambda e, psv=psv, cc=cc, dv=dv: e.scalar_tensor_tensor(out=dv, in0=psv[:, :, 0:L], scalar=cw[:, 0, cc:cc + 1], in1=dv, op0=ALU.mult, op1=ALU.add))
                if last:
                    T.op("dve", [pk], ["cnew"], lambda e, psv=psv, cc=cc: e.tensor_copy(out=cnew[:, 0:nseg, :, cc], in_=psv[:, :, L:L + 2]))
            T.op("act", ["tg%d" % i2], ["tg%d" % i2], lambda e: e.activation(out=tgs[i2][:, 0:nt], in_=tgs[i2][:, 0:nt], func=AF.Silu))
            T.op("pool", ["tg%d" % i2, "tv%d" % i2], [ak], lambda e: e.tensor_tensor(out=aT[:, c, 0:nt], in0=tgs[i2][:, 0:nt], in1=tvs[i2][:, 0:nt], op=ALU.mult))

        def up_loop():
            mm(0)
            for c in range(NFC):
                if c + 1 < NFC:
                    mm(c + 1)
                ew(c)
                yield
        interleave(up_loop(), next_prep if (next_prep is not None) else None, 2)
        def epi():
            for j in range(ntile):
                n = min(128, nt - j * 128)
                T.dma("sp", "xe", [], ["xe"], xe[0:n, :], x1_src[j * 128:j * 128 + n, :])
                yield
                for hf in range(2):
                    bi = 6 + hf

                    def dm(e, bi=bi, hf=hf, j=j, n=n):
                        r = None
                        for c in range(NFC):
                            r = e.matmul(bank[bi][0:n, :], lhsT=aT[:, c, j * 128:j * 128 + n], rhs=w_dn_c[c][:, hf * 512:(hf + 1) * 512], start=(c == 0), stop=(c == NFC - 1))
                        return r
                    T.op("pe", [ak] + ["w_dn%d" % c_ for c_ in range(NFC)], [bk[bi]], dm)
                    yield
                    T.op("dve", [bk[bi], "xe"], ["xe"], lambda e, bi=bi, hf=hf, n=n: e.tensor_tensor(out=xe[0:n, hf * 512:(hf + 1) * 512], in0=bank[bi][0:n, :], in1=xe[0:n, hf * 512:(hf + 1) * 512], op=ALU.add))
                    yield
                T.op("act", ["xe"], ["yt", "ss3"], lambda e, n=n: e.activation(out=yt[0:n, :], in_=xe[0:n, :], func=AF.Square, accum_out=ss[0:n, 3:4]))
                yield
                T.op("dve", ["ss3"], ["ss3"], lambda e, n=n: e.tensor_scalar(out=ss[0:n, 3:4], in0=ss[0:n, 3:4], scalar1=1.0 / D, scalar2=1e-6, op0=ALU.mult, op1=ALU.add))
                yield
                T.op("pool", ["ss3", "mhalf"], ["ss3"], lambda e, n=n: e.tensor_tensor(out=ss[0:n, 3:4], in0=ss[0:n, 3:4], in1=mhalf[0:n, 0:1], op=ALU.pow))
                yield
                T.op("dve", ["xe", "ss3", "g_fin"], ["yt"],
                     lambda e, n=n: e.scalar_tensor_tensor(out=yt[0:n, :], in0=xe[0:n, :], scalar=ss[0:n, 3:4], in1=g_fin[0:n, :], op0=ALU.mult, op1=ALU.mult))
                yield
                T.dma("sp", "yt", ["yt"], [], y_dst_fn(j, n), yt[0:n, :])
                yield
        if last:
            for sg in range(nseg):
                T.op("pe", ["cnew", "cst_f"], [bk[0]], lambda e, sg=sg: e.transpose(out=bank[0][0:88, 0:128], in_=cnew[:, sg].rearrange("p j c -> p (j c)"), identity=ident_f))
                T.op("act", [bk[0]], ["cnewT"], lambda e: e.copy(out=cnewT[0:88, :], in_=bank[0][0:88, 0:128]))
                T.dma("sp", "cno", ["cnewT"], [], ocv_dsts[sg].rearrange("j (c p) -> (j c) p", p=128), cnewT[0:88, :])
        return epi()

    def chain_gens(gens):
        for g_ in gens:
            yield from g_

    for b in range(NS):
        for j in range(2):
            load_fm(scv[b, j].rearrange("(c p) -> c p", p=128), 44, cprev[:, b, j, :], "cprev")
    y_s_flat = y_s.rearrange("b t d -> (b t) d")
    ng = T_P // FG
    specs = []
    for g in range(ng):
        r0 = g * FG
        specs.append(dict(x1=x1_d[r0:r0 + FG, :], nt=FG, y=(lambda j, n, r0=r0: y_p[r0 + j * 128:r0 + j * 128 + n, :]), last=(g == ng - 1),
                          ocv=[ocv_p], halo=("zero" if g == 0 else "prev"), nseg=1))
    specs.append(dict(x1=x1_d[T_P:T_P + NS * T_S, :], nt=NS * T_S, y=(lambda j, n: y_s_flat[j * 128:j * 128 + n, :]), last=True,
                      ocv=[ocv_s[b] for b in range(NS)], halo="state", nseg=NS))
    run(ffn_prep(0, specs[0]["x1"], specs[0]["nt"], specs[0]["halo"], specs[0]["nseg"]))
    if not wuk_done[0]:
        for k in range(3):
            T.dma("pool", "w%d" % (wcnt[0] % 4), [], ["wuk%d" % k], w_up_sb[:, k, :], w_up_d[k * 128:(k + 1) * 128, :])
            wcnt[0] += 1
    for c in range(NFC):
        for gv in range(2):
            cc = gv * NFC + c
            T.dma("pool", "w%d" % (wcnt[0] % 4), [], ["wu%d" % cc], w_up_sb[:, 3:8, cc * 128:(cc + 1) * 128],
                  w_up_d[384:1024, cc * 128:(cc + 1) * 128].rearrange("(k p) n -> p k n", p=128))
            wcnt[0] += 1
    pend = None
    for gi, sp_ in enumerate(specs):
        sides = []
        if gi + 1 < len(specs):
            n_ = specs[gi + 1]
            sides.append(ffn_prep(gi + 1, n_["x1"], n_["nt"], n_["halo"], n_["nseg"]))
        if pend is not None:
            sides.append(pend)
        pend = ffn_group(gi, sp_["x1"], sp_["nt"], sp_["y"], sp_["last"], sp_["ocv"], sp_["halo"], sp_["nseg"],
                         next_prep=chain_gens(sides) if sides else None)
    run(pend)

    T.finish("sp")
    return nc


def _consts():
    ident = np.eye(128, dtype=np.float32)
    s = np.arange(128)
    tri = (s[:, None] <= s[None, :]).astype(np.float32)
    ones = np.ones((128, 128), np.float32)
    mtri = np.where(s[:, None] <= s[None, :], 0.0, NEGM).astype(np.float32)
    mblk = np.where((s[:, None] // 64) <= (s[None, :] // 64), 0.0, NEGM).astype(np.float32)
    return np.concatenate([ident, tri, ones, mtri, mblk], axis=1)


def _rope_tab():
    half = 16
    inv = (np.float32(10000.0) ** (-np.arange(half, dtype=np.float32) / np.float32(half))).astype(np.float32)
    pos = np.concatenate([np.arange(T_P), PAST + np.arange(T_S)]).astype(np.float32)
    ang = (pos[:, None] * inv[None, :]).astype(np.float32)
    c = np.cos(ang).astype(np.float32); s = np.sin(ang).astype(np.float32)
    sc = np.float32(96.0 ** -0.5)
    return np.concatenate([c * sc, s * sc, c, s], axis=1).astype(np.float32)


_NC_CACHE = {}


def kernel(x_prompt, x_sample, cache_fox_k, cache_fox_v, cache_fox_logf, cache_mla_latent,
           cache_mla_krope, state_ffn_conv, attn_norm, w_in, b_forget, q_norm, w_q_up, kv_norm,
           w_uk, w_uv, w_out, ffn_norm, w_up, conv_w, conv_b, w_down, final_norm, _cores=None):
    f = lambda a: np.ascontiguousarray(np.asarray(a, dtype=np.float32))
    if "nc" not in _NC_CACHE:
        _NC_CACHE["nc"] = build_program()
    nc = _NC_CACHE["nc"]
    shared = {
        "attn_norm": f(attn_norm[0]), "w_in": f(w_in[0]), "b_forget": f(b_forget[0]), "q_norm": f(q_norm[0]),
        "w_q_up": f(w_q_up[0]), "kv_norm": f(kv_norm[0]), "w_uk": f(np.asarray(w_uk[0]).reshape(256, 512)),
        "w_uv": f(np.asarray(w_uv[0]).reshape(256, 512)), "w_out": f(w_out[0]), "ffn_norm": f(ffn_norm[0]),
        "w_up": f(w_up[0]), "conv_w": f(conv_w[0]), "conv_b": f(conv_b[0]), "w_down": f(w_down[0]),
        "final_norm": f(final_norm), "rope_tab": _rope_tab(), "consts": _consts(),
    }
    cores = list(range(NCORES)) if _cores is None else _cores
    in_maps = []
    for c in cores:
        sl = slice(NS * c, NS * (c + 1))
        m = dict(shared)
        m["xp"] = f(x_prompt[c]); m["xs"] = f(x_sample[sl])
        m["cfk"] = f(np.asarray(cache_fox_k[0, sl]).reshape(NS, PAST, 512))
        m["cfv"] = f(np.asarray(cache_fox_v[0, sl]).reshape(NS, PAST, 512))
        m["clf"] = f(cache_fox_logf[0, sl]); m["cml"] = f(cache_mla_latent[0, sl]); m["ckr"] = f(cache_mla_krope[0, sl])
        m["scv"] = f(state_ffn_conv[0, sl])
        in_maps.append(m)
    res = run_bass_kernel_spmd(nc, in_maps, core_ids=cores)
    R = res.results
    cat = lambda k: np.concatenate([np.asarray(r[k]) for r in R], axis=0)
    stk = lambda k: np.stack([np.asarray(r[k]) for r in R], axis=0)
    nb = len(cores)
    outs = (
        stk("y_p"), cat("y_s"),
        stk("ok_p").reshape(1, nb, T_P, 8, 64), stk("ov_p").reshape(1, nb, T_P, 8, 64), stk("olf_p").reshape(1, nb, T_P, 8),
        stk("oml_p").reshape(1, nb, T_P, 256), stk("okr_p").reshape(1, nb, T_P, 32), stk("ocv_p").reshape(1, nb, 2, 2 * DFF),
        cat("ok_s").reshape(1, nb * NS, T_S, 8, 64), cat("ov_s").reshape(1, nb * NS, T_S, 8, 64), cat("olf_s").reshape(1, nb * NS, T_S, 8),
        cat("oml_s").reshape(1, nb * NS, T_S, 256), cat("okr_s").reshape(1, nb * NS, T_S, 32), cat("ocv_s").reshape(1, nb * NS, 2, 2 * DFF),
    )
    return tuple(np.ascontiguousarray(o.astype(np.float32)) for o in outs)
```
